# Optimizing a Trainium2 kernel written in Bass

```python
import math
import jax, jax.numpy as jnp
from jax import lax
import numpy as np

D_MODEL = 1024
BATCH = 16
SEQ = 2048
DEPTH = 1
DEC_BATCH = 4
DEC_SEQ = 4096
PAST_LEN = 128

HEAD_DIM = 64
N_HEADS_A = 8
N_KV_A = 2
N_HEADS_B = 8
N_KV_B = 2
G_A = N_HEADS_A // N_KV_A
G_B = N_HEADS_B // N_KV_B
WIDTH_A = N_HEADS_A * HEAD_DIM
WIDTH_B = N_HEADS_B * HEAD_DIM
KVW_A = N_KV_A * HEAD_DIM
KVW_B = N_KV_B * HEAD_DIM
IN_WIDTH = WIDTH_A + 2 * KVW_A + WIDTH_B + 2 * KVW_B
BLOCK = 128
WINDOW = 128
GRID_W = 64
ROPE_THETA = 10000.0
ROPE_HALF = HEAD_DIM // 2
N_BUCKETS = 32
MAX_DISTANCE = 128
D_FF = int(math.ceil((8 * D_MODEL / 3) / 256) * 256)
EPS = 1e-6
NEG_INF = -1e30

kernel_name = "hybrid_gqa_window_sink_encoder"


def rmsnorm(x, gain):
    xf = x.astype(jnp.float32)
    var = jnp.mean(xf * xf, axis=-1, keepdims=True)
    return (xf * lax.rsqrt(var + EPS) * gain.astype(jnp.float32)).astype(x.dtype)


def rotate_half(x):
    x1, x2 = jnp.split(x, 2, axis=-1)
    return jnp.concatenate([-x2, x1], axis=-1)


def axial_rope_tables(S):
    ROWS = S // GRID_W
    pos = jnp.arange(S)
    rows = jnp.repeat(jnp.arange(ROWS), GRID_W).astype(jnp.float32)
    cols = (pos % GRID_W).astype(jnp.float32)
    inv_freq = 1.0 / (ROPE_THETA ** (jnp.arange(0, ROPE_HALF, 2, dtype=jnp.float32) / ROPE_HALF))
    fr = rows[:, None] * inv_freq[None, :]
    fc = cols[:, None] * inv_freq[None, :]
    er = jnp.concatenate([fr, fr], axis=-1)
    ec = jnp.concatenate([fc, fc], axis=-1)
    return jnp.cos(er), jnp.sin(er), jnp.cos(ec), jnp.sin(ec)


def apply_axial_rope(x, tables):
    cr, sr, cc, sc = tables
    xr, xc = x[..., :ROPE_HALF], x[..., ROPE_HALF:]
    dt = x.dtype
    cr, sr, cc, sc = (t[None, :, None, :].astype(dt) for t in (cr, sr, cc, sc))
    yr = xr * cr + rotate_half(xr) * sr
    yc = xc * cc + rotate_half(xc) * sc
    return jnp.concatenate([yr, yc], axis=-1)


def t5_bucket(rel):
    half = N_BUCKETS // 2
    max_exact = half // 2
    ret = jnp.where(rel > 0, half, 0)
    n = jnp.abs(rel)
    nf = jnp.maximum(n, 1).astype(jnp.float32)
    large = max_exact + (jnp.log(nf / max_exact) / math.log(MAX_DISTANCE / max_exact) * (half - max_exact)).astype(jnp.int32)
    large = jnp.minimum(large, half - 1)
    return ret + jnp.where(n < max_exact, n, large)


def global_attention(q, k, v):
    B, S, KV, G, D = q.shape
    nb = S // BLOCK
    scale = 1.0 / math.sqrt(D)
    qb = q.reshape(B, nb, BLOCK, KV, G, D).transpose(1, 0, 2, 3, 4, 5)
    kf = k.astype(jnp.float32)
    vf = v.astype(jnp.float32)

    def one_block(qblk):
        s = jnp.einsum('bqkgd,bskd->bkgqs', qblk.astype(jnp.float32), kf) * scale
        p = jax.nn.softmax(s, axis=-1)
        o = jnp.einsum('bkgqs,bskd->bqkgd', p, vf)
        return o.astype(q.dtype)

    out = lax.map(one_block, qb)
    return out.transpose(1, 0, 2, 3, 4, 5).reshape(B, S, KV * G * D)


def window_attention(q, k, v, sink, rel_bias):
    B, S, KV, G, D = q.shape
    nb = S // BLOCK
    scale = 1.0 / math.sqrt(D)
    qb = q.reshape(B, nb, BLOCK, KV, G, D)
    pad = ((0, 0), (BLOCK, BLOCK), (0, 0), (0, 0))
    kp = jnp.pad(k, pad).reshape(B, nb + 2, BLOCK, KV, D)
    vp = jnp.pad(v, pad).reshape(B, nb + 2, BLOCK, KV, D)
    kw = jnp.concatenate([kp[:, :-2], kp[:, 1:-1], kp[:, 2:]], axis=2)
    vw = jnp.concatenate([vp[:, :-2], vp[:, 1:-1], vp[:, 2:]], axis=2)
    s = jnp.einsum('bnqkgd,bnckd->bnkgqc', qb.astype(jnp.float32), kw.astype(jnp.float32)) * scale
    a = jnp.arange(BLOCK)[:, None]
    c = jnp.arange(3 * BLOCK)[None, :]
    rel = c - BLOCK - a
    qpos = jnp.arange(nb)[:, None, None] * BLOCK + a[None]
    kpos = qpos + rel[None]
    valid = (jnp.abs(rel)[None] <= WINDOW) & (kpos >= 0) & (kpos < S)
    bias = rel_bias.astype(jnp.float32)[t5_bucket(rel)]
    bias = bias.transpose(2, 0, 1).reshape(KV, G, BLOCK, 3 * BLOCK)
    s = s + bias[None, None]
    s = jnp.where(valid[None, :, None, None], s, NEG_INF)
    sk = sink.astype(jnp.float32).reshape(1, 1, KV, G, 1, 1)
    m = jnp.maximum(jnp.max(s, axis=-1, keepdims=True), sk)
    p = jnp.exp(s - m)
    denom = jnp.sum(p, axis=-1, keepdims=True) + jnp.exp(sk - m)
    o = jnp.einsum('bnkgqc,bnckd->bnqkgd', p / denom, vw.astype(jnp.float32))
    return o.astype(q.dtype).reshape(B, S, KV * G * D)


def encoder_layer(x, norm_mix_pre, norm_mix_post, w_in, q_norm_a, k_norm_a, sink_b, rel_bias,
                  w_branch_a, w_branch_b, w_gate, b_gate, w_out,
                  norm_ffn_pre, norm_ffn_post, w_ffn_gate, w_ffn_up, w_ffn_down):
    B, S, _ = x.shape
    h = rmsnorm(x, norm_mix_pre)
    proj = h @ w_in
    o1 = WIDTH_A
    o2 = o1 + KVW_A
    o3 = o2 + KVW_A
    o4 = o3 + WIDTH_B
    o5 = o4 + KVW_B
    qa = proj[..., :o1].reshape(B, S, N_HEADS_A, HEAD_DIM)
    ka = proj[..., o1:o2].reshape(B, S, N_KV_A, HEAD_DIM)
    va = proj[..., o2:o3].reshape(B, S, N_KV_A, HEAD_DIM)
    qb = proj[..., o3:o4].reshape(B, S, N_KV_B, G_B, HEAD_DIM)
    kb = proj[..., o4:o5].reshape(B, S, N_KV_B, HEAD_DIM)
    vb = proj[..., o5:].reshape(B, S, N_KV_B, HEAD_DIM)

    tables = axial_rope_tables(S)
    qa = apply_axial_rope(rmsnorm(qa, q_norm_a), tables).reshape(B, S, N_KV_A, G_A, HEAD_DIM)
    ka = apply_axial_rope(rmsnorm(ka, k_norm_a), tables)
    ya = global_attention(qa, ka, va)

    yb = window_attention(qb, kb, vb, sink_b, rel_bias)

    gates = jax.nn.sigmoid((h @ w_gate + b_gate).astype(jnp.float32)).astype(x.dtype)
    mix = gates[..., :D_MODEL] * (ya @ w_branch_a) + gates[..., D_MODEL:] * (yb @ w_branch_b)
    x = x + rmsnorm(mix @ w_out, norm_mix_post)

    h2 = rmsnorm(x, norm_ffn_pre)
    f = (jax.nn.silu(h2 @ w_ffn_gate) * (h2 @ w_ffn_up)) @ w_ffn_down
    return x + rmsnorm(f, norm_ffn_post)


def setup_inputs(seed: int = 0) -> dict:
    key = jax.random.key(seed)
    ks = jax.random.split(key, 24)
    f32 = jnp.float32

    def w(k, shape, fan_in):
        return jax.random.normal(k, shape, f32) * (fan_in ** -0.5)

    def gain(k, shape):
        return 1.0 + 0.05 * jax.random.normal(k, shape, f32)

    return {
        "x_prompt": jax.random.normal(ks[0], (BATCH, SEQ, D_MODEL), f32),
        "x_sample": jax.random.normal(ks[1], (DEC_BATCH, DEC_SEQ, D_MODEL), f32),
        "norm_mix_pre": gain(ks[2], (DEPTH, D_MODEL)),
        "norm_mix_post": gain(ks[3], (DEPTH, D_MODEL)),
        "w_in": w(ks[4], (DEPTH, D_MODEL, IN_WIDTH), D_MODEL),
        "q_norm_a": gain(ks[5], (DEPTH, HEAD_DIM)),
        "k_norm_a": gain(ks[6], (DEPTH, HEAD_DIM)),
        "sink_b": 0.5 * jax.random.normal(ks[7], (DEPTH, N_HEADS_B), f32),
        "rel_bias": 0.1 * jax.random.normal(ks[8], (N_BUCKETS, N_HEADS_B), f32),
        "w_branch_a": w(ks[9], (DEPTH, WIDTH_A, D_MODEL), WIDTH_A),
        "w_branch_b": w(ks[10], (DEPTH, WIDTH_B, D_MODEL), WIDTH_B),
        "w_gate": w(ks[11], (DEPTH, D_MODEL, 2 * D_MODEL), D_MODEL),
        "b_gate": 0.02 * jax.random.normal(ks[12], (DEPTH, 2 * D_MODEL), f32),
        "w_out": w(ks[13], (DEPTH, D_MODEL, D_MODEL), D_MODEL),
        "norm_ffn_pre": gain(ks[14], (DEPTH, D_MODEL)),
        "norm_ffn_post": gain(ks[15], (DEPTH, D_MODEL)),
        "w_ffn_gate": w(ks[16], (DEPTH, D_MODEL, D_FF), D_MODEL),
        "w_ffn_up": w(ks[17], (DEPTH, D_MODEL, D_FF), D_MODEL),
        "w_ffn_down": w(ks[18], (DEPTH, D_FF, D_MODEL), D_FF),
    }


def reference(x_prompt, x_sample, norm_mix_pre, norm_mix_post, w_in, q_norm_a, k_norm_a, sink_b,
              rel_bias, w_branch_a, w_branch_b, w_gate, b_gate, w_out,
              norm_ffn_pre, norm_ffn_post, w_ffn_gate, w_ffn_up, w_ffn_down):
    y_prompt = x_prompt
    y_sample = x_sample
    for l in range(DEPTH):
        layer_args = (norm_mix_pre[l], norm_mix_post[l], w_in[l], q_norm_a[l], k_norm_a[l], sink_b[l],
                      rel_bias, w_branch_a[l], w_branch_b[l], w_gate[l], b_gate[l], w_out[l],
                      norm_ffn_pre[l], norm_ffn_post[l], w_ffn_gate[l], w_ffn_up[l], w_ffn_down[l])
        y_prompt = encoder_layer(y_prompt, *layer_args)
        y_sample = encoder_layer(y_sample, *layer_args)
    return (y_prompt, y_sample)
```

```python
import math
from contextlib import ExitStack

import numpy as np
import concourse.bass as bass
import concourse.mybir as mybir
from concourse.bass_utils import run_bass_kernel_spmd

F32 = mybir.dt.float32
BF16 = mybir.dt.bfloat16
AF = mybir.ActivationFunctionType
ALU = mybir.AluOpType
AX = mybir.AxisListType

D = 1024
KC = 8
DFF = 2816
NJ = 22
CH = 4
EPS = 1e-6
NBLK = 30
BLK_KV, BLK_QA, BLK_QB, BLK_MG, BLK_WO, BLK_GU, BLK_WD = 0, 1, 2, 3, 11, 13, 24
NS = 4
SEM_CAP = 8000


class Buf:
    __slots__ = ("name", "w", "r", "excl")

    def __init__(self, name, excl=False):
        self.name = name
        self.w = None
        self.r = []
        self.excl = excl


class Op:
    __slots__ = ("eng", "fn", "deps", "dma", "needed", "ms", "sem", "val", "pre")

    def __init__(self, eng, fn, dma):
        self.eng = eng
        self.fn = fn
        self.dma = dma
        self.deps = []
        self.needed = False
        self.ms = 0
        self.sem = None
        self.val = 0
        self.pre = None


ENGS = ("pe", "act", "dve", "pool", "sp")


class Prog:
    def __init__(self):
        self.ops = {e: [] for e in ENGS}

    cut = False
    stop_at = None

    def mark(self, name):
        if self.stop_at is not None and name == self.stop_at:
            self.cut = True

    def add(self, eng, fn, reads=(), writes=(), dma=False):
        op = Op(eng, fn, dma)
        if self.cut:
            return op
        ex = [b for b in reads if b.excl and b not in writes]
        if ex:
            writes = list(writes) + ex
        raw = set()
        oth = set()
        for b in reads:
            if b.w is not None:
                raw.add(b.w)
        for b in writes:
            if b.w is not None:
                oth.add(b.w)
            for r in b.r:
                oth.add(r)
        for d in raw | oth:
            if d is op:
                continue
            if (not d.dma) and (not dma) and d.eng == eng:
                if eng == "pe":
                    continue
            op.deps.append(d)
            d.needed = True
        for b in reads:
            if not dma:
                b.r = [r for r in b.r if r.dma or r.eng != eng]
            b.r.append(op)
        for b in writes:
            b.w = op
            b.r = []
        self.ops[eng].append(op)
        return op

    def prepare(self, nc, es, dma_ring_sizes):
        csems = {}
        self.csems = csems
        for e in ENGS:
            n = sum(1 for o in self.ops[e] if o.needed and not o.dma)
            ne = max(1, (n + SEM_CAP - 1) // SEM_CAP)
            csems[e] = [es.enter_context(nc.semaphore(f"c_{e}_{i}")) for i in range(ne)]
            cnt = 0
            for o in self.ops[e]:
                if o.needed and not o.dma:
                    cnt += 1
                    o.ms = cnt
        for e in ENGS:
            R = dma_ring_sizes.get(e, 0)
            if R == 0:
                assert not any(o.dma for o in self.ops[e])
                continue
            ring = [es.enter_context(nc.semaphore(f"d_{e}_{i}")) for i in range(R)]
            i = 0
            for o in self.ops[e]:
                if o.dma:
                    o.sem = ring[i % R]
                    o.val = 16 * (i // R + 1)
                    o.pre = (ring[i % R], 16 * (i // R)) if i >= R else None
                    i += 1

    def emit(self, block):
        csems = self.csems

        def run_engine(eng_name, eobj):
            seen_c = {e: 0 for e in ENGS}
            seen_d = {}
            for o in self.ops[eng_name]:
                for d in o.deps:
                    if d.dma:
                        k = id(d.sem)
                        if seen_d.get(k, 0) >= d.val:
                            continue
                        eobj.wait_ge(d.sem, d.val)
                        seen_d[k] = d.val
                    else:
                        if seen_c[d.eng] >= d.ms:
                            continue
                        ep = (d.ms - 1) // SEM_CAP
                        eobj.wait_ge(csems[d.eng][ep], (d.ms - 1) % SEM_CAP + 1)
                        seen_c[d.eng] = d.ms
                if o.dma and o.pre is not None:
                    k = id(o.pre[0])
                    if seen_d.get(k, 0) < o.pre[1]:
                        eobj.wait_ge(o.pre[0], o.pre[1])
                        seen_d[k] = o.pre[1]
                ins = o.fn(eobj)
                if o.dma:
                    ins.then_inc(o.sem, 16)
                elif o.needed:
                    ep = (o.ms - 1) // SEM_CAP
                    ins.then_inc(csems[eng_name][ep], 1)
            last = {}
            for o in self.ops[eng_name]:
                if o.dma:
                    last[id(o.sem)] = (o.sem, o.val)
            for sem, val in last.values():
                if seen_d.get(id(sem), 0) < val:
                    eobj.wait_ge(sem, val)

        @block.tensor
        def _(t):
            run_engine("pe", t)

        @block.scalar
        def _(s):
            run_engine("act", s)

        @block.vector
        def _(v):
            run_engine("dve", v)

        @block.gpsimd
        def _(g):
            run_engine("pool", g)

        @block.sync
        def _(sy):
            run_engine("sp", sy)


class Unit:
    def __init__(self, n_own, n_oth, halo):
        self.n_own, self.n_oth, self.halo = n_own, n_oth, halo
        self.n_rows = (n_own + n_oth + (2 if halo else 0)) * 128


def build(units, stop_at=None):
    nc = bass.Bass("TRN2", target_bir_lowering=False)
    NROW = sum(u.n_rows for u in units)
    NOUT = sum(u.n_own for u in units) * 128
    MAXKV = max(u.n_own + u.n_oth for u in units)
    MAXKB = max(u.n_own + 2 for u in units)

    xin = nc.dram_tensor("xin", [NROW, D], F32, kind="ExternalInput").ap()
    rope = nc.dram_tensor("rope", [NROW, 128], F32, kind="ExternalInput").ap()
    wsrc = nc.dram_tensor("wsrc", [NBLK, 128, 4096], F32, kind="ExternalInput").ap()
    c128 = nc.dram_tensor("c128", [128, 288], F32, kind="ExternalInput").ap()
    crow = nc.dram_tensor("crow", [1, 2184], F32, kind="ExternalInput").ap()
    flags = nc.dram_tensor("flags", [1, 2], F32, kind="ExternalInput").ap()
    ohx = nc.dram_tensor("ohx", [33, 512], F32, kind="ExternalInput").ap()
    rbaug = nc.dram_tensor("rbaug", [33, 8], F32, kind="ExternalInput").ap()
    yout = nc.dram_tensor("yout", [NOUT, D], F32, kind="ExternalOutput").ap()
    wsc = nc.dram_tensor("wsc", [NBLK, 128, 4096], BF16, kind="Internal").ap()
    wvd = nc.dram_tensor("wvd", [8, 512], F32, kind="Internal").ap()

    P = Prog()
    P.stop_at = stop_at
    es = ExitStack()
    with es:
        def sb(name, shape, dt):
            return es.enter_context(nc.sbuf_tensor(name, shape, dt))

        ring = [sb(f"ring{i}", [128, 4096], BF16) for i in range(NS)]
        kaT = sb("kaT", [128, MAXKV * 128], BF16)
        va = sb("va", [128, MAXKV, 192], BF16)
        kbT = sb("kbT", [128, MAXKB * 128], BF16)
        vb = sb("vb", [128, MAXKB, 192], BF16)
        xc = sb("xc", [128, 2 * CH, D], F32)
        xn = sb("xn", [128, 2, D], BF16)
        hT = sb("hT", [128, KC, 512], BF16)
        fs = [sb(f"fs{i}", [128, D], F32) for i in range(4)]
        qn16 = sb("qn16", [128, 2, 512], BF16)
        ktmp = sb("ktmp", [128, 2, 2, 256], BF16)
        qaT = sb("qaT", [128, 4, 512], BF16)
        qbT = sb("qbT", [128, 4, 512], BF16)
        pt = sb("pt", [128, 6, 1024], BF16)
        pb0 = sb("pb0", [128, 2, 1024], BF16)
        yaT = sb("yaT", [128, 4, 512], BF16)
        ybT = sb("ybT", [128, 4, 512], BF16)
        aT = sb("aT", [128, NJ, 512], BF16)
        mixT = aT[:, NJ - KC:NJ, :]
        Et = sb("Et", [128, 8, 384], BF16)
        hT1 = sb("hT1", [128, KC, 256], BF16)
        cpk = sb("cpk", [128, 288], F32)
        crb = sb("crb", [128, 2184], F32)
        flg = sb("flg", [128, 2], F32)
        ident = sb("ident", [128, 128], BF16)
        ropet = sb("ropet", [128, 4, 128], F32)
        stat = sb("stat", [128, 12, 64], F32)
        epsb = sb("epsb", [128, 1], F32)
        stA = sb("stA", [128, 2, 64], F32)
        stP = sb("stP", [128, 2, 64], F32)
        expsink = sb("expsink", [128, 8], F32)
        rba_s = sb("rba_s", [33, 8], F32)

        rc = fs[1][:].rearrange("p (j q) -> p j q", j=2)
        gab = pb0[:].rearrange("p k (a q) -> p k a q", a=2)
        sg = pt[:, 2:4, 0:512]
        junk = pt[:, 4:6, :]
        PSALL = es.enter_context(nc.psum_tensor("psall", [128, 4096], F32))
        PSALLb = PSALL[:].bitcast(BF16)
        PS = [PSALL[:, i * 1024:(i + 1) * 1024] for i in range(4)]
        PSb = [PSALLb[:, i * 2048:(i + 1) * 2048] for i in range(4)]

        B_ring = [Buf(f"ring{i}") for i in range(NS)]
        B_fs = [[Buf(f"fs{i}a"), Buf(f"fs{i}b")] for i in range(4)]
        B_pt = [Buf(f"pt{i}") for i in range(6)]
        B_pb0 = [Buf("pb0a"), Buf("pb0b")]
        B_wsc = [Buf(f"wsc{i}") for i in range(NBLK)]
        B_kaT = [Buf(f"kaT{i}") for i in range(MAXKV)]
        B_va = [Buf(f"va{i}") for i in range(MAXKV)]
        B_kbT = [Buf(f"kbT{i}") for i in range(MAXKB)]
        B_vb = [Buf(f"vb{i}") for i in range(MAXKB)]
        B_xc = [Buf(f"xc{i}") for i in range(2 * CH)]
        B_xn = [Buf(f"xn{i}") for i in range(2)]
        B_hT = [Buf(f"hT{i}") for i in range(CH)]
        B_qn = Buf("qn16a"), Buf("qn16b")
        B_kt = [[Buf("kt00"), Buf("kt01")], [Buf("kt10"), Buf("kt11")]]
        B_qT = [Buf(f"qT{i}") for i in range(CH)]
        B_pb = Buf("pb")
        B_yaT = [Buf(f"yaT{i}") for i in range(4)]
        B_ybT = [Buf(f"ybT{i}") for i in range(CH)]
        B_rc = B_fs[1]
        B_gab = B_pb0
        B_aT = [Buf(f"aT{i}") for i in range(NJ)]
        B_mixT = B_aT[NJ - KC:NJ]
        B_sg = B_pt[2:4]
        B_const = Buf("const")
        B_E = Buf("E")
        B_rope = [Buf(f"rope{i}") for i in range(4)]
        B_stat = [Buf(f"stat{i}") for i in range(12)]
        B_ps = [[Buf(f"ps{i}a", True), Buf(f"ps{i}b", True)] for i in range(4)]
        B_misc = [Buf(f"misc{i}") for i in range(8)]
        B_wvd = Buf("wvd")
        B_stA = [Buf("stA0"), Buf("stA1")]
        B_hT1 = [Buf("hT1a"), Buf("hT1b")]
        B_stP = [Buf("stP0"), Buf("stP1")]
        B_hk = [Buf("hk0"), Buf("hk1")]

        def psb(i, h):
            return PS[i][:, h * 512:(h + 1) * 512]

        def psb16(i, h):
            return PSb[i][:, h * 1024:(h + 1) * 1024]

        identf = cpk[:, 0:128]
        Jf = cpk[:, 128:256]
        g1T = cpk[:, 256:264]
        g3T = cpk[:, 264:272]
        bgT = cpk[:, 272:288]
        g2b = crb[:, 0:1024]
        g4b = crb[:, 1024:2048]
        gq64 = crb[:, 2048:2112]
        gk64 = crb[:, 2112:2176]
        sinkb = crb[:, 2176:2184]

        stat_i = [0]
        B_junk = B_pt[4:6]
        junk_i = [0]

        def new_junk():
            i = junk_i[0] % 2
            junk_i[0] += 1
            return junk[:, i, :], B_junk[i]

        def new_stat():
            i = stat_i[0] % 12
            stat_i[0] += 1
            return stat[:, i, :], B_stat[i]

        P.add("sp", lambda e: e.dma_start(out=cpk[:], in_=c128), writes=[B_const], dma=True)
        P.add("sp", lambda e: e.dma_start(out=crb[:], in_=crow.partition_broadcast(128)), writes=[B_misc[0]], dma=True)
        P.add("sp", lambda e: e.dma_start(out=flg[:], in_=flags.partition_broadcast(128)), writes=[B_misc[1]], dma=True)
        P.add("sp", lambda e: e.dma_start(out=rba_s[:], in_=rbaug), writes=[B_misc[3]], dma=True)
        P.add("dve", lambda e: e.memset(epsb[:], EPS), writes=[B_misc[4]])
        P.add("dve", lambda e: e.tensor_copy(out=ident[:], in_=identf), reads=[B_const], writes=[B_misc[5]])
        B_ident = B_misc[5]
        B_crb = B_misc[0]
        B_ones = Buf("ones")
        P.add("pool", lambda e: e.memset(va[:, :, 64:128], 1.0), writes=[B_ones])
        P.add("pool", lambda e: e.memset(vb[:, :, 64:128], 1.0), writes=[B_ones])

        P.mark("consts")
        seq = []
        ws_dry = [True]
        ws = {"issued": 0, "cons": 0}
        ws_done = set()

        def ws_prefetch():
            if ws_dry[0]:
                return
            while ws["issued"] < len(seq) and ws["issued"] < ws["cons"] + NS:
                i = ws["issued"]
                blk = seq[i]
                k = i % NS
                if blk not in ws_done:
                    ws_done.add(blk)
                    P.add("pool", lambda e, k=k, blk=blk: e.dma_start(out=ring[k][:], in_=wsrc[blk]), writes=[B_ring[k]], dma=True)
                    P.add("sp", lambda e, k=k, blk=blk: e.dma_start(out=wsc[blk], in_=ring[k][:]),
                          reads=[B_ring[k]], writes=[B_wsc[blk]], dma=True)
                else:
                    P.add("sp", lambda e, k=k, blk=blk: e.dma_start(out=ring[k][:], in_=wsc[blk]),
                          reads=[B_wsc[blk]], writes=[B_ring[k]], dma=True)
                ws["issued"] += 1

        def ws_acquire(blk, ahead=0):
            if ws_dry[0]:
                seq.append(blk)
                return 0
            i = ws["cons"] + ahead
            assert seq[i] == blk, (i, seq[i], blk)
            if ahead == 0:
                ws_prefetch()
            return i % NS

        def ws_release():
            if ws_dry[0]:
                return
            ws["cons"] += 1
            ws_prefetch()

        P.mark("conv")
        B_Eh = B_misc[7]
        B_sink = B_misc[4]
        P.add("act", lambda e: e.activation(out=expsink[:], in_=sinkb, func=AF.Exp), reads=[B_crb], writes=[B_misc[4]])
        def emit_etable():
            P.add("sp", lambda e: e.dma_start(out=fs[3][0:33, 0:512], in_=ohx), writes=[B_fs[3][0]], dma=True)
            P.add("pe", lambda e: e.matmul(PS[0][0:8, 0:512], rba_s[0:33, 0:8], fs[3][0:33, 0:512], start=True, stop=True),
                  reads=[B_fs[3][0], B_misc[3]], writes=[B_ps[0][0]])
            P.add("dve", lambda e: e.tensor_copy(out=fs[2][0:8, 0:512], in_=PS[0][0:8, 0:512]), reads=[B_ps[0][0]], writes=[B_fs[2][0]])
            P.add("sp", lambda e: e.dma_start(out=wvd, in_=fs[2][0:8, 0:512]), reads=[B_fs[2][0]], writes=[B_wvd], dma=True)
            for h in range(8):
                k = h % 2
                hap = bass.AP(tensor=wvd.tensor, offset=h * 512, ap=[[1, 128], [1, 384]])
                P.add("sp", lambda e, k=k, hap=hap: e.dma_start(out=fs[k][:, 0:384], in_=hap), reads=[B_wvd], writes=[B_fs[k][0]], dma=True)
                P.add("pe", lambda e, k=k: e.matmul(PS[1 + k][:, 0:384], Jf, fs[k][:, 0:384], start=True, stop=True),
                      reads=[B_const, B_fs[k][0]], writes=[B_ps[1 + k][0]])
                P.add("act", lambda e, k=k, h=h: e.activation(out=Et[:, h, :], in_=PS[1 + k][:, 0:384], func=AF.Exp),
                      reads=[B_ps[1 + k][0]], writes=[B_E])

        P.mark("etable")
        def rstd_batch(ssq_ap, ssq_buf, n, inv_n):
            st, stb = new_stat()
            P.add("act", lambda e: e.activation(out=st[:, 0:n], in_=ssq_ap, func=AF.Sqrt, scale=inv_n, bias=epsb[:, 0:1]),
                  reads=[ssq_buf, B_sink], writes=[stb])
            P.add("dve", lambda e: e.reciprocal(out=st[:, 32:32 + n], in_=st[:, 0:n]),
                  reads=[stb], writes=[stb])
            return st[:, 32:32 + n], stb

        xn_i = [0]
        tp_i = [0]

        def norm_stats(srcs, st=None, stb=None):
            n = len(srcs)
            if st is None:
                st, stb = new_stat()
            for i, (sap, sbuf_) in enumerate(srcs):
                jk, jkb = new_junk()
                P.add("act", lambda e, sap=sap, i=i, jk=jk: e.activation(out=jk, in_=sap, func=AF.Square, accum_out=st[:, i:i + 1]),
                      reads=[sbuf_], writes=[stb, jkb])
            P.add("act", lambda e: e.activation(out=st[:, 16:16 + n], in_=st[:, 0:n], func=AF.Sqrt, scale=1.0 / D, bias=epsb[:, 0:1]),
                  reads=[stb, B_sink], writes=[stb])
            P.add("dve", lambda e: e.reciprocal(out=st[:, 32:32 + n], in_=st[:, 16:16 + n]),
                  reads=[stb], writes=[stb])
            return st[:, 32:32 + n], stb

        def norm_transpose(srcs, gT, hslots, tbanks=None, pre=None, hdst=None):
            n = len(srcs)
            rs, rsb = pre if pre is not None else norm_stats(srcs)
            info = []
            hten, hbufs = hdst if hdst is not None else (hT, B_hT)

            def do_xn(i):
                sap, sbuf_ = srcs[i]
                k = xn_i[0] % 2
                xn_i[0] += 1
                P.add("dve", lambda e: e.tensor_scalar(out=xn[:, k, :], in0=sap, scalar1=rs[:, i:i + 1], scalar2=None, op0=ALU.mult),
                      reads=[sbuf_, rsb], writes=[B_xn[k]])
                if tbanks is None:
                    pi, ph = tp_i[0] % 4, 0
                    tp_i[0] += 1
                else:
                    pi, ph = tbanks[i]
                info.append((k, pi, ph))

            def do_tr(i):
                k, pi, ph = info[i]
                pv = psb16(pi, ph)

                def tr(e):
                    ins = None
                    for kc in range(KC):
                        ins = e.transpose(pv[:, kc * 128:(kc + 1) * 128], xn[:, k, kc * 128:(kc + 1) * 128], ident[:])
                    return ins
                P.add("pe", tr, reads=[B_xn[k], B_ident], writes=[B_ps[pi][ph]])

            def do_ev(i):
                k, pi, ph = info[i]
                pv = psb16(pi, ph)
                hs = hslots[i]
                P.add("dve", lambda e: e.tensor_tensor(
                    out=hten[:, :, hs * 128:(hs + 1) * 128], in0=pv.rearrange("p (k t) -> p k t", k=KC),
                    in1=gT.unsqueeze(2).to_broadcast([128, KC, 128]), op=ALU.mult),
                    reads=[B_ps[pi][ph], B_const], writes=[hbufs[hs]])
            do_xn(0)
            if n > 1:
                do_xn(1)
            for i in range(n):
                do_tr(i)
                if i + 1 < n and i >= 1:
                    pass
                do_ev(i)
                if i + 2 < n:
                    do_xn(i + 2)

        def qk_post(ps_ap, ps_buf, H, g64, rope_ap, rope_buf, fsq, fsq_buf, t1_ap, t1_buf, ssq_ap, ssq_buf):
            W = H * 64
            sq = fsq[:, 0:W]
            xg = fsq[:, 512:512 + W]
            P.add("act", lambda e: e.activation(out=sq, in_=ps_ap, func=AF.Square), reads=[ps_buf], writes=[fsq_buf[0]])
            P.add("dve", lambda e: e.tensor_reduce(out=ssq_ap, in_=sq.rearrange("p (h d) -> p h d", d=64), axis=AX.X, op=ALU.add),
                  reads=[fsq_buf[0]], writes=[ssq_buf])
            P.add("dve", lambda e: e.tensor_tensor(out=xg.rearrange("p (h d) -> p h d", d=64), in0=ps_ap.rearrange("p (h d) -> p h d", d=64),
                                                   in1=g64.unsqueeze(1).to_broadcast([128, H, 64]), op=ALU.mult),
                  reads=[ps_buf, B_crb], writes=[fsq_buf[1]])
            cosb = rope_ap[:, 0:64].unsqueeze(1).to_broadcast([128, H, 64])
            sin4 = rope_ap[:, 64:128].rearrange("p (r h d) -> p r h d", r=2, h=2)
            xg5 = xg.rearrange("p (H r h d) -> p H r h d", r=2, h=2, d=16)
            t15 = t1_ap.rearrange("p (H r h d) -> p H r h d", r=2, h=2, d=16)
            P.add("pool", lambda e: e.tensor_tensor(out=t1_ap.rearrange("p (h d) -> p h d", d=64), in0=xg.rearrange("p (h d) -> p h d", d=64), in1=cosb, op=ALU.mult),
                  reads=[fsq_buf[1], rope_buf], writes=[t1_buf])
            P.add("pool", lambda e: e.tensor_tensor(out=sq.rearrange("p (H r h d) -> p H r h d", r=2, h=2, d=16)[:, :, :, 0, :], in0=xg5[:, :, :, 1, :],
                                                    in1=sin4[:, :, 0, :].unsqueeze(1).to_broadcast([128, H, 2, 16]), op=ALU.mult),
                  reads=[fsq_buf[1], rope_buf, fsq_buf[0], ssq_buf], writes=[fsq_buf[0]])
            P.add("pool", lambda e: e.tensor_tensor(out=sq.rearrange("p (H r h d) -> p H r h d", r=2, h=2, d=16)[:, :, :, 1, :], in0=xg5[:, :, :, 0, :],
                                                    in1=sin4[:, :, 1, :].unsqueeze(1).to_broadcast([128, H, 2, 16]), op=ALU.mult),
                  reads=[fsq_buf[1], rope_buf], writes=[fsq_buf[0]])
            P.add("pool", lambda e: e.tensor_tensor(out=t1_ap, in0=t1_ap, in1=sq, op=ALU.add),
                  reads=[t1_buf, fsq_buf[0]], writes=[t1_buf])

        def emit_main():
            row0 = 0
            out0 = 0
            gch = [0]
            def do_group(u, row0, tiles, g0):
                grp = tiles[g0:g0 + 2]
                p = (g0 // 2) % 2
                n = len(grp)
                W = n * 128
                kind = grp[0][0]
                idx0 = grp[0][1]
                assert all(k_ == kind for k_, _ in grp)
                hasA = kind != "halo"
                hasB = kind != "oth"
                S = {}

                def sa():
                    xo1 = CH * (gch[0] % 2)
                    S['xo1'] = xo1
                    srcs = []
                    for gi in range(n):
                        r = row0 + (g0 + gi) * 128
                        P.add("sp", lambda e, gi=gi, r=r, p=p: e.dma_start(out=xc[:, xo1 + 2 * p + gi, :], in_=xin[r:r + 128, :]), writes=[B_xc[xo1 + 2 * p + gi]], dma=True)
                        if hasA:
                            P.add("sp", lambda e, gi=gi, r=r, p=p: e.dma_start(out=ropet[:, 2 * p + gi, :], in_=rope[r:r + 128, :]), writes=[B_rope[2 * p + gi]], dma=True)
                        srcs.append((xc[:, xo1 + 2 * p + gi, :], B_xc[xo1 + 2 * p + gi]))
                    S['srcs'] = srcs
                    rs, rsb = norm_stats(srcs, st=stP[:, p, :], stb=B_stP[p])
                    for gi in range(n):
                        sap, sbuf_ = srcs[gi]
                        P.add("dve", lambda e, gi=gi, sap=sap: e.tensor_scalar(out=pt[:, gi, :], in0=sap, scalar1=rs[:, gi:gi + 1], scalar2=None, op0=ALU.mult),
                              reads=[sbuf_, rsb], writes=[B_pt[gi]])

                def sh2():
                    for gi in range(n):
                        pv = psb16(p, gi)

                        def tr(e, gi=gi, pv=pv):
                            ins = None
                            for kc in range(KC):
                                ins = e.transpose(pv[:, kc * 128:(kc + 1) * 128], pt[:, gi, kc * 128:(kc + 1) * 128], ident[:])
                            return ins
                        P.add("pe", tr, reads=[B_pt[gi], B_ident], writes=[B_ps[p][gi]])
                        P.add("dve", lambda e, gi=gi, pv=pv: e.tensor_tensor(
                            out=hT1[:, :, gi * 128:(gi + 1) * 128], in0=pv.rearrange("p (k t) -> p k t", k=KC),
                            in1=g1T.unsqueeze(2).to_broadcast([128, KC, 128]), op=ALU.mult),
                            reads=[B_ps[p][gi], B_const], writes=[B_hT1[gi]])

                def sb_():
                    kslot = ws_acquire(BLK_KV)
                    wkv = ring[kslot]
                    banks = [B_ps[2 + p][gi] for gi in range(n)]
                    for gi in range(n):
                        def kvproj(e, gi=gi):
                            ins = None
                            for kc in range(KC):
                                ins = e.matmul(psb(2 + p, gi), hT1[:, kc, gi * 128:(gi + 1) * 128], wkv[:, kc * 512:(kc + 1) * 512],
                                               start=(kc == 0), stop=(kc == KC - 1))
                            return ins
                        P.add("pe", kvproj, reads=[B_hT1[gi], B_ring[kslot]], writes=[banks[gi]])
                    ws_release()

                    pkv = PSALL[:, (2 + p) * 1024:(2 + p) * 1024 + n * 512].rearrange("p (t c) -> p t c", c=512)
                    W = n * 128
                    ropes = [B_rope[2 * p + gi] for gi in range(n)]
                    if hasA:
                        kst, kstb = new_stat()
                        sq = fs[1 + 2 * p][:, 0:W]
                        xg = fs[1 + 2 * p][:, 512:512 + W]
                        t1 = fs[2][:, p * 512:p * 512 + W]
                        P.add("act", lambda e: e.activation(out=sq.rearrange("p (t c) -> p t c", c=128), in_=pkv[:, :, 0:128], func=AF.Square),
                              reads=banks, writes=[B_fs[1 + 2 * p][0]])
                        P.add("dve", lambda e: e.tensor_reduce(out=kst[:, 0:2 * n], in_=sq.rearrange("p (h d) -> p h d", d=64), axis=AX.X, op=ALU.add),
                              reads=[B_fs[1 + 2 * p][0]], writes=[kstb])
                        P.add("dve", lambda e: e.tensor_tensor(out=xg.rearrange("p (t h d) -> p t h d", h=2, d=64),
                                                               in0=pkv[:, :, 0:128].rearrange("p t (h d) -> p t h d", d=64),
                                                               in1=gk64.unsqueeze(1).unsqueeze(1).to_broadcast([128, n, 2, 64]), op=ALU.mult),
                              reads=banks + [B_crb], writes=[B_fs[1 + 2 * p][1]])
                        P.add("pool", lambda e: e.tensor_tensor(out=t1.rearrange("p (t h d) -> p t h d", h=2, d=64),
                                                                in0=xg.rearrange("p (t h d) -> p t h d", h=2, d=64),
                                                                in1=ropet[:, 2 * p:2 * p + n, 0:64].unsqueeze(2).to_broadcast([128, n, 2, 64]), op=ALU.mult),
                              reads=[B_fs[1 + 2 * p][1]] + ropes, writes=[B_fs[2][p]])
                        sq6 = sq.rearrange("p (t h r f d) -> p t h r f d", h=2, r=2, f=2, d=16)
                        xg6 = xg.rearrange("p (t h r f d) -> p t h r f d", h=2, r=2, f=2, d=16)
                        sn5 = ropet[:, 2 * p:2 * p + n, 64:128].rearrange("p t (r f d) -> p t r f d", r=2, f=2)
                        for hd in range(2):
                            for hf in range(2):
                                P.add("pool", lambda e, hd=hd, hf=hf: e.tensor_tensor(out=sq6[:, :, hd, :, hf, :], in0=xg6[:, :, hd, :, 1 - hf, :],
                                                                                      in1=sn5[:, :, :, hf, :], op=ALU.mult),
                                      reads=[B_fs[1 + 2 * p][1], kstb] + ropes, writes=[B_fs[1 + 2 * p][0]])
                        P.add("pool", lambda e: e.tensor_tensor(out=t1, in0=t1, in1=sq, op=ALU.add),
                              reads=[B_fs[2][p], B_fs[1 + 2 * p][0]], writes=[B_fs[2][p]])
                        P.add("act", lambda e: e.activation(
                            out=va[:, idx0:idx0 + n, :].rearrange("p t (a d) -> p t a d", d=64)[:, :, 0:3:2, :],
                            in_=pkv[:, :, 128:256].rearrange("p t (a d) -> p t a d", d=64), func=AF.Copy),
                            reads=banks + [B_ones], writes=[B_va[idx0 + gi] for gi in range(n)])
                    if hasB:
                        P.add("act", lambda e: e.activation(out=ktmp[:, p, 1, 0:W].rearrange("p (t c) -> p t c", c=128), in_=pkv[:, :, 256:384], func=AF.Copy),
                              reads=banks, writes=[B_kt[p][1]])
                        P.add("dve", lambda e: e.tensor_copy(
                            out=vb[:, idx0:idx0 + n, :].rearrange("p t (a d) -> p t a d", d=64)[:, :, 0:3:2, :],
                            in_=pkv[:, :, 384:512].rearrange("p t (a d) -> p t a d", d=64)),
                            reads=banks + [B_ones], writes=[B_vb[idx0 + gi] for gi in range(n)])
                        if kind == "halo":
                            for gi in range(n):
                                P.add("dve", lambda e, gi=gi: e.memset(vb[:, idx0 + gi, 64:128], 1.0), reads=[B_ones], writes=[B_vb[idx0 + gi]])
                                P.add("dve", lambda e, gi=gi: e.tensor_scalar(out=vb[:, idx0 + gi, :], in0=vb[:, idx0 + gi, :], scalar1=flg[:, gi:gi + 1],
                                                                            scalar2=None, op0=ALU.mult),
                                      reads=[B_vb[idx0 + gi], B_misc[1]], writes=[B_vb[idx0 + gi]])
                    if hasA:
                        krs, krsb = rstd_batch(kst[:, 0:2 * n], kstb, 2 * n, 1.0 / 64)
                        P.add("pool", lambda e: e.tensor_tensor(out=ktmp[:, p, 0, 0:W].rearrange("p (h d) -> p h d", d=64), in0=t1.rearrange("p (h d) -> p h d", d=64),
                                                               in1=krs.unsqueeze(2).to_broadcast([128, 2 * n, 64]), op=ALU.mult),
                              reads=[B_fs[2][p], krsb], writes=[B_kt[p][0]])

                def sc():
                    pbk = 2 + p
                    pv = psb16(pbk, 0)

                    def trk(e):
                        ins = None
                        for gi in range(n):
                            if hasA:
                                ins = e.transpose(pv[:, gi * 128:(gi + 1) * 128], ktmp[:, p, 0, gi * 128:(gi + 1) * 128], ident[:])
                            if hasB:
                                ins = e.transpose(pv[:, 512 + gi * 128:512 + (gi + 1) * 128], ktmp[:, p, 1, gi * 128:(gi + 1) * 128], ident[:])
                        return ins
                    P.add("pe", trk, reads=[B_kt[p][0], B_kt[p][1], B_ident], writes=[B_ps[pbk][0]])
                    if hasA:
                        P.add("dve", lambda e: e.tensor_copy(out=kaT[:, idx0 * 128:(idx0 + n) * 128], in_=pv[:, 0:W]),
                              reads=[B_ps[pbk][0]], writes=[B_kaT[idx0 + gi] for gi in range(n)])
                    if hasB:
                        P.add("act", lambda e: e.activation(out=kbT[:, idx0 * 128:(idx0 + n) * 128], in_=pv[:, 512:512 + W], func=AF.Copy),
                              reads=[B_ps[pbk][0]], writes=[B_kbT[idx0 + gi] for gi in range(n)])
                return sa, sh2, sb_, sc

            def make_pass1(u, row0):
                tiles = [("own", i) for i in range(u.n_own)] + [("oth", u.n_own + i) for i in range(u.n_oth)]
                if u.halo:
                    tiles += [("halo", u.n_own), ("halo", u.n_own + 1)]
                groups = [do_group(u, row0, tiles, g0) for g0 in range(0, len(tiles), 2)]
                ng = len(groups)

                def mk(k):
                    def step():
                        if 0 <= k - 3 < ng:
                            groups[k - 3][3]()
                        if 0 <= k - 2 < ng:
                            groups[k - 2][2]()
                        if 0 <= k - 1 < ng:
                            groups[k - 1][1]()
                        if 0 <= k < ng:
                            groups[k][0]()
                    return step
                steps = [mk(k) for k in range(ng + 3)]
                return steps

            rows = []
            r_ = 0
            for u in units:
                rows.append(r_)
                r_ += u.n_rows
            for s_ in make_pass1(units[0], rows[0]):
                s_()
            emit_etable()
            for ui, u in enumerate(units):
                nkv = u.n_own + u.n_oth
                next_p1 = make_pass1(units[ui + 1], rows[ui + 1]) if ui + 1 < len(units) else []

                def p1hook(next_p1=next_p1):
                    if next_p1:
                        next_p1.pop(0)()
                def stageA(c, par, u=u, nkv=nkv, row0=row0, out0=out0):
                    T0 = c * CH
                    xo = CH * par
                    S = {}

                    def a0():
                        srcs = []
                        for t in range(CH):
                            r = row0 + (T0 + t) * 128
                            P.add("sp", lambda e, t=t, r=r: e.dma_start(out=xc[:, xo + t, :], in_=xin[r:r + 128, :]), writes=[B_xc[xo + t]], dma=True)
                            P.add("sp", lambda e, t=t, r=r: e.dma_start(out=ropet[:, t, :], in_=rope[r:r + 128, :]), writes=[B_rope[t]], dma=True)
                            srcs.append((xc[:, xo + t, :], B_xc[xo + t]))
                        S['srcs'] = srcs
                        S['pre'] = norm_stats(srcs, st=stA[:, par, :], stb=B_stA[par])

                    def a1():
                        norm_transpose(S['srcs'], g1T, list(range(CH)), pre=S['pre'])
                        sa = ws_acquire(BLK_QA)
                        sbq = ws_acquire(BLK_QB, ahead=1)
                        qst, qstb = new_stat()
                        S['qst'], S['qstb'] = qst, qstb
                        for t in range(CH):
                            def qproj(e, t=t):
                                ins = None
                                for kc in range(KC):
                                    e.matmul(psb(2, t % 2), hT[:, kc, t * 128:(t + 1) * 128], ring[sa][:, kc * 512:(kc + 1) * 512],
                                             start=(kc == 0), stop=(kc == KC - 1))
                                for kc in range(KC):
                                    ins = e.matmul(psb(3, t % 2), hT[:, kc, t * 128:(t + 1) * 128], ring[sbq][:, kc * 512:(kc + 1) * 512],
                                                   start=(kc == 0), stop=(kc == KC - 1))
                                return ins
                            P.add("pe", qproj, reads=[B_hT[t], B_ring[sa], B_ring[sbq]], writes=[B_ps[2][t % 2], B_ps[3][t % 2]])
                            qk_post(psb(2, t % 2), B_ps[2][t % 2], 8, gq64, ropet[:, t, :], B_rope[t], fs[t % 2], B_fs[t % 2],
                                    fs[2 + t // 2][:, (t % 2) * 512:(t % 2) * 512 + 512], B_fs[2 + t // 2][t % 2],
                                    qst[:, t * 8:t * 8 + 8], qstb)
                            P.add("act", lambda e, t=t: e.activation(out=qn16[:, 1, :], in_=psb(3, t % 2), func=AF.Copy),
                                  reads=[B_ps[3][t % 2]], writes=[B_qn[1]])
                            pvb = psb16(t % 2, 1)

                            def trb(e, pvb=pvb):
                                ins = None
                                for j in range(4):
                                    ins = e.transpose(pvb[:, j * 128:(j + 1) * 128], qn16[:, 1, j * 128:(j + 1) * 128], ident[:])
                                return ins
                            P.add("pe", trb, reads=[B_qn[1], B_ident], writes=[B_ps[t % 2][1]])
                            P.add("dve", lambda e, t=t, pvb=pvb: e.tensor_copy(out=qbT[:, :, t * 128:(t + 1) * 128],
                                                                              in_=pvb[:, 0:512].rearrange("p (j q) -> p j q", j=4)),
                                  reads=[B_ps[t % 2][1]], writes=[B_qT[t]])
                        ws_release()
                        ws_release()

                    def part2a():
                        qst, qstb = S['qst'], S['qstb']
                        qrs, qrsb = rstd_batch(qst[:, 0:32], qstb, 32, 1.0 / 64)
                        for t in range(CH):
                            t1 = fs[2 + t // 2][:, (t % 2) * 512:(t % 2) * 512 + 512]
                            P.add("dve", lambda e, t=t, t1=t1: e.tensor_tensor(
                                out=yaT[:, t, :].rearrange("p (h d) -> p h d", d=64), in0=t1.rearrange("p (h d) -> p h d", d=64),
                                in1=qrs[:, t * 8:t * 8 + 8].unsqueeze(2).to_broadcast([128, 8, 64]), op=ALU.mult),
                                reads=[B_fs[2 + t // 2][t % 2], qrsb], writes=[B_yaT[t]])

                    def part2b():
                        for t in range(CH):
                            pva = psb16(0, t % 2)

                            def tra(e, pva=pva, t=t):
                                ins = None
                                for j in range(4):
                                    ins = e.transpose(pva[:, j * 128:(j + 1) * 128], yaT[:, t, j * 128:(j + 1) * 128], ident[:])
                                return ins
                            P.add("pe", tra, reads=[B_yaT[t], B_ident], writes=[B_ps[0][t % 2]])
                            P.add("act", lambda e, t=t, pva=pva: e.activation(out=qaT[:, :, t * 128:(t + 1) * 128],
                                                                              in_=pva[:, 0:512].rearrange("p (j q) -> p j q", j=4), func=AF.Copy),
                                  reads=[B_ps[0][t % 2], B_qT[t]], writes=[B_qT[t]])
                    return a0, a1, part2a, part2b

                def chunk_rest(c, par, nextA, p1h, u=u, nkv=nkv, row0=row0, out0=out0):
                    T0 = c * CH
                    xo = CH * par
                    P.mark("stageA")
                    steps = [(jp, i) for jp in range(4) for i in range(nkv)]

                    def emit_qk(g):
                        jp, i = steps[g]
                        s = g % 2

                        def f(e, jp=jp, i=i, s=s):
                            e.matmul(psb(s, 0), kaT[0:64, i * 128:(i + 1) * 128], qaT[0:64, jp, :], start=True, stop=True)
                            return e.matmul(psb(s, 1), kaT[64:128, i * 128:(i + 1) * 128], qaT[64:128, jp, :], start=True, stop=True)
                        P.add("pe", f, reads=[B_kaT[i]] + B_qT, writes=[B_ps[s][0], B_ps[s][1]])

                    def emit_exp_pv(g):
                        jp, i = steps[g]
                        s = g % 2
                        k = g % 3
                        a = 2 + (jp % 2)
                        P.add("act", lambda e, s=s, k=k: e.activation(out=pt[:, k, :], in_=PS[s], func=AF.Exp, scale=0.125),
                              reads=[B_ps[s][0], B_ps[s][1]], writes=[B_pt[k]])

                        def f(e, i=i, k=k, a=a):
                            e.matmul(psb(a, 0), va[:, i, 0:128], pt[:, k, 0:512], start=(i == 0), stop=(i == nkv - 1))
                            return e.matmul(psb(a, 1), va[:, i, 64:192], pt[:, k, 512:1024], start=(i == 0), stop=(i == nkv - 1))
                        P.add("pe", f, reads=[B_va[i], B_pt[k], B_ones], writes=[B_ps[a][0], B_ps[a][1]])
                        if i == nkv - 1:
                            P.add("dve", lambda e, a=a: e.reciprocal(out=rc[64:128, 0, :], in_=PS[a][64:128, 0:512]),
                                  reads=[B_ps[a][0]], writes=[B_rc[0]])
                            P.add("dve", lambda e, a=a, jp=jp: e.tensor_tensor(out=yaT[0:64, jp, :], in0=PS[a][0:64, 0:512], in1=rc[64:128, 0, :], op=ALU.mult),
                                  reads=[B_ps[a][0], B_rc[0]], writes=[B_yaT[jp]])
                            P.add("dve", lambda e, a=a: e.reciprocal(out=rc[0:64, 1, :], in_=PS[a][0:64, 512:1024]),
                                  reads=[B_ps[a][1]], writes=[B_rc[1]])
                            P.add("dve", lambda e, a=a, jp=jp: e.tensor_tensor(out=yaT[64:128, jp, :], in0=PS[a][64:128, 512:1024], in1=rc[0:64, 1, :], op=ALU.mult),
                                  reads=[B_ps[a][1], B_rc[1], B_yaT[jp]], writes=[B_yaT[jp]])
                    emit_qk(0)
                    for g in range(len(steps)):
                        if g + 1 < len(steps):
                            emit_qk(g + 1)
                        emit_exp_pv(g)

                    P.mark("stageB")
                    qblks = []
                    for t in range(CH):
                        T = T0 + t
                        blks = []
                        for o in range(3):
                            kb = T + o - 1
                            if 0 <= kb < u.n_own:
                                blks.append((kb, ("E", 2 - o)))
                            elif u.halo:
                                blks.append((u.n_own + (0 if o == 0 else 1), ("H", 0 if o == 0 else 1)))
                        qblks.append(blks)
                    bcnt = [0]

                    def emit_bq(t):
                        for bi, (kb, esel) in enumerate(qblks[t]):
                            s = bcnt[0] % 2
                            bcnt[0] += 1
                            k = (t % 2) * 3 + bi

                            def f(e, t=t, kb=kb, s=s):
                                e.matmul(psb(s, 0), kbT[0:64, kb * 128:(kb + 1) * 128], qbT[0:64, :, t * 128:(t + 1) * 128], start=True, stop=True)
                                return e.matmul(psb(s, 1), kbT[64:128, kb * 128:(kb + 1) * 128], qbT[64:128, :, t * 128:(t + 1) * 128], start=True, stop=True)
                            P.add("pe", f, reads=[B_kbT[kb], B_qT[t]], writes=[B_ps[s][0], B_ps[s][1]])
                            P.add("act", lambda e, s=s: e.activation(out=pb0[:, s, :], in_=PS[s], func=AF.Exp, scale=0.125),
                                  reads=[B_ps[s][0], B_ps[s][1]], writes=[B_pb0[s]])
                            if esel[0] == "E":
                                eap = Et[:, :, esel[1] * 128:(esel[1] + 1) * 128]
                                ebuf = B_E
                            else:
                                hb_ = 2 if esel[1] == 0 else 0
                                eap = Et[:, :, hb_ * 128:(hb_ + 1) * 128]
                                ebuf = B_E
                            P.add("dve" if bi != 1 else "pool", lambda e, k=k, s=s, eap=eap: e.tensor_tensor(out=pt[:, k, :].rearrange("p (h q) -> p h q", h=8),
                                                                                     in0=pb0[:, s, :].rearrange("p (h q) -> p h q", h=8), in1=eap, op=ALU.mult),
                                  reads=[B_pb0[s], ebuf], writes=[B_pt[k]])

                    def emit_bpv(t):
                        blks = qblks[t]
                        nb = len(blks)
                        k0 = (t % 2) * 3

                        def f(e, blks=blks, nb=nb, k0=k0):
                            ins = None
                            for h in range(8):
                                for bi, (kb, esel) in enumerate(blks):
                                    if h < 4:
                                        o_ = PS[2][:, h * 65:(h + 1) * 65]
                                        r_ = vb[:, kb, 0:65]
                                    else:
                                        o_ = PS[2][:, 512 + (h - 4) * 65:512 + (h - 3) * 65]
                                        r_ = vb[:, kb, 127:192]
                                    ins = e.matmul(o_, pt[:, k0 + bi, h * 128:(h + 1) * 128], r_, start=(bi == 0), stop=(bi == nb - 1))
                            return ins
                        P.add("pe", f, reads=[B_vb[kb] for kb, _ in blks] + [B_pt[k0 + bi] for bi in range(nb)] + [B_ones],
                              writes=[B_ps[2][0], B_ps[2][1]])
                        st, stb = new_stat()
                        a0 = PS[2][:, 0:260].rearrange("p (h c) -> p h c", c=65)
                        a1 = PS[2][:, 512:772].rearrange("p (h c) -> p h c", c=65)
                        P.add("dve", lambda e: e.tensor_tensor(out=st[:, 0:4].unsqueeze(2), in0=a0[:, :, 64:65],
                                                               in1=expsink[:, 0:4].unsqueeze(2), op=ALU.add),
                              reads=[B_ps[2][0], B_sink], writes=[stb])
                        P.add("dve", lambda e: e.tensor_tensor(out=st[:, 4:8].unsqueeze(2), in0=a1[:, :, 0:1],
                                                               in1=expsink[:, 4:8].unsqueeze(2), op=ALU.add),
                              reads=[B_ps[2][1], B_sink, stb], writes=[stb])
                        P.add("dve", lambda e: e.reciprocal(out=st[:, 8:16], in_=st[:, 0:8]), reads=[stb], writes=[stb])
                        yv = qn16[:, 0, :].rearrange("p (j k d) -> p j k d", j=4, k=2)
                        P.add("dve", lambda e: e.tensor_tensor(out=yv[:, :, 0, :], in0=a0[:, :, 0:64],
                                                               in1=st[:, 8:12].unsqueeze(2).to_broadcast([128, 4, 64]), op=ALU.mult),
                              reads=[B_ps[2][0], stb], writes=[B_qn[0]])
                        P.add("dve", lambda e: e.tensor_tensor(out=yv[:, :, 1, :], in0=a1[:, :, 1:65],
                                                               in1=st[:, 12:16].unsqueeze(2).to_broadcast([128, 4, 64]), op=ALU.mult),
                              reads=[B_ps[2][1], stb, B_qn[0]], writes=[B_qn[0]])
                        pvy = psb16(3, t % 2)

                        def try_(e):
                            ins = None
                            for j in range(4):
                                ins = e.transpose(pvy[:, j * 128:(j + 1) * 128], qn16[:, 0, j * 128:(j + 1) * 128], ident[:])
                            return ins
                        P.add("pe", try_, reads=[B_qn[0], B_ident], writes=[B_ps[3][t % 2]])
                        P.add("act", lambda e: e.activation(out=ybT[:, :, t * 128:(t + 1) * 128],
                                                            in_=pvy[:, 0:512].rearrange("p (j q) -> p j q", j=4), func=AF.Copy),
                              reads=[B_ps[3][t % 2]], writes=[B_ybT[t]])
                    emit_bq(0)
                    for t in range(CH):
                        if t + 1 < CH:
                            emit_bq(t + 1)
                        emit_bpv(t)

                    P.mark("stageB2")
                    nxt2 = nextA() if nextA is not None else None
                    if nxt2 is not None:
                        nxt2[0]()
                    for f_ in range(8):
                        sl = ws_acquire(BLK_MG + f_)
                        w = ring[sl]
                        pg, pz = (0, 1) if f_ % 2 == 0 else (2, 3)

                        def mg(e, w=w, pg=pg, pz=pz):
                            ins = None
                            for kc in range(KC):
                                e.matmul(psb(pg, 0), w[:, kc * 128:(kc + 1) * 128], hT[:, kc, :], start=(kc == 0), stop=(kc == KC - 1))
                            for kc in range(KC):
                                e.matmul(psb(pg, 1), w[:, 1024 + kc * 128:1024 + (kc + 1) * 128], hT[:, kc, :], start=(kc == 0), stop=(kc == KC - 1))
                            for pc in range(4):
                                e.matmul(psb(pz, 0), w[:, 2048 + pc * 128:2048 + (pc + 1) * 128], yaT[:, pc, :], start=(pc == 0), stop=(pc == 3))
                            for pc in range(4):
                                ins = e.matmul(psb(pz, 1), w[:, 2560 + pc * 128:2560 + (pc + 1) * 128], ybT[:, pc, :], start=(pc == 0), stop=(pc == 3))
                            return ins
                        P.add("pe", mg, reads=[B_ring[sl]] + B_hT + B_yaT + B_ybT,
                              writes=[B_ps[pg][0], B_ps[pg][1], B_ps[pz][0], B_ps[pz][1]])
                        ws_release()
                        k = f_ % 2
                        P.add("act", lambda e, pg=pg, k=k, f_=f_: e.activation(out=gab[:, k, 0, :], in_=psb(pg, 0), func=AF.Sigmoid, bias=bgT[:, f_:f_ + 1]),
                              reads=[B_ps[pg][0], B_const], writes=[B_gab[k]])
                        P.add("act", lambda e, pg=pg, k=k, f_=f_: e.activation(out=gab[:, k, 1, :], in_=psb(pg, 1), func=AF.Sigmoid, bias=bgT[:, 8 + f_:9 + f_]),
                              reads=[B_ps[pg][1], B_const], writes=[B_gab[k]])
                        P.add("dve", lambda e, pz=pz, k=k: e.tensor_tensor(out=fs[0][:, 0:512], in0=psb(pz, 0), in1=gab[:, k, 0, :], op=ALU.mult),
                              reads=[B_ps[pz][0], B_gab[k]], writes=[B_fs[0][0]])
                        P.add("dve", lambda e, pz=pz, k=k: e.tensor_tensor(out=fs[0][:, 512:1024], in0=psb(pz, 1), in1=gab[:, k, 1, :], op=ALU.mult),
                              reads=[B_ps[pz][1], B_gab[k]], writes=[B_fs[0][1]])
                        P.add("pool", lambda e, f_=f_: e.tensor_tensor(out=mixT[:, f_, :], in0=fs[0][:, 0:512], in1=fs[0][:, 512:1024], op=ALU.add),
                              reads=[B_fs[0][0], B_fs[0][1]], writes=[B_mixT[f_]])

                    P.mark("stageC1")
                    for h in range(2):
                        sl = ws_acquire(BLK_WO + h)
                        w = ring[sl]
                        for t in range(CH):
                            def wo(e, w=w, t=t, h=h):
                                ins = None
                                for kc in range(KC):
                                    ins = e.matmul(psb(t, h), mixT[:, kc, t * 128:(t + 1) * 128], w[:, kc * 512:(kc + 1) * 512],
                                                   start=(kc == 0), stop=(kc == KC - 1))
                                return ins
                            P.add("pe", wo, reads=[B_ring[sl]] + B_mixT, writes=[B_ps[t][h]])
                        ws_release()

                    def post_norm_residual(gb, dst_out):
                        st, stb = new_stat()
                        for t in range(CH):
                            jk, jkb = new_junk()
                            P.add("act", lambda e, t=t, jk=jk: e.activation(out=jk, in_=PS[t], func=AF.Square, accum_out=st[:, t:t + 1]),
                                  reads=[B_ps[t][0], B_ps[t][1]], writes=[stb, jkb])
                            P.add("dve", lambda e, t=t: e.tensor_tensor(out=fs[t][:], in0=PS[t], in1=gb, op=ALU.mult),
                                  reads=[B_ps[t][0], B_ps[t][1], B_crb], writes=[B_fs[t][0], B_fs[t][1]])
                        rs, rsb = rstd_batch(st[:, 0:CH], stb, CH, 1.0 / D)
                        for t in range(CH):
                            P.add("dve", lambda e, t=t: e.scalar_tensor_tensor(out=xc[:, xo + t, :], in0=fs[t][:], scalar=rs[:, t:t + 1],
                                                                                in1=xc[:, xo + t, :], op0=ALU.mult, op1=ALU.add),
                                  reads=[B_xc[xo + t], B_fs[t][0], B_fs[t][1], rsb], writes=[B_xc[xo + t]])
                            if dst_out is not None:
                                r = dst_out + t * 128
                                P.add("pool", lambda e, t=t, r=r: e.dma_start(out=yout[r:r + 128, :], in_=xc[:, xo + t, :]), reads=[B_xc[xo + t]], dma=True)
                    post_norm_residual(g2b, None)

                    P.mark("stageC2")
                    norm_transpose([(xc[:, xo + t, :], B_xc[xo + t]) for t in range(CH)], g3T, list(range(CH)))
                    for b in range(11):
                        sl = ws_acquire(BLK_GU + b)
                        w = ring[sl]
                        for jj in range(2):
                            j = 2 * b + jj
                            pi = j % 4

                            def gu(e, w=w, jj=jj, pi=pi):
                                ins = None
                                for kc in range(KC):
                                    e.matmul(psb(pi, 0), w[:, jj * 2048 + kc * 128:jj * 2048 + (kc + 1) * 128], hT[:, kc, :], start=(kc == 0), stop=(kc == KC - 1))
                                for kc in range(KC):
                                    ins = e.matmul(psb(pi, 1), w[:, jj * 2048 + 1024 + kc * 128:jj * 2048 + 1024 + (kc + 1) * 128], hT[:, kc, :],
                                                   start=(kc == 0), stop=(kc == KC - 1))
                                return ins
                            P.add("pe", gu, reads=[B_ring[sl]] + B_hT, writes=[B_ps[pi][0], B_ps[pi][1]])
                            k = j % 2
                            P.add("act", lambda e, pi=pi, k=k: e.activation(out=sg[:, k, :], in_=psb(pi, 0), func=AF.Silu),
                                  reads=[B_ps[pi][0]], writes=[B_sg[k]])
                            P.add("dve", lambda e, pi=pi, k=k, j=j: e.tensor_tensor(out=aT[:, j, :], in0=psb(pi, 1), in1=sg[:, k, :], op=ALU.mult),
                                  reads=[B_ps[pi][1], B_sg[k]], writes=[B_aT[j]])
                        ws_release()
                        if p1h is not None:
                            p1h()
                    if nxt2 is not None:
                        nxt2[1]()
                    for wbk in range(6):
                        sl = ws_acquire(BLK_WD + wbk)
                        w = ring[sl]
                        for jj in range(4):
                            j = 4 * wbk + jj
                            if j >= NJ:
                                break

                            def dn(e, w=w, jj=jj, j=j):
                                ins = None
                                for t in range(CH):
                                    for h in range(2):
                                        ins = e.matmul(psb(t, h), aT[:, j, t * 128:(t + 1) * 128], w[:, jj * 1024 + h * 512:jj * 1024 + (h + 1) * 512],
                                                       start=(j == 0), stop=(j == NJ - 1))
                                return ins
                            P.add("pe", dn, reads=[B_ring[sl], B_aT[j]], writes=[B_ps[t][h] for t in range(CH) for h in range(2)])
                        ws_release()
                    if nxt2 is not None:
                        nxt2[2]()
                    post_norm_residual(g4b, out0 + T0 * 128)
                    if nxt2 is not None:
                        nxt2[3]()
                nch = u.n_own // CH
                par0 = gch[0] % 2
                for f_ in stageA(0, par0):
                    f_()
                for c in range(nch):
                    par = gch[0] % 2
                    gch[0] += 1
                    nextA = (lambda c=c, par=par: stageA(c + 1, 1 - par)) if c + 1 < nch else None
                    chunk_rest(c, par, nextA, p1hook if c + 1 == nch else None)
                while next_p1:
                    next_p1.pop(0)()
                row0 += u.n_rows
                out0 += u.n_own * 128

        P.cut = True
        emit_main()
        P.cut = False
        ws_dry[0] = False
        emit_main()
        P.prepare(nc, es, {"sp": 16, "pool": 40})
        block = es.enter_context(nc.Block())
        P.emit(block)
    return nc


HEAD_DIM = 64
GRID_W = 64
ROPE_THETA = 10000.0
ROPE_HALF = 32
N_BUCKETS = 32
MAX_DISTANCE = 128
PAIR_ORDER = [0, 4, 1, 5, 2, 6, 3, 7]


def _rope_table(S):
    import jax
    import jax.numpy as jnp
    with jax.default_device(jax.devices("cpu")[0]):
        ROWS = S // GRID_W
        pos = jnp.arange(S)
        rows = jnp.repeat(jnp.arange(ROWS), GRID_W).astype(jnp.float32)
        cols = (pos % GRID_W).astype(jnp.float32)
        inv_freq = 1.0 / (ROPE_THETA ** (jnp.arange(0, ROPE_HALF, 2, dtype=jnp.float32) / ROPE_HALF))
        fr = rows[:, None] * inv_freq[None, :]
        fc = cols[:, None] * inv_freq[None, :]
        er = jnp.concatenate([fr, fr], axis=-1)
        ec = jnp.concatenate([fc, fc], axis=-1)
        cr, sr, cc, sc = (np.asarray(a, dtype=np.float32) for a in (jnp.cos(er), jnp.sin(er), jnp.cos(ec), jnp.sin(ec)))
    sgn = np.concatenate([-np.ones(16, np.float32), np.ones(16, np.float32)])
    return np.concatenate([cr, cc, sr * sgn, sc * sgn], axis=1).astype(np.float32)


def _bucket_onehot():
    import jax
    import jax.numpy as jnp
    with jax.default_device(jax.devices("cpu")[0]):
        rel = 255 - jnp.arange(512)
        half = N_BUCKETS // 2
        max_exact = half // 2
        ret = jnp.where(rel > 0, half, 0)
        n = jnp.abs(rel)
        nf = jnp.maximum(n, 1).astype(jnp.float32)
        large = max_exact + (jnp.log(nf / max_exact) / math.log(MAX_DISTANCE / max_exact) * (half - max_exact)).astype(jnp.int32)
        large = jnp.minimum(large, half - 1)
        bucket = np.asarray(ret + jnp.where(n < max_exact, n, large))
        rel = np.asarray(rel)
    oh = np.zeros((33, 512), np.float32)
    valid = np.abs(rel) <= 128
    for j in range(512):
        if valid[j]:
            oh[bucket[j], j] = 1.0
        else:
            oh[32, j] = -30000.0
    return oh


def _weight_blocks(inp):
    w_in = np.asarray(inp["w_in"][0], np.float32)
    w_gate = np.asarray(inp["w_gate"][0], np.float32)
    wa = np.asarray(inp["w_branch_a"][0], np.float32)
    wb = np.asarray(inp["w_branch_b"][0], np.float32)
    w_out = np.asarray(inp["w_out"][0], np.float32)
    wg = np.asarray(inp["w_ffn_gate"][0], np.float32)
    wu = np.asarray(inp["w_ffn_up"][0], np.float32)
    wd = np.asarray(inp["w_ffn_down"][0], np.float32)
    blk = np.zeros((NBLK, 128, 4096), np.float32)

    def kmaj(w):
        K, N = w.shape
        return w.reshape(K // 128, 128, N).transpose(1, 0, 2)
    qa_cols = np.concatenate([np.arange(h * 64, (h + 1) * 64) for h in PAIR_ORDER])
    kv_cols = np.concatenate([np.arange(512, 768), np.arange(1280, 1536)])
    blk[BLK_KV] = kmaj(w_in[:, kv_cols]).reshape(128, 4096)
    blk[BLK_QA] = kmaj(w_in[:, qa_cols]).reshape(128, 4096)
    blk[BLK_QB] = kmaj(w_in[:, 768 + qa_cols]).reshape(128, 4096)
    rows = qa_cols
    wa_p = kmaj(wa[rows, :])
    wb_p = kmaj(wb[rows, :])
    wgk = kmaj(w_gate)
    for f in range(8):
        b = blk[BLK_MG + f]
        b[:, 0:1024] = wgk[:, :, f * 128:(f + 1) * 128].reshape(128, 1024)
        b[:, 1024:2048] = wgk[:, :, 1024 + f * 128:1024 + (f + 1) * 128].reshape(128, 1024)
        b[:, 2048:2560] = wa_p[:, :, f * 128:(f + 1) * 128].reshape(128, 512)
        b[:, 2560:3072] = wb_p[:, :, f * 128:(f + 1) * 128].reshape(128, 512)
    wok = kmaj(w_out)
    for h in range(2):
        blk[BLK_WO + h] = wok[:, :, h * 512:(h + 1) * 512].reshape(128, 4096)
    wgk2 = kmaj(wg)
    wuk2 = kmaj(wu)
    for b_ in range(11):
        for jj in range(2):
            j = 2 * b_ + jj
            blk[BLK_GU + b_][:, jj * 2048:jj * 2048 + 1024] = wgk2[:, :, j * 128:(j + 1) * 128].reshape(128, 1024)
            blk[BLK_GU + b_][:, jj * 2048 + 1024:jj * 2048 + 2048] = wuk2[:, :, j * 128:(j + 1) * 128].reshape(128, 1024)
    wdk = kmaj(wd)
    for b_ in range(6):
        js = list(range(4 * b_, min(4 * b_ + 4, NJ)))
        blk[BLK_WD + b_][:, 0:len(js) * 1024] = wdk[:, js, :].reshape(128, len(js) * 1024)
    return blk


def _consts(inp):
    c128 = np.zeros((128, 288), np.float32)
    c128[:, 0:128] = np.eye(128, dtype=np.float32)
    c128[:, 128:256] = np.eye(128, dtype=np.float32)[::-1]
    c128[:, 256:264] = np.asarray(inp["norm_mix_pre"][0], np.float32).reshape(8, 128).T
    c128[:, 264:272] = np.asarray(inp["norm_ffn_pre"][0], np.float32).reshape(8, 128).T
    c128[:, 272:288] = np.asarray(inp["b_gate"][0], np.float32).reshape(16, 128).T
    crow = np.concatenate([np.asarray(inp["norm_mix_post"][0], np.float32), np.asarray(inp["norm_ffn_post"][0], np.float32),
                           np.asarray(inp["q_norm_a"][0], np.float32), np.asarray(inp["k_norm_a"][0], np.float32),
                           np.asarray(inp["sink_b"][0], np.float32)])[None, :]
    rbaug = np.concatenate([np.asarray(inp["rel_bias"], np.float32), np.ones((1, 8), np.float32)], axis=0)
    return c128, np.ascontiguousarray(crow), np.ascontiguousarray(rbaug)


def _core_inputs(inp, prompts, sample, units):
    xs, ropes = [], []
    for xp in prompts:
        xs.append(xp)
        ropes.append(_rope_table(xp.shape[0]))
    flags = np.zeros((1, 2), np.float32)
    if sample is not None:
        xf, half = sample
        S2 = xf.shape[0]
        H = S2 // 2
        rt = _rope_table(S2)
        own = slice(half * H, (half + 1) * H)
        oth = slice((1 - half) * H, (2 - half) * H)
        z = np.zeros((128, D), np.float32)
        prev = xf[half * H - 128:half * H] if half == 1 else z
        nxt = xf[(half + 1) * H:(half + 1) * H + 128] if half == 0 else z
        flags[0, 0] = 1.0 if half == 1 else 0.0
        flags[0, 1] = 1.0 if half == 0 else 0.0
        xs += [xf[own], xf[oth], prev, nxt]
        ropes += [rt[own], rt[oth], np.zeros((256, 128), np.float32)]
    xin = np.ascontiguousarray(np.concatenate(xs, axis=0), dtype=np.float32)
    rope = np.ascontiguousarray(np.concatenate(ropes, axis=0), dtype=np.float32)
    return xin, rope, flags


_CACHE = {}


def kernel(**inp):
    xp = np.asarray(inp["x_prompt"], np.float32)
    xsm = np.asarray(inp["x_sample"], np.float32)
    B, S, _ = xp.shape
    B2, S2, _ = xsm.shape
    ncores = 8
    ppc = B // ncores
    assert B2 * 2 == ncores
    units = [Unit(S // 128, 0, False) for _ in range(ppc)] + [Unit(S2 // 256, S2 // 256, True)]
    key = (S, S2, ppc)
    if key not in _CACHE:
        _CACHE[key] = build(units)
    nc = _CACHE[key]
    blk = _weight_blocks(inp)
    c128, crow, rbaug = _consts(inp)
    ohx = _bucket_onehot()
    in_maps = []
    for c in range(ncores):
        xin, rope, flags = _core_inputs(inp, [xp[c * ppc + i] for i in range(ppc)], (xsm[c // 2], c % 2), units)
        in_maps.append({"xin": xin, "rope": rope, "wsrc": blk, "c128": c128, "crow": crow, "flags": flags,
                        "ohx": ohx, "rbaug": rbaug})
    res = run_bass_kernel_spmd(nc, in_maps, core_ids=list(range(ncores)))
    yp = np.zeros_like(xp)
    ys = np.zeros_like(xsm)
    H = S2 // 2
    for c in range(ncores):
        y = np.asarray(res.results[c]["yout"], np.float32)
        for i in range(ppc):
            yp[c * ppc + i] = y[i * S:(i + 1) * S]
        ys[c // 2, (c % 2) * H:(c % 2 + 1) * H] = y[ppc * S:ppc * S + H]
    return (yp, ys)
```

```python
import math
from contextlib import ExitStack

import numpy as np
import concourse.bass as bass
import concourse.mybir as mybir
from concourse.bass_utils import run_bass_kernel_spmd

F32 = mybir.dt.float32
BF16 = mybir.dt.bfloat16
AF = mybir.ActivationFunctionType
ALU = mybir.AluOpType
AX = mybir.AxisListType

D = 1024
KC = 8
DFF = 2816
NJ = 22
CH = 4
EPS = 1e-6
NBLK = 30
BLK_KV, BLK_QA, BLK_QB, BLK_MG, BLK_WO, BLK_GU, BLK_WD = 0, 1, 2, 3, 11, 13, 24
NS = 4
SEM_CAP = 8000


class Buf:
    __slots__ = ("name", "w", "r", "excl")

    def __init__(self, name, excl=False):
        self.name = name
        self.w = None
        self.r = []
        self.excl = excl


class Op:
    __slots__ = ("eng", "fn", "deps", "dma", "needed", "ms", "sem", "val", "pre")

    def __init__(self, eng, fn, dma):
        self.eng = eng
        self.fn = fn
        self.dma = dma
        self.deps = []
        self.needed = False
        self.ms = 0
        self.sem = None
        self.val = 0
        self.pre = None


ENGS = ("pe", "act", "dve", "pool", "sp")


class Prog:
    def __init__(self):
        self.ops = {e: [] for e in ENGS}

    cut = False
    stop_at = None

    def mark(self, name):
        if self.stop_at is not None and name == self.stop_at:
            self.cut = True

    def add(self, eng, fn, reads=(), writes=(), dma=False):
        op = Op(eng, fn, dma)
        if self.cut:
            return op
        ex = [b for b in reads if b.excl and b not in writes]
        if ex:
            writes = list(writes) + ex
        raw = set()
        oth = set()
        for b in reads:
            if b.w is not None:
                raw.add(b.w)
        for b in writes:
            if b.w is not None:
                oth.add(b.w)
            for r in b.r:
                oth.add(r)
        for d in raw | oth:
            if d is op:
                continue
            if (not d.dma) and (not dma) and d.eng == eng:
                if eng == "pe":
                    continue
            op.deps.append(d)
            d.needed = True
        for b in reads:
            if not dma:
                b.r = [r for r in b.r if r.dma or r.eng != eng]
            b.r.append(op)
        for b in writes:
            b.w = op
            b.r = []
        self.ops[eng].append(op)
        return op

    def prepare(self, nc, es, dma_ring_sizes):
        csems = {}
        self.csems = csems
        for e in ENGS:
            n = sum(1 for o in self.ops[e] if o.needed and not o.dma)
            ne = max(1, (n + SEM_CAP - 1) // SEM_CAP)
            csems[e] = [es.enter_context(nc.semaphore(f"c_{e}_{i}")) for i in range(ne)]
            cnt = 0
            for o in self.ops[e]:
                if o.needed and not o.dma:
                    cnt += 1
                    o.ms = cnt
        for e in ENGS:
            R = dma_ring_sizes.get(e, 0)
            if R == 0:
                assert not any(o.dma for o in self.ops[e])
                continue
            ring = [es.enter_context(nc.semaphore(f"d_{e}_{i}")) for i in range(R)]
            i = 0
            for o in self.ops[e]:
                if o.dma:
                    o.sem = ring[i % R]
                    o.val = 16 * (i // R + 1)
                    o.pre = (ring[i % R], 16 * (i // R)) if i >= R else None
                    i += 1

    def emit(self, block):
        csems = self.csems

        def run_engine(eng_name, eobj):
            seen_c = {e: 0 for e in ENGS}
            seen_d = {}
            for o in self.ops[eng_name]:
                for d in o.deps:
                    if d.dma:
                        k = id(d.sem)
                        if seen_d.get(k, 0) >= d.val:
                            continue
                        eobj.wait_ge(d.sem, d.val)
                        seen_d[k] = d.val
                    else:
                        if seen_c[d.eng] >= d.ms:
                            continue
                        ep = (d.ms - 1) // SEM_CAP
                        eobj.wait_ge(csems[d.eng][ep], (d.ms - 1) % SEM_CAP + 1)
                        seen_c[d.eng] = d.ms
                if o.dma and o.pre is not None:
                    k = id(o.pre[0])
                    if seen_d.get(k, 0) < o.pre[1]:
                        eobj.wait_ge(o.pre[0], o.pre[1])
                        seen_d[k] = o.pre[1]
                ins = o.fn(eobj)
                if o.dma:
                    ins.then_inc(o.sem, 16)
                elif o.needed:
                    ep = (o.ms - 1) // SEM_CAP
                    ins.then_inc(csems[eng_name][ep], 1)
            last = {}
            for o in self.ops[eng_name]:
                if o.dma:
                    last[id(o.sem)] = (o.sem, o.val)
            for sem, val in last.values():
                if seen_d.get(id(sem), 0) < val:
                    eobj.wait_ge(sem, val)

        @block.tensor
        def _(t):
            run_engine("pe", t)

        @block.scalar
        def _(s):
            run_engine("act", s)

        @block.vector
        def _(v):
            run_engine("dve", v)

        @block.gpsimd
        def _(g):
            run_engine("pool", g)

        @block.sync
        def _(sy):
            run_engine("sp", sy)


class Unit:
    def __init__(self, n_own, n_oth, halo):
        self.n_own, self.n_oth, self.halo = n_own, n_oth, halo
        self.n_rows = (n_own + n_oth + (2 if halo else 0)) * 128


def build(units, stop_at=None):
    nc = bass.Bass("TRN2", target_bir_lowering=False)
    NROW = sum(u.n_rows for u in units)
    NOUT = sum(u.n_own for u in units) * 128
    MAXKV = max(u.n_own + u.n_oth for u in units)
    MAXKB = max(u.n_own + 2 for u in units)

    xin = nc.dram_tensor("xin", [NROW, D], F32, kind="ExternalInput").ap()
    rope = nc.dram_tensor("rope", [NROW, 128], F32, kind="ExternalInput").ap()
    wsrc = nc.dram_tensor("wsrc", [NBLK, 128, 4096], F32, kind="ExternalInput").ap()
    c128 = nc.dram_tensor("c128", [128, 288], F32, kind="ExternalInput").ap()
    crow = nc.dram_tensor("crow", [1, 2184], F32, kind="ExternalInput").ap()
    flags = nc.dram_tensor("flags", [1, 2], F32, kind="ExternalInput").ap()
    ohx = nc.dram_tensor("ohx", [33, 512], F32, kind="ExternalInput").ap()
    rbaug = nc.dram_tensor("rbaug", [33, 8], F32, kind="ExternalInput").ap()
    yout = nc.dram_tensor("yout", [NOUT, D], F32, kind="ExternalOutput").ap()
    wsc = nc.dram_tensor("wsc", [NBLK, 128, 4096], BF16, kind="Internal").ap()
    wvd = nc.dram_tensor("wvd", [8, 512], F32, kind="Internal").ap()

    P = Prog()
    P.stop_at = stop_at
    es = ExitStack()
    with es:
        def sb(name, shape, dt):
            return es.enter_context(nc.sbuf_tensor(name, shape, dt))

        ring = [sb(f"ring{i}", [128, 4096], BF16) for i in range(NS)]
        kaT = sb("kaT", [128, MAXKV * 128], BF16)
        va = sb("va", [128, MAXKV, 192], BF16)
        kbT = sb("kbT", [128, MAXKB * 128], BF16)
        vb = sb("vb", [128, MAXKB, 192], BF16)
        xc = sb("xc", [128, 2 * CH, D], F32)
        xn = sb("xn", [128, 2, D], BF16)
        hT = sb("hT", [128, KC, 512], BF16)
        fs = [sb(f"fs{i}", [128, D], F32) for i in range(4)]
        qn16 = sb("qn16", [128, 2, 512], BF16)
        ktmp = sb("ktmp", [128, 2, 2, 256], BF16)
        qaT = sb("qaT", [128, 4, 512], BF16)
        qbT = sb("qbT", [128, 4, 512], BF16)
        pt = sb("pt", [128, 6, 1024], BF16)
        pb0 = sb("pb0", [128, 2, 1024], BF16)
        yaT = sb("yaT", [128, 4, 512], BF16)
        ybT = sb("ybT", [128, 4, 512], BF16)
        aT = sb("aT", [128, NJ, 512], BF16)
        mixT = aT[:, NJ - KC:NJ, :]
        Et = sb("Et", [128, 8, 384], BF16)
        hT1 = sb("hT1", [128, KC, 256], BF16)
        cpk = sb("cpk", [128, 288], F32)
        crb = sb("crb", [128, 2184], F32)
        flg = sb("flg", [128, 2], F32)
        ident = sb("ident", [128, 128], BF16)
        ropet = sb("ropet", [128, 4, 128], F32)
        stat = sb("stat", [128, 12, 64], F32)
        epsb = sb("epsb", [128, 1], F32)
        stA = sb("stA", [128, 2, 64], F32)
        stP = sb("stP", [128, 2, 64], F32)
        expsink = sb("expsink", [128, 8], F32)
        rba_s = sb("rba_s", [33, 8], F32)

        rc = fs[1][:].rearrange("p (j q) -> p j q", j=2)
        gab = pb0[:].rearrange("p k (a q) -> p k a q", a=2)
        sg = pt[:, 2:4, 0:512]
        junk = pt[:, 4:6, :]
        PSALL = es.enter_context(nc.psum_tensor("psall", [128, 4096], F32))
        PSALLb = PSALL[:].bitcast(BF16)
        PS = [PSALL[:, i * 1024:(i + 1) * 1024] for i in range(4)]
        PSb = [PSALLb[:, i * 2048:(i + 1) * 2048] for i in range(4)]

        B_ring = [Buf(f"ring{i}") for i in range(NS)]
        B_fs = [[Buf(f"fs{i}a"), Buf(f"fs{i}b")] for i in range(4)]
        B_pt = [Buf(f"pt{i}") for i in range(6)]
        B_pb0 = [Buf("pb0a"), Buf("pb0b")]
        B_wsc = [Buf(f"wsc{i}") for i in range(NBLK)]
        B_kaT = [Buf(f"kaT{i}") for i in range(MAXKV)]
        B_va = [Buf(f"va{i}") for i in range(MAXKV)]
        B_kbT = [Buf(f"kbT{i}") for i in range(MAXKB)]
        B_vb = [Buf(f"vb{i}") for i in range(MAXKB)]
        B_xc = [Buf(f"xc{i}") for i in range(2 * CH)]
        B_xn = [Buf(f"xn{i}") for i in range(2)]
        B_hT = [Buf(f"hT{i}") for i in range(CH)]
        B_qn = Buf("qn16a"), Buf("qn16b")
        B_kt = [[Buf("kt00"), Buf("kt01")], [Buf("kt10"), Buf("kt11")]]
        B_qT = [Buf(f"qT{i}") for i in range(CH)]
        B_pb = Buf("pb")
        B_yaT = [Buf(f"yaT{i}") for i in range(4)]
        B_ybT = [Buf(f"ybT{i}") for i in range(CH)]
        B_rc = B_fs[1]
        B_gab = B_pb0
        B_aT = [Buf(f"aT{i}") for i in range(NJ)]
        B_mixT = B_aT[NJ - KC:NJ]
        B_sg = B_pt[2:4]
        B_const = Buf("const")
        B_E = Buf("E")
        B_rope = [Buf(f"rope{i}") for i in range(4)]
        B_stat = [Buf(f"stat{i}") for i in range(12)]
        B_ps = [[Buf(f"ps{i}a", True), Buf(f"ps{i}b", True)] for i in range(4)]
        B_misc = [Buf(f"misc{i}") for i in range(8)]
        B_wvd = Buf("wvd")
        B_stA = [Buf("stA0"), Buf("stA1")]
        B_hT1 = [Buf("hT1a"), Buf("hT1b")]
        B_stP = [Buf("stP0"), Buf("stP1")]
        B_hk = [Buf("hk0"), Buf("hk1")]

        def psb(i, h):
            return PS[i][:, h * 512:(h + 1) * 512]

        def psb16(i, h):
            return PSb[i][:, h * 1024:(h + 1) * 1024]

        identf = cpk[:, 0:128]
        Jf = cpk[:, 128:256]
        g1T = cpk[:, 256:264]
        g3T = cpk[:, 264:272]
        bgT = cpk[:, 272:288]
        g2b = crb[:, 0:1024]
        g4b = crb[:, 1024:2048]
        gq64 = crb[:, 2048:2112]
        gk64 = crb[:, 2112:2176]
        sinkb = crb[:, 2176:2184]

        stat_i = [0]
        B_junk = B_pt[4:6]
        junk_i = [0]

        def new_junk():
            i = junk_i[0] % 2
            junk_i[0] += 1
            return junk[:, i, :], B_junk[i]

        def new_stat():
            i = stat_i[0] % 12
            stat_i[0] += 1
            return stat[:, i, :], B_stat[i]

        P.add("sp", lambda e: e.dma_start(out=cpk[:], in_=c128), writes=[B_const], dma=True)
        P.add("sp", lambda e: e.dma_start(out=crb[:], in_=crow.partition_broadcast(128)), writes=[B_misc[0]], dma=True)
        P.add("sp", lambda e: e.dma_start(out=flg[:], in_=flags.partition_broadcast(128)), writes=[B_misc[1]], dma=True)
        P.add("sp", lambda e: e.dma_start(out=rba_s[:], in_=rbaug), writes=[B_misc[3]], dma=True)
        P.add("dve", lambda e: e.memset(epsb[:], EPS), writes=[B_misc[4]])
        P.add("dve", lambda e: e.tensor_copy(out=ident[:], in_=identf), reads=[B_const], writes=[B_misc[5]])
        B_ident = B_misc[5]
        B_crb = B_misc[0]
        B_ones = Buf("ones")
        P.add("pool", lambda e: e.memset(va[:, :, 64:128], 1.0), writes=[B_ones])
        P.add("pool", lambda e: e.memset(vb[:, :, 64:128], 1.0), writes=[B_ones])

        P.mark("consts")
        seq = []
        ws_dry = [True]
        ws = {"issued": 0, "cons": 0}
        ws_done = set()

        def ws_prefetch():
            if ws_dry[0]:
                return
            while ws["issued"] < len(seq) and ws["issued"] < ws["cons"] + NS:
                i = ws["issued"]
                blk = seq[i]
                k = i % NS
                if blk not in ws_done:
                    ws_done.add(blk)
                    P.add("pool", lambda e, k=k, blk=blk: e.dma_start(out=ring[k][:], in_=wsrc[blk]), writes=[B_ring[k]], dma=True)
                    P.add("sp", lambda e, k=k, blk=blk: e.dma_start(out=wsc[blk], in_=ring[k][:]),
                          reads=[B_ring[k]], writes=[B_wsc[blk]], dma=True)
                else:
                    P.add("sp", lambda e, k=k, blk=blk: e.dma_start(out=ring[k][:], in_=wsc[blk]),
                          reads=[B_wsc[blk]], writes=[B_ring[k]], dma=True)
                ws["issued"] += 1

        def ws_acquire(blk, ahead=0):
            if ws_dry[0]:
                seq.append(blk)
                return 0
            i = ws["cons"] + ahead
            assert seq[i] == blk, (i, seq[i], blk)
            if ahead == 0:
                ws_prefetch()
            return i % NS

        def ws_release():
            if ws_dry[0]:
                return
            ws["cons"] += 1
            ws_prefetch()

        P.mark("conv")
        B_Eh = B_misc[7]
        B_sink = B_misc[4]
        P.add("act", lambda e: e.activation(out=expsink[:], in_=sinkb, func=AF.Exp), reads=[B_crb], writes=[B_misc[4]])
        def emit_etable():
            P.add("sp", lambda e: e.dma_start(out=fs[3][0:33, 0:512], in_=ohx), writes=[B_fs[3][0]], dma=True)
            P.add("pe", lambda e: e.matmul(PS[0][0:8, 0:512], rba_s[0:33, 0:8], fs[3][0:33, 0:512], start=True, stop=True),
                  reads=[B_fs[3][0], B_misc[3]], writes=[B_ps[0][0]])
            P.add("dve", lambda e: e.tensor_copy(out=fs[2][0:8, 0:512], in_=PS[0][0:8, 0:512]), reads=[B_ps[0][0]], writes=[B_fs[2][0]])
            P.add("sp", lambda e: e.dma_start(out=wvd, in_=fs[2][0:8, 0:512]), reads=[B_fs[2][0]], writes=[B_wvd], dma=True)
            for h in range(8):
                k = h % 2
                hap = bass.AP(tensor=wvd.tensor, offset=h * 512, ap=[[1, 128], [1, 384]])
                P.add("sp", lambda e, k=k, hap=hap: e.dma_start(out=fs[k][:, 0:384], in_=hap), reads=[B_wvd], writes=[B_fs[k][0]], dma=True)
                P.add("pe", lambda e, k=k: e.matmul(PS[1 + k][:, 0:384], Jf, fs[k][:, 0:384], start=True, stop=True),
                      reads=[B_const, B_fs[k][0]], writes=[B_ps[1 + k][0]])
                P.add("act", lambda e, k=k, h=h: e.activation(out=Et[:, h, :], in_=PS[1 + k][:, 0:384], func=AF.Copy, scale=8.0),
                      reads=[B_ps[1 + k][0]], writes=[B_E])

        P.mark("etable")
        def rstd_batch(ssq_ap, ssq_buf, n, inv_n):
            st, stb = new_stat()
            P.add("act", lambda e: e.activation(out=st[:, 0:n], in_=ssq_ap, func=AF.Sqrt, scale=inv_n, bias=epsb[:, 0:1]),
                  reads=[ssq_buf, B_sink], writes=[stb])
            P.add("dve", lambda e: e.reciprocal(out=st[:, 32:32 + n], in_=st[:, 0:n]),
                  reads=[stb], writes=[stb])
            return st[:, 32:32 + n], stb

        xn_i = [0]
        tp_i = [0]

        def norm_stats(srcs, st=None, stb=None):
            n = len(srcs)
            if st is None:
                st, stb = new_stat()
            for i, (sap, sbuf_) in enumerate(srcs):
                jk, jkb = new_junk()
                P.add("act", lambda e, sap=sap, i=i, jk=jk: e.activation(out=jk, in_=sap, func=AF.Square, accum_out=st[:, i:i + 1]),
                      reads=[sbuf_], writes=[stb, jkb])
            P.add("act", lambda e: e.activation(out=st[:, 16:16 + n], in_=st[:, 0:n], func=AF.Sqrt, scale=1.0 / D, bias=epsb[:, 0:1]),
                  reads=[stb, B_sink], writes=[stb])
            P.add("dve", lambda e: e.reciprocal(out=st[:, 32:32 + n], in_=st[:, 16:16 + n]),
                  reads=[stb], writes=[stb])
            return st[:, 32:32 + n], stb

        def norm_transpose(srcs, gT, hslots, tbanks=None, pre=None, hdst=None):
            n = len(srcs)
            rs, rsb = pre if pre is not None else norm_stats(srcs)
            info = []
            hten, hbufs = hdst if hdst is not None else (hT, B_hT)

            def do_xn(i):
                sap, sbuf_ = srcs[i]
                k = xn_i[0] % 2
                xn_i[0] += 1
                P.add("dve", lambda e: e.tensor_scalar(out=xn[:, k, :], in0=sap, scalar1=rs[:, i:i + 1], scalar2=None, op0=ALU.mult),
                      reads=[sbuf_, rsb], writes=[B_xn[k]])
                if tbanks is None:
                    pi, ph = tp_i[0] % 4, 0
                    tp_i[0] += 1
                else:
                    pi, ph = tbanks[i]
                info.append((k, pi, ph))

            def do_tr(i):
                k, pi, ph = info[i]
                pv = psb16(pi, ph)

                def tr(e):
                    ins = None
                    for kc in range(KC):
                        ins = e.transpose(pv[:, kc * 128:(kc + 1) * 128], xn[:, k, kc * 128:(kc + 1) * 128], ident[:])
                    return ins
                P.add("pe", tr, reads=[B_xn[k], B_ident], writes=[B_ps[pi][ph]])

            def do_ev(i):
                k, pi, ph = info[i]
                pv = psb16(pi, ph)
                hs = hslots[i]
                P.add("dve", lambda e: e.tensor_tensor(
                    out=hten[:, :, hs * 128:(hs + 1) * 128], in0=pv.rearrange("p (k t) -> p k t", k=KC),
                    in1=gT.unsqueeze(2).to_broadcast([128, KC, 128]), op=ALU.mult),
                    reads=[B_ps[pi][ph], B_const], writes=[hbufs[hs]])
            do_xn(0)
            if n > 1:
                do_xn(1)
            for i in range(n):
                do_tr(i)
                if i + 1 < n and i >= 1:
                    pass
                do_ev(i)
                if i + 2 < n:
                    do_xn(i + 2)

        def qk_post(ps_ap, ps_buf, H, g64, rope_ap, rope_buf, fsq, fsq_buf, t1_ap, t1_buf, ssq_ap, ssq_buf):
            W = H * 64
            sq = fsq[:, 0:W]
            xg = fsq[:, 512:512 + W]
            P.add("act", lambda e: e.activation(out=sq, in_=ps_ap, func=AF.Square), reads=[ps_buf], writes=[fsq_buf[0]])
            P.add("dve", lambda e: e.tensor_reduce(out=ssq_ap, in_=sq.rearrange("p (h d) -> p h d", d=64), axis=AX.X, op=ALU.add),
                  reads=[fsq_buf[0]], writes=[ssq_buf])
            P.add("dve", lambda e: e.tensor_tensor(out=xg.rearrange("p (h d) -> p h d", d=64), in0=ps_ap.rearrange("p (h d) -> p h d", d=64),
                                                   in1=g64.unsqueeze(1).to_broadcast([128, H, 64]), op=ALU.mult),
                  reads=[ps_buf, B_crb], writes=[fsq_buf[1]])
            cosb = rope_ap[:, 0:64].unsqueeze(1).to_broadcast([128, H, 64])
            sin4 = rope_ap[:, 64:128].rearrange("p (r h d) -> p r h d", r=2, h=2)
            xg5 = xg.rearrange("p (H r h d) -> p H r h d", r=2, h=2, d=16)
            t15 = t1_ap.rearrange("p (H r h d) -> p H r h d", r=2, h=2, d=16)
            P.add("pool", lambda e: e.tensor_tensor(out=t1_ap.rearrange("p (h d) -> p h d", d=64), in0=xg.rearrange("p (h d) -> p h d", d=64), in1=cosb, op=ALU.mult),
                  reads=[fsq_buf[1], rope_buf], writes=[t1_buf])
            P.add("pool", lambda e: e.tensor_tensor(out=sq.rearrange("p (H r h d) -> p H r h d", r=2, h=2, d=16)[:, :, :, 0, :], in0=xg5[:, :, :, 1, :],
                                                    in1=sin4[:, :, 0, :].unsqueeze(1).to_broadcast([128, H, 2, 16]), op=ALU.mult),
                  reads=[fsq_buf[1], rope_buf, fsq_buf[0], ssq_buf], writes=[fsq_buf[0]])
            P.add("pool", lambda e: e.tensor_tensor(out=sq.rearrange("p (H r h d) -> p H r h d", r=2, h=2, d=16)[:, :, :, 1, :], in0=xg5[:, :, :, 0, :],
                                                    in1=sin4[:, :, 1, :].unsqueeze(1).to_broadcast([128, H, 2, 16]), op=ALU.mult),
                  reads=[fsq_buf[1], rope_buf], writes=[fsq_buf[0]])
            P.add("pool", lambda e: e.tensor_tensor(out=t1_ap, in0=t1_ap, in1=sq, op=ALU.add),
                  reads=[t1_buf, fsq_buf[0]], writes=[t1_buf])

        def emit_main():
            row0 = 0
            out0 = 0
            gch = [0]
            def do_group(u, row0, tiles, g0):
                grp = tiles[g0:g0 + 2]
                p = (g0 // 2) % 2
                n = len(grp)
                W = n * 128
                kind = grp[0][0]
                idx0 = grp[0][1]
                assert all(k_ == kind for k_, _ in grp)
                hasA = kind != "halo"
                hasB = kind != "oth"
                S = {}

                def sa():
                    xo1 = CH * (gch[0] % 2)
                    S['xo1'] = xo1
                    srcs = []
                    for gi in range(n):
                        r = row0 + (g0 + gi) * 128
                        P.add("sp", lambda e, gi=gi, r=r, p=p: e.dma_start(out=xc[:, xo1 + 2 * p + gi, :], in_=xin[r:r + 128, :]), writes=[B_xc[xo1 + 2 * p + gi]], dma=True)
                        if hasA:
                            P.add("sp", lambda e, gi=gi, r=r, p=p: e.dma_start(out=ropet[:, 2 * p + gi, :], in_=rope[r:r + 128, :]), writes=[B_rope[2 * p + gi]], dma=True)
                        srcs.append((xc[:, xo1 + 2 * p + gi, :], B_xc[xo1 + 2 * p + gi]))
                    S['srcs'] = srcs
                    rs, rsb = norm_stats(srcs, st=stP[:, p, :], stb=B_stP[p])
                    for gi in range(n):
                        sap, sbuf_ = srcs[gi]
                        P.add("dve", lambda e, gi=gi, sap=sap: e.tensor_scalar(out=pt[:, gi, :], in0=sap, scalar1=rs[:, gi:gi + 1], scalar2=None, op0=ALU.mult),
                              reads=[sbuf_, rsb], writes=[B_pt[gi]])

                def sh2():
                    for gi in range(n):
                        pv = psb16(p, gi)

                        def tr(e, gi=gi, pv=pv):
                            ins = None
                            for kc in range(KC):
                                ins = e.transpose(pv[:, kc * 128:(kc + 1) * 128], pt[:, gi, kc * 128:(kc + 1) * 128], ident[:])
                            return ins
                        P.add("pe", tr, reads=[B_pt[gi], B_ident], writes=[B_ps[p][gi]])
                        P.add("dve", lambda e, gi=gi, pv=pv: e.tensor_tensor(
                            out=hT1[:, :, gi * 128:(gi + 1) * 128], in0=pv.rearrange("p (k t) -> p k t", k=KC),
                            in1=g1T.unsqueeze(2).to_broadcast([128, KC, 128]), op=ALU.mult),
                            reads=[B_ps[p][gi], B_const], writes=[B_hT1[gi]])

                def sb_():
                    kslot = ws_acquire(BLK_KV)
                    wkv = ring[kslot]
                    banks = [B_ps[2 + p][gi] for gi in range(n)]
                    for gi in range(n):
                        def kvproj(e, gi=gi):
                            ins = None
                            for kc in range(KC):
                                ins = e.matmul(psb(2 + p, gi), hT1[:, kc, gi * 128:(gi + 1) * 128], wkv[:, kc * 512:(kc + 1) * 512],
                                               start=(kc == 0), stop=(kc == KC - 1))
                            return ins
                        P.add("pe", kvproj, reads=[B_hT1[gi], B_ring[kslot]], writes=[banks[gi]])
                    ws_release()

                    pkv = PSALL[:, (2 + p) * 1024:(2 + p) * 1024 + n * 512].rearrange("p (t c) -> p t c", c=512)
                    W = n * 128
                    ropes = [B_rope[2 * p + gi] for gi in range(n)]
                    if hasA:
                        kst, kstb = new_stat()
                        sq = fs[1 + 2 * p][:, 0:W]
                        xg = fs[1 + 2 * p][:, 512:512 + W]
                        t1 = fs[2][:, p * 512:p * 512 + W]
                        P.add("act", lambda e: e.activation(out=sq.rearrange("p (t c) -> p t c", c=128), in_=pkv[:, :, 0:128], func=AF.Square),
                              reads=banks, writes=[B_fs[1 + 2 * p][0]])
                        P.add("dve", lambda e: e.tensor_reduce(out=kst[:, 0:2 * n], in_=sq.rearrange("p (h d) -> p h d", d=64), axis=AX.X, op=ALU.add),
                              reads=[B_fs[1 + 2 * p][0]], writes=[kstb])
                        P.add("dve", lambda e: e.tensor_tensor(out=xg.rearrange("p (t h d) -> p t h d", h=2, d=64),
                                                               in0=pkv[:, :, 0:128].rearrange("p t (h d) -> p t h d", d=64),
                                                               in1=gk64.unsqueeze(1).unsqueeze(1).to_broadcast([128, n, 2, 64]), op=ALU.mult),
                              reads=banks + [B_crb], writes=[B_fs[1 + 2 * p][1]])
                        P.add("pool", lambda e: e.tensor_tensor(out=t1.rearrange("p (t h d) -> p t h d", h=2, d=64),
                                                                in0=xg.rearrange("p (t h d) -> p t h d", h=2, d=64),
                                                                in1=ropet[:, 2 * p:2 * p + n, 0:64].unsqueeze(2).to_broadcast([128, n, 2, 64]), op=ALU.mult),
                              reads=[B_fs[1 + 2 * p][1]] + ropes, writes=[B_fs[2][p]])
                        sq6 = sq.rearrange("p (t h r f d) -> p t h r f d", h=2, r=2, f=2, d=16)
                        xg6 = xg.rearrange("p (t h r f d) -> p t h r f d", h=2, r=2, f=2, d=16)
                        sn5 = ropet[:, 2 * p:2 * p + n, 64:128].rearrange("p t (r f d) -> p t r f d", r=2, f=2)
                        for hd in range(2):
                            for hf in range(2):
                                P.add("pool", lambda e, hd=hd, hf=hf: e.tensor_tensor(out=sq6[:, :, hd, :, hf, :], in0=xg6[:, :, hd, :, 1 - hf, :],
                                                                                      in1=sn5[:, :, :, hf, :], op=ALU.mult),
                                      reads=[B_fs[1 + 2 * p][1], kstb] + ropes, writes=[B_fs[1 + 2 * p][0]])
                        P.add("pool", lambda e: e.tensor_tensor(out=t1, in0=t1, in1=sq, op=ALU.add),
                              reads=[B_fs[2][p], B_fs[1 + 2 * p][0]], writes=[B_fs[2][p]])
                        P.add("act", lambda e: e.activation(
                            out=va[:, idx0:idx0 + n, :].rearrange("p t (a d) -> p t a d", d=64)[:, :, 0:3:2, :],
                            in_=pkv[:, :, 128:256].rearrange("p t (a d) -> p t a d", d=64), func=AF.Copy),
                            reads=banks + [B_ones], writes=[B_va[idx0 + gi] for gi in range(n)])
                    if hasB:
                        P.add("act", lambda e: e.activation(out=ktmp[:, p, 1, 0:W].rearrange("p (t c) -> p t c", c=128), in_=pkv[:, :, 256:384], func=AF.Copy),
                              reads=banks, writes=[B_kt[p][1]])
                        P.add("dve", lambda e: e.tensor_copy(
                            out=vb[:, idx0:idx0 + n, :].rearrange("p t (a d) -> p t a d", d=64)[:, :, 0:3:2, :],
                            in_=pkv[:, :, 384:512].rearrange("p t (a d) -> p t a d", d=64)),
                            reads=banks + [B_ones], writes=[B_vb[idx0 + gi] for gi in range(n)])
                        if kind == "halo":
                            for gi in range(n):
                                P.add("dve", lambda e, gi=gi: e.memset(vb[:, idx0 + gi, 64:128], 1.0), reads=[B_ones], writes=[B_vb[idx0 + gi]])
                                P.add("dve", lambda e, gi=gi: e.tensor_scalar(out=vb[:, idx0 + gi, :], in0=vb[:, idx0 + gi, :], scalar1=flg[:, gi:gi + 1],
                                                                            scalar2=None, op0=ALU.mult),
                                      reads=[B_vb[idx0 + gi], B_misc[1]], writes=[B_vb[idx0 + gi]])
                    if hasA:
                        krs, krsb = rstd_batch(kst[:, 0:2 * n], kstb, 2 * n, 1.0 / 64)
                        P.add("pool", lambda e: e.tensor_tensor(out=ktmp[:, p, 0, 0:W].rearrange("p (h d) -> p h d", d=64), in0=t1.rearrange("p (h d) -> p h d", d=64),
                                                               in1=krs.unsqueeze(2).to_broadcast([128, 2 * n, 64]), op=ALU.mult),
                              reads=[B_fs[2][p], krsb], writes=[B_kt[p][0]])

                def sc():
                    pbk = 2 + p
                    pv = psb16(pbk, 0)

                    def trk(e):
                        ins = None
                        for gi in range(n):
                            if hasA:
                                ins = e.transpose(pv[:, gi * 128:(gi + 1) * 128], ktmp[:, p, 0, gi * 128:(gi + 1) * 128], ident[:])
                            if hasB:
                                ins = e.transpose(pv[:, 512 + gi * 128:512 + (gi + 1) * 128], ktmp[:, p, 1, gi * 128:(gi + 1) * 128], ident[:])
                        return ins
                    P.add("pe", trk, reads=[B_kt[p][0], B_kt[p][1], B_ident], writes=[B_ps[pbk][0]])
                    if hasA:
                        P.add("dve", lambda e: e.tensor_copy(out=kaT[:, idx0 * 128:(idx0 + n) * 128], in_=pv[:, 0:W]),
                              reads=[B_ps[pbk][0]], writes=[B_kaT[idx0 + gi] for gi in range(n)])
                    if hasB:
                        P.add("act", lambda e: e.activation(out=kbT[:, idx0 * 128:(idx0 + n) * 128], in_=pv[:, 512:512 + W], func=AF.Copy),
                              reads=[B_ps[pbk][0]], writes=[B_kbT[idx0 + gi] for gi in range(n)])
                return sa, sh2, sb_, sc

            def make_pass1(u, row0):
                tiles = [("own", i) for i in range(u.n_own)] + [("oth", u.n_own + i) for i in range(u.n_oth)]
                if u.halo:
                    tiles += [("halo", u.n_own), ("halo", u.n_own + 1)]
                groups = [do_group(u, row0, tiles, g0) for g0 in range(0, len(tiles), 2)]
                ng = len(groups)

                def mk(k):
                    def step():
                        if 0 <= k - 3 < ng:
                            groups[k - 3][3]()
                        if 0 <= k - 2 < ng:
                            groups[k - 2][2]()
                        if 0 <= k - 1 < ng:
                            groups[k - 1][1]()
                        if 0 <= k < ng:
                            groups[k][0]()
                    return step
                steps = [mk(k) for k in range(ng + 3)]
                return steps

            rows = []
            r_ = 0
            for u in units:
                rows.append(r_)
                r_ += u.n_rows
            for s_ in make_pass1(units[0], rows[0]):
                s_()
            emit_etable()
            for ui, u in enumerate(units):
                nkv = u.n_own + u.n_oth
                next_p1 = make_pass1(units[ui + 1], rows[ui + 1]) if ui + 1 < len(units) else []

                def p1hook(next_p1=next_p1):
                    if next_p1:
                        next_p1.pop(0)()
                def stageA(c, par, u=u, nkv=nkv, row0=row0, out0=out0):
                    T0 = c * CH
                    xo = CH * par
                    S = {}

                    def a0():
                        srcs = []
                        for t in range(CH):
                            r = row0 + (T0 + t) * 128
                            P.add("sp", lambda e, t=t, r=r: e.dma_start(out=xc[:, xo + t, :], in_=xin[r:r + 128, :]), writes=[B_xc[xo + t]], dma=True)
                            P.add("sp", lambda e, t=t, r=r: e.dma_start(out=ropet[:, t, :], in_=rope[r:r + 128, :]), writes=[B_rope[t]], dma=True)
                            srcs.append((xc[:, xo + t, :], B_xc[xo + t]))
                        S['srcs'] = srcs
                        S['pre'] = norm_stats(srcs, st=stA[:, par, :], stb=B_stA[par])

                    def a1():
                        norm_transpose(S['srcs'], g1T, list(range(CH)), pre=S['pre'])
                        sa = ws_acquire(BLK_QA)
                        sbq = ws_acquire(BLK_QB, ahead=1)
                        qst, qstb = new_stat()
                        S['qst'], S['qstb'] = qst, qstb
                        for t in range(CH):
                            def qproj(e, t=t):
                                ins = None
                                for kc in range(KC):
                                    e.matmul(psb(2, t % 2), hT[:, kc, t * 128:(t + 1) * 128], ring[sa][:, kc * 512:(kc + 1) * 512],
                                             start=(kc == 0), stop=(kc == KC - 1))
                                for kc in range(KC):
                                    ins = e.matmul(psb(3, t % 2), hT[:, kc, t * 128:(t + 1) * 128], ring[sbq][:, kc * 512:(kc + 1) * 512],
                                                   start=(kc == 0), stop=(kc == KC - 1))
                                return ins
                            P.add("pe", qproj, reads=[B_hT[t], B_ring[sa], B_ring[sbq]], writes=[B_ps[2][t % 2], B_ps[3][t % 2]])
                            qk_post(psb(2, t % 2), B_ps[2][t % 2], 8, gq64, ropet[:, t, :], B_rope[t], fs[t % 2], B_fs[t % 2],
                                    fs[2 + t // 2][:, (t % 2) * 512:(t % 2) * 512 + 512], B_fs[2 + t // 2][t % 2],
                                    qst[:, t * 8:t * 8 + 8], qstb)
                            P.add("act", lambda e, t=t: e.activation(out=qn16[:, 1, :], in_=psb(3, t % 2), func=AF.Copy),
                                  reads=[B_ps[3][t % 2]], writes=[B_qn[1]])
                            pvb = psb16(t % 2, 1)

                            def trb(e, pvb=pvb):
                                ins = None
                                for j in range(4):
                                    ins = e.transpose(pvb[:, j * 128:(j + 1) * 128], qn16[:, 1, j * 128:(j + 1) * 128], ident[:])
                                return ins
                            P.add("pe", trb, reads=[B_qn[1], B_ident], writes=[B_ps[t % 2][1]])
                            P.add("dve", lambda e, t=t, pvb=pvb: e.tensor_copy(out=qbT[:, :, t * 128:(t + 1) * 128],
                                                                              in_=pvb[:, 0:512].rearrange("p (j q) -> p j q", j=4)),
                                  reads=[B_ps[t % 2][1]], writes=[B_qT[t]])
                        ws_release()
                        ws_release()

                    def part2a():
                        qst, qstb = S['qst'], S['qstb']
                        qrs, qrsb = rstd_batch(qst[:, 0:32], qstb, 32, 1.0 / 64)
                        for t in range(CH):
                            t1 = fs[2 + t // 2][:, (t % 2) * 512:(t % 2) * 512 + 512]
                            P.add("dve", lambda e, t=t, t1=t1: e.tensor_tensor(
                                out=yaT[:, t, :].rearrange("p (h d) -> p h d", d=64), in0=t1.rearrange("p (h d) -> p h d", d=64),
                                in1=qrs[:, t * 8:t * 8 + 8].unsqueeze(2).to_broadcast([128, 8, 64]), op=ALU.mult),
                                reads=[B_fs[2 + t // 2][t % 2], qrsb], writes=[B_yaT[t]])

                    def part2b():
                        for t in range(CH):
                            pva = psb16(0, t % 2)

                            def tra(e, pva=pva, t=t):
                                ins = None
                                for j in range(4):
                                    ins = e.transpose(pva[:, j * 128:(j + 1) * 128], yaT[:, t, j * 128:(j + 1) * 128], ident[:])
                                return ins
                            P.add("pe", tra, reads=[B_yaT[t], B_ident], writes=[B_ps[0][t % 2]])
                            P.add("act", lambda e, t=t, pva=pva: e.activation(out=qaT[:, :, t * 128:(t + 1) * 128],
                                                                              in_=pva[:, 0:512].rearrange("p (j q) -> p j q", j=4), func=AF.Copy),
                                  reads=[B_ps[0][t % 2], B_qT[t]], writes=[B_qT[t]])
                    return a0, a1, part2a, part2b

                def chunk_rest(c, par, nextA, p1h, u=u, nkv=nkv, row0=row0, out0=out0):
                    T0 = c * CH
                    xo = CH * par
                    P.mark("stageA")
                    steps = [(jp, i) for jp in range(4) for i in range(nkv)]

                    def emit_qk(g):
                        jp, i = steps[g]
                        s = g % 2

                        def f(e, jp=jp, i=i, s=s):
                            e.matmul(psb(s, 0), kaT[0:64, i * 128:(i + 1) * 128], qaT[0:64, jp, :], start=True, stop=True)
                            return e.matmul(psb(s, 1), kaT[64:128, i * 128:(i + 1) * 128], qaT[64:128, jp, :], start=True, stop=True)
                        P.add("pe", f, reads=[B_kaT[i]] + B_qT, writes=[B_ps[s][0], B_ps[s][1]])

                    def emit_exp_pv(g):
                        jp, i = steps[g]
                        s = g % 2
                        k = g % 3
                        a = 2 + (jp % 2)
                        P.add("act", lambda e, s=s, k=k: e.activation(out=pt[:, k, :], in_=PS[s], func=AF.Exp, scale=0.125),
                              reads=[B_ps[s][0], B_ps[s][1]], writes=[B_pt[k]])

                        def f(e, i=i, k=k, a=a):
                            e.matmul(psb(a, 0), va[:, i, 0:128], pt[:, k, 0:512], start=(i == 0), stop=(i == nkv - 1))
                            return e.matmul(psb(a, 1), va[:, i, 64:192], pt[:, k, 512:1024], start=(i == 0), stop=(i == nkv - 1))
                        P.add("pe", f, reads=[B_va[i], B_pt[k], B_ones], writes=[B_ps[a][0], B_ps[a][1]])
                        if i == nkv - 1:
                            P.add("dve", lambda e, a=a: e.reciprocal(out=rc[64:128, 0, :], in_=PS[a][64:128, 0:512]),
                                  reads=[B_ps[a][0]], writes=[B_rc[0]])
                            P.add("dve", lambda e, a=a, jp=jp: e.tensor_tensor(out=yaT[0:64, jp, :], in0=PS[a][0:64, 0:512], in1=rc[64:128, 0, :], op=ALU.mult),
                                  reads=[B_ps[a][0], B_rc[0]], writes=[B_yaT[jp]])
                            P.add("dve", lambda e, a=a: e.reciprocal(out=rc[0:64, 1, :], in_=PS[a][0:64, 512:1024]),
                                  reads=[B_ps[a][1]], writes=[B_rc[1]])
                            P.add("dve", lambda e, a=a, jp=jp: e.tensor_tensor(out=yaT[64:128, jp, :], in0=PS[a][64:128, 512:1024], in1=rc[0:64, 1, :], op=ALU.mult),
                                  reads=[B_ps[a][1], B_rc[1], B_yaT[jp]], writes=[B_yaT[jp]])
                    emit_qk(0)
                    for g in range(len(steps)):
                        if g + 1 < len(steps):
                            emit_qk(g + 1)
                        emit_exp_pv(g)

                    P.mark("stageB")
                    qblks = []
                    for t in range(CH):
                        T = T0 + t
                        blks = []
                        for o in range(3):
                            kb = T + o - 1
                            if 0 <= kb < u.n_own:
                                blks.append((kb, ("E", 2 - o)))
                            elif u.halo:
                                blks.append((u.n_own + (0 if o == 0 else 1), ("H", 0 if o == 0 else 1)))
                        qblks.append(blks)
                    bcnt = [0]

                    def emit_bq(t):
                        for bi, (kb, esel) in enumerate(qblks[t]):
                            s = bcnt[0] % 2
                            bcnt[0] += 1
                            k = (t % 2) * 3 + bi

                            if esel[0] == "E":
                                eb_ = esel[1]
                            else:
                                eb_ = 2 if esel[1] == 0 else 0
                            bia0 = Et[:, 0:4, eb_ * 128:(eb_ + 1) * 128]
                            bia1 = Et[:, 4:8, eb_ * 128:(eb_ + 1) * 128]

                            def f(e, t=t, kb=kb, s=s, bia0=bia0, bia1=bia1):
                                e.matmul(psb(s, 0), kbT[0:64, kb * 128:(kb + 1) * 128], qbT[0:64, :, t * 128:(t + 1) * 128], start=True, stop=False)
                                e.matmul(psb(s, 1), kbT[64:128, kb * 128:(kb + 1) * 128], qbT[64:128, :, t * 128:(t + 1) * 128], start=True, stop=False)
                                e.matmul(psb(s, 0), ident[:], bia0, start=False, stop=True)
                                return e.matmul(psb(s, 1), ident[:], bia1, start=False, stop=True)
                            P.add("pe", f, reads=[B_kbT[kb], B_qT[t], B_E, B_ident], writes=[B_ps[s][0], B_ps[s][1]])
                            P.add("act", lambda e, s=s, k=k: e.activation(out=pt[:, k, :], in_=PS[s], func=AF.Exp, scale=0.125),
                                  reads=[B_ps[s][0], B_ps[s][1]], writes=[B_pt[k]])

                    def emit_bpv(t):
                        blks = qblks[t]
                        nb = len(blks)
                        k0 = (t % 2) * 3

                        def f(e, blks=blks, nb=nb, k0=k0):
                            ins = None
                            for h in range(8):
                                for bi, (kb, esel) in enumerate(blks):
                                    if h < 4:
                                        o_ = PS[2][:, h * 65:(h + 1) * 65]
                                        r_ = vb[:, kb, 0:65]
                                    else:
                                        o_ = PS[2][:, 512 + (h - 4) * 65:512 + (h - 3) * 65]
                                        r_ = vb[:, kb, 127:192]
                                    ins = e.matmul(o_, pt[:, k0 + bi, h * 128:(h + 1) * 128], r_, start=(bi == 0), stop=(bi == nb - 1))
                            return ins
                        P.add("pe", f, reads=[B_vb[kb] for kb, _ in blks] + [B_pt[k0 + bi] for bi in range(nb)] + [B_ones],
                              writes=[B_ps[2][0], B_ps[2][1]])
                        st, stb = new_stat()
                        a0 = PS[2][:, 0:260].rearrange("p (h c) -> p h c", c=65)
                        a1 = PS[2][:, 512:772].rearrange("p (h c) -> p h c", c=65)
                        P.add("dve", lambda e: e.tensor_tensor(out=st[:, 0:4].unsqueeze(2), in0=a0[:, :, 64:65],
                                                               in1=expsink[:, 0:4].unsqueeze(2), op=ALU.add),
                              reads=[B_ps[2][0], B_sink], writes=[stb])
                        P.add("dve", lambda e: e.tensor_tensor(out=st[:, 4:8].unsqueeze(2), in0=a1[:, :, 0:1],
                                                               in1=expsink[:, 4:8].unsqueeze(2), op=ALU.add),
                              reads=[B_ps[2][1], B_sink, stb], writes=[stb])
                        P.add("dve", lambda e: e.reciprocal(out=st[:, 8:16], in_=st[:, 0:8]), reads=[stb], writes=[stb])
                        yv = qn16[:, 0, :].rearrange("p (j k d) -> p j k d", j=4, k=2)
                        P.add("dve", lambda e: e.tensor_tensor(out=yv[:, :, 0, :], in0=a0[:, :, 0:64],
                                                               in1=st[:, 8:12].unsqueeze(2).to_broadcast([128, 4, 64]), op=ALU.mult),
                              reads=[B_ps[2][0], stb], writes=[B_qn[0]])
                        P.add("dve", lambda e: e.tensor_tensor(out=yv[:, :, 1, :], in0=a1[:, :, 1:65],
                                                               in1=st[:, 12:16].unsqueeze(2).to_broadcast([128, 4, 64]), op=ALU.mult),
                              reads=[B_ps[2][1], stb, B_qn[0]], writes=[B_qn[0]])
                        pvy = psb16(3, t % 2)

                        def try_(e):
                            ins = None
                            for j in range(4):
                                ins = e.transpose(pvy[:, j * 128:(j + 1) * 128], qn16[:, 0, j * 128:(j + 1) * 128], ident[:])
                            return ins
                        P.add("pe", try_, reads=[B_qn[0], B_ident], writes=[B_ps[3][t % 2]])
                        P.add("act", lambda e: e.activation(out=ybT[:, :, t * 128:(t + 1) * 128],
                                                            in_=pvy[:, 0:512].rearrange("p (j q) -> p j q", j=4), func=AF.Copy),
                              reads=[B_ps[3][t % 2]], writes=[B_ybT[t]])
                    emit_bq(0)
                    for t in range(CH):
                        if t + 1 < CH:
                            emit_bq(t + 1)
                        emit_bpv(t)

                    P.mark("stageB2")
                    nxt2 = nextA() if nextA is not None else None
                    if nxt2 is not None:
                        nxt2[0]()
                    for f_ in range(8):
                        sl = ws_acquire(BLK_MG + f_)
                        w = ring[sl]
                        pg, pz = (0, 1) if f_ % 2 == 0 else (2, 3)

                        def mg(e, w=w, pg=pg, pz=pz):
                            ins = None
                            for kc in range(KC):
                                e.matmul(psb(pg, 0), w[:, kc * 128:(kc + 1) * 128], hT[:, kc, :], start=(kc == 0), stop=(kc == KC - 1))
                            for kc in range(KC):
                                e.matmul(psb(pg, 1), w[:, 1024 + kc * 128:1024 + (kc + 1) * 128], hT[:, kc, :], start=(kc == 0), stop=(kc == KC - 1))
                            for pc in range(4):
                                e.matmul(psb(pz, 0), w[:, 2048 + pc * 128:2048 + (pc + 1) * 128], yaT[:, pc, :], start=(pc == 0), stop=(pc == 3))
                            for pc in range(4):
                                ins = e.matmul(psb(pz, 1), w[:, 2560 + pc * 128:2560 + (pc + 1) * 128], ybT[:, pc, :], start=(pc == 0), stop=(pc == 3))
                            return ins
                        P.add("pe", mg, reads=[B_ring[sl]] + B_hT + B_yaT + B_ybT,
                              writes=[B_ps[pg][0], B_ps[pg][1], B_ps[pz][0], B_ps[pz][1]])
                        ws_release()
                        k = f_ % 2
                        P.add("act", lambda e, pg=pg, k=k, f_=f_: e.activation(out=gab[:, k, 0, :], in_=psb(pg, 0), func=AF.Sigmoid, bias=bgT[:, f_:f_ + 1]),
                              reads=[B_ps[pg][0], B_const], writes=[B_gab[k]])
                        P.add("act", lambda e, pg=pg, k=k, f_=f_: e.activation(out=gab[:, k, 1, :], in_=psb(pg, 1), func=AF.Sigmoid, bias=bgT[:, 8 + f_:9 + f_]),
                              reads=[B_ps[pg][1], B_const], writes=[B_gab[k]])
                        P.add("dve", lambda e, pz=pz, k=k: e.tensor_tensor(out=fs[0][:, 0:512], in0=psb(pz, 0), in1=gab[:, k, 0, :], op=ALU.mult),
                              reads=[B_ps[pz][0], B_gab[k]], writes=[B_fs[0][0]])
                        P.add("dve", lambda e, pz=pz, k=k: e.tensor_tensor(out=fs[0][:, 512:1024], in0=psb(pz, 1), in1=gab[:, k, 1, :], op=ALU.mult),
                              reads=[B_ps[pz][1], B_gab[k]], writes=[B_fs[0][1]])
                        P.add("pool", lambda e, f_=f_: e.tensor_tensor(out=mixT[:, f_, :], in0=fs[0][:, 0:512], in1=fs[0][:, 512:1024], op=ALU.add),
                              reads=[B_fs[0][0], B_fs[0][1]], writes=[B_mixT[f_]])

                    P.mark("stageC1")
                    for h in range(2):
                        sl = ws_acquire(BLK_WO + h)
                        w = ring[sl]
                        for t in range(CH):
                            def wo(e, w=w, t=t, h=h):
                                ins = None
                                for kc in range(KC):
                                    ins = e.matmul(psb(t, h), mixT[:, kc, t * 128:(t + 1) * 128], w[:, kc * 512:(kc + 1) * 512],
                                                   start=(kc == 0), stop=(kc == KC - 1))
                                return ins
                            P.add("pe", wo, reads=[B_ring[sl]] + B_mixT, writes=[B_ps[t][h]])
                        ws_release()

                    def post_norm_residual(gb, dst_out):
                        st, stb = new_stat()
                        for t in range(CH):
                            jk, jkb = new_junk()
                            P.add("act", lambda e, t=t, jk=jk: e.activation(out=jk, in_=PS[t], func=AF.Square, accum_out=st[:, t:t + 1]),
                                  reads=[B_ps[t][0], B_ps[t][1]], writes=[stb, jkb])
                            P.add("dve", lambda e, t=t: e.tensor_tensor(out=fs[t][:], in0=PS[t], in1=gb, op=ALU.mult),
                                  reads=[B_ps[t][0], B_ps[t][1], B_crb], writes=[B_fs[t][0], B_fs[t][1]])
                        rs, rsb = rstd_batch(st[:, 0:CH], stb, CH, 1.0 / D)
                        for t in range(CH):
                            P.add("dve", lambda e, t=t: e.scalar_tensor_tensor(out=xc[:, xo + t, :], in0=fs[t][:], scalar=rs[:, t:t + 1],
                                                                                in1=xc[:, xo + t, :], op0=ALU.mult, op1=ALU.add),
                                  reads=[B_xc[xo + t], B_fs[t][0], B_fs[t][1], rsb], writes=[B_xc[xo + t]])
                            if dst_out is not None:
                                r = dst_out + t * 128
                                P.add("pool", lambda e, t=t, r=r: e.dma_start(out=yout[r:r + 128, :], in_=xc[:, xo + t, :]), reads=[B_xc[xo + t]], dma=True)
                    post_norm_residual(g2b, None)

                    P.mark("stageC2")
                    norm_transpose([(xc[:, xo + t, :], B_xc[xo + t]) for t in range(CH)], g3T, list(range(CH)))
                    for b in range(11):
                        sl = ws_acquire(BLK_GU + b)
                        w = ring[sl]
                        for jj in range(2):
                            j = 2 * b + jj
                            pi = j % 4

                            def gu(e, w=w, jj=jj, pi=pi):
                                ins = None
                                for kc in range(KC):
                                    e.matmul(psb(pi, 0), w[:, jj * 2048 + kc * 128:jj * 2048 + (kc + 1) * 128], hT[:, kc, :], start=(kc == 0), stop=(kc == KC - 1))
                                for kc in range(KC):
                                    ins = e.matmul(psb(pi, 1), w[:, jj * 2048 + 1024 + kc * 128:jj * 2048 + 1024 + (kc + 1) * 128], hT[:, kc, :],
                                                   start=(kc == 0), stop=(kc == KC - 1))
                                return ins
                            P.add("pe", gu, reads=[B_ring[sl]] + B_hT, writes=[B_ps[pi][0], B_ps[pi][1]])
                            k = j % 2
                            P.add("act", lambda e, pi=pi, k=k: e.activation(out=sg[:, k, :], in_=psb(pi, 0), func=AF.Silu),
                                  reads=[B_ps[pi][0]], writes=[B_sg[k]])
                            P.add("dve", lambda e, pi=pi, k=k, j=j: e.tensor_tensor(out=aT[:, j, :], in0=psb(pi, 1), in1=sg[:, k, :], op=ALU.mult),
                                  reads=[B_ps[pi][1], B_sg[k]], writes=[B_aT[j]])
                        ws_release()
                        if p1h is not None:
                            p1h()
                    if nxt2 is not None:
                        nxt2[1]()
                    for wbk in range(6):
                        sl = ws_acquire(BLK_WD + wbk)
                        w = ring[sl]
                        for jj in range(4):
                            j = 4 * wbk + jj
                            if j >= NJ:
                                break

                            def dn(e, w=w, jj=jj, j=j):
                                ins = None
                                for t in range(CH):
                                    for h in range(2):
                                        ins = e.matmul(psb(t, h), aT[:, j, t * 128:(t + 1) * 128], w[:, jj * 1024 + h * 512:jj * 1024 + (h + 1) * 512],
                                                       start=(j == 0), stop=(j == NJ - 1))
                                return ins
                            P.add("pe", dn, reads=[B_ring[sl], B_aT[j]], writes=[B_ps[t][h] for t in range(CH) for h in range(2)])
                        ws_release()
                    if nxt2 is not None:
                        nxt2[2]()
                    post_norm_residual(g4b, out0 + T0 * 128)
                    if nxt2 is not None:
                        nxt2[3]()
                nch = u.n_own // CH
                par0 = gch[0] % 2
                for f_ in stageA(0, par0):
                    f_()
                for c in range(nch):
                    par = gch[0] % 2
                    gch[0] += 1
                    nextA = (lambda c=c, par=par: stageA(c + 1, 1 - par)) if c + 1 < nch else None
                    chunk_rest(c, par, nextA, p1hook if c + 1 == nch else None)
                while next_p1:
                    next_p1.pop(0)()
                row0 += u.n_rows
                out0 += u.n_own * 128

        P.cut = True
        emit_main()
        P.cut = False
        ws_dry[0] = False
        emit_main()
        P.prepare(nc, es, {"sp": 16, "pool": 40})
        block = es.enter_context(nc.Block())
        P.emit(block)
    return nc


HEAD_DIM = 64
GRID_W = 64
ROPE_THETA = 10000.0
ROPE_HALF = 32
N_BUCKETS = 32
MAX_DISTANCE = 128
PAIR_ORDER = [0, 4, 1, 5, 2, 6, 3, 7]


def _rope_table(S):
    import jax
    import jax.numpy as jnp
    with jax.default_device(jax.devices("cpu")[0]):
        ROWS = S // GRID_W
        pos = jnp.arange(S)
        rows = jnp.repeat(jnp.arange(ROWS), GRID_W).astype(jnp.float32)
        cols = (pos % GRID_W).astype(jnp.float32)
        inv_freq = 1.0 / (ROPE_THETA ** (jnp.arange(0, ROPE_HALF, 2, dtype=jnp.float32) / ROPE_HALF))
        fr = rows[:, None] * inv_freq[None, :]
        fc = cols[:, None] * inv_freq[None, :]
        er = jnp.concatenate([fr, fr], axis=-1)
        ec = jnp.concatenate([fc, fc], axis=-1)
        cr, sr, cc, sc = (np.asarray(a, dtype=np.float32) for a in (jnp.cos(er), jnp.sin(er), jnp.cos(ec), jnp.sin(ec)))
    sgn = np.concatenate([-np.ones(16, np.float32), np.ones(16, np.float32)])
    return np.concatenate([cr, cc, sr * sgn, sc * sgn], axis=1).astype(np.float32)


def _bucket_onehot():
    import jax
    import jax.numpy as jnp
    with jax.default_device(jax.devices("cpu")[0]):
        rel = 255 - jnp.arange(512)
        half = N_BUCKETS // 2
        max_exact = half // 2
        ret = jnp.where(rel > 0, half, 0)
        n = jnp.abs(rel)
        nf = jnp.maximum(n, 1).astype(jnp.float32)
        large = max_exact + (jnp.log(nf / max_exact) / math.log(MAX_DISTANCE / max_exact) * (half - max_exact)).astype(jnp.int32)
        large = jnp.minimum(large, half - 1)
        bucket = np.asarray(ret + jnp.where(n < max_exact, n, large))
        rel = np.asarray(rel)
    oh = np.zeros((33, 512), np.float32)
    valid = np.abs(rel) <= 128
    for j in range(512):
        if valid[j]:
            oh[bucket[j], j] = 1.0
        else:
            oh[32, j] = -30000.0
    return oh


def _weight_blocks(inp):
    w_in = np.asarray(inp["w_in"][0], np.float32)
    w_gate = np.asarray(inp["w_gate"][0], np.float32)
    wa = np.asarray(inp["w_branch_a"][0], np.float32)
    wb = np.asarray(inp["w_branch_b"][0], np.float32)
    w_out = np.asarray(inp["w_out"][0], np.float32)
    wg = np.asarray(inp["w_ffn_gate"][0], np.float32)
    wu = np.asarray(inp["w_ffn_up"][0], np.float32)
    wd = np.asarray(inp["w_ffn_down"][0], np.float32)
    blk = np.zeros((NBLK, 128, 4096), np.float32)

    def kmaj(w):
        K, N = w.shape
        return w.reshape(K // 128, 128, N).transpose(1, 0, 2)
    qa_cols = np.concatenate([np.arange(h * 64, (h + 1) * 64) for h in PAIR_ORDER])
    kv_cols = np.concatenate([np.arange(512, 768), np.arange(1280, 1536)])
    blk[BLK_KV] = kmaj(w_in[:, kv_cols]).reshape(128, 4096)
    blk[BLK_QA] = kmaj(w_in[:, qa_cols]).reshape(128, 4096)
    blk[BLK_QB] = kmaj(w_in[:, 768 + qa_cols]).reshape(128, 4096)
    rows = qa_cols
    wa_p = kmaj(wa[rows, :])
    wb_p = kmaj(wb[rows, :])
    wgk = kmaj(w_gate)
    for f in range(8):
        b = blk[BLK_MG + f]
        b[:, 0:1024] = wgk[:, :, f * 128:(f + 1) * 128].reshape(128, 1024)
        b[:, 1024:2048] = wgk[:, :, 1024 + f * 128:1024 + (f + 1) * 128].reshape(128, 1024)
        b[:, 2048:2560] = wa_p[:, :, f * 128:(f + 1) * 128].reshape(128, 512)
        b[:, 2560:3072] = wb_p[:, :, f * 128:(f + 1) * 128].reshape(128, 512)
    wok = kmaj(w_out)
    for h in range(2):
        blk[BLK_WO + h] = wok[:, :, h * 512:(h + 1) * 512].reshape(128, 4096)
    wgk2 = kmaj(wg)
    wuk2 = kmaj(wu)
    for b_ in range(11):
        for jj in range(2):
            j = 2 * b_ + jj
            blk[BLK_GU + b_][:, jj * 2048:jj * 2048 + 1024] = wgk2[:, :, j * 128:(j + 1) * 128].reshape(128, 1024)
            blk[BLK_GU + b_][:, jj * 2048 + 1024:jj * 2048 + 2048] = wuk2[:, :, j * 128:(j + 1) * 128].reshape(128, 1024)
    wdk = kmaj(wd)
    for b_ in range(6):
        js = list(range(4 * b_, min(4 * b_ + 4, NJ)))
        blk[BLK_WD + b_][:, 0:len(js) * 1024] = wdk[:, js, :].reshape(128, len(js) * 1024)
    return blk


def _consts(inp):
    c128 = np.zeros((128, 288), np.float32)
    c128[:, 0:128] = np.eye(128, dtype=np.float32)
    c128[:, 128:256] = np.eye(128, dtype=np.float32)[::-1]
    c128[:, 256:264] = np.asarray(inp["norm_mix_pre"][0], np.float32).reshape(8, 128).T
    c128[:, 264:272] = np.asarray(inp["norm_ffn_pre"][0], np.float32).reshape(8, 128).T
    c128[:, 272:288] = np.asarray(inp["b_gate"][0], np.float32).reshape(16, 128).T
    crow = np.concatenate([np.asarray(inp["norm_mix_post"][0], np.float32), np.asarray(inp["norm_ffn_post"][0], np.float32),
                           np.asarray(inp["q_norm_a"][0], np.float32), np.asarray(inp["k_norm_a"][0], np.float32),
                           np.asarray(inp["sink_b"][0], np.float32)])[None, :]
    rbaug = np.concatenate([np.asarray(inp["rel_bias"], np.float32), np.ones((1, 8), np.float32)], axis=0)
    return c128, np.ascontiguousarray(crow), np.ascontiguousarray(rbaug)


def _core_inputs(inp, prompts, sample, units):
    xs, ropes = [], []
    for xp in prompts:
        xs.append(xp)
        ropes.append(_rope_table(xp.shape[0]))
    flags = np.zeros((1, 2), np.float32)
    if sample is not None:
        xf, half = sample
        S2 = xf.shape[0]
        H = S2 // 2
        rt = _rope_table(S2)
        own = slice(half * H, (half + 1) * H)
        oth = slice((1 - half) * H, (2 - half) * H)
        z = np.zeros((128, D), np.float32)
        prev = xf[half * H - 128:half * H] if half == 1 else z
        nxt = xf[(half + 1) * H:(half + 1) * H + 128] if half == 0 else z
        flags[0, 0] = 1.0 if half == 1 else 0.0
        flags[0, 1] = 1.0 if half == 0 else 0.0
        xs += [xf[own], xf[oth], prev, nxt]
        ropes += [rt[own], rt[oth], np.zeros((256, 128), np.float32)]
    xin = np.ascontiguousarray(np.concatenate(xs, axis=0), dtype=np.float32)
    rope = np.ascontiguousarray(np.concatenate(ropes, axis=0), dtype=np.float32)
    return xin, rope, flags


_CACHE = {}


def kernel(**inp):
    xp = np.asarray(inp["x_prompt"], np.float32)
    xsm = np.asarray(inp["x_sample"], np.float32)
    B, S, _ = xp.shape
    B2, S2, _ = xsm.shape
    ncores = 8
    ppc = B // ncores
    assert B2 * 2 == ncores
    units = [Unit(S // 128, 0, False) for _ in range(ppc)] + [Unit(S2 // 256, S2 // 256, True)]
    key = (S, S2, ppc)
    if key not in _CACHE:
        _CACHE[key] = build(units)
    nc = _CACHE[key]
    blk = _weight_blocks(inp)
    c128, crow, rbaug = _consts(inp)
    ohx = _bucket_onehot()
    in_maps = []
    for c in range(ncores):
        xin, rope, flags = _core_inputs(inp, [xp[c * ppc + i] for i in range(ppc)], (xsm[c // 2], c % 2), units)
        in_maps.append({"xin": xin, "rope": rope, "wsrc": blk, "c128": c128, "crow": crow, "flags": flags,
                        "ohx": ohx, "rbaug": rbaug})
    res = run_bass_kernel_spmd(nc, in_maps, core_ids=list(range(ncores)))
    yp = np.zeros_like(xp)
    ys = np.zeros_like(xsm)
    H = S2 // 2
    for c in range(ncores):
        y = np.asarray(res.results[c]["yout"], np.float32)
        for i in range(ppc):
            yp[c * ppc + i] = y[i * S:(i + 1) * S]
        ys[c // 2, (c % 2) * H:(c % 2 + 1) * H] = y[ppc * S:ppc * S + H]
    return (yp, ys)
```

```python
import math
from contextlib import ExitStack

import numpy as np
import concourse.bass as bass
import concourse.mybir as mybir
from concourse.bass_utils import run_bass_kernel_spmd

F32 = mybir.dt.float32
BF16 = mybir.dt.bfloat16
AF = mybir.ActivationFunctionType
ALU = mybir.AluOpType
AX = mybir.AxisListType

D = 1024
KC = 8
DFF = 2816
NJ = 22
CH = 4
EPS = 1e-6
NBLK = 30
BLK_KV, BLK_QA, BLK_QB, BLK_MG, BLK_WO, BLK_GU, BLK_WD = 0, 1, 2, 3, 11, 13, 24
NS = 4
SEM_CAP = 8000


class Buf:
    __slots__ = ("name", "w", "r", "excl")

    def __init__(self, name, excl=False):
        self.name = name
        self.w = None
        self.r = []
        self.excl = excl


class Op:
    __slots__ = ("eng", "fn", "deps", "dma", "needed", "ms", "sem", "val", "pre")

    def __init__(self, eng, fn, dma):
        self.eng = eng
        self.fn = fn
        self.dma = dma
        self.deps = []
        self.needed = False
        self.ms = 0
        self.sem = None
        self.val = 0
        self.pre = None


ENGS = ("pe", "act", "dve", "pool", "sp")


class Prog:
    def __init__(self):
        self.ops = {e: [] for e in ENGS}

    cut = False
    stop_at = None

    def mark(self, name):
        if self.stop_at is not None and name == self.stop_at:
            self.cut = True

    def add(self, eng, fn, reads=(), writes=(), dma=False):
        op = Op(eng, fn, dma)
        if self.cut:
            return op
        ex = [b for b in reads if b.excl and b not in writes]
        if ex:
            writes = list(writes) + ex
        raw = set()
        oth = set()
        for b in reads:
            if b.w is not None:
                raw.add(b.w)
        for b in writes:
            if b.w is not None:
                oth.add(b.w)
            for r in b.r:
                oth.add(r)
        for d in raw | oth:
            if d is op:
                continue
            if (not d.dma) and (not dma) and d.eng == eng:
                if eng == "pe":
                    continue
            op.deps.append(d)
            d.needed = True
        for b in reads:
            if not dma:
                b.r = [r for r in b.r if r.dma or r.eng != eng]
            b.r.append(op)
        for b in writes:
            b.w = op
            b.r = []
        self.ops[eng].append(op)
        return op

    def prepare(self, nc, es, dma_ring_sizes):
        csems = {}
        self.csems = csems
        for e in ENGS:
            n = sum(1 for o in self.ops[e] if o.needed and not o.dma)
            ne = max(1, (n + SEM_CAP - 1) // SEM_CAP)
            csems[e] = [es.enter_context(nc.semaphore(f"c_{e}_{i}")) for i in range(ne)]
            cnt = 0
            for o in self.ops[e]:
                if o.needed and not o.dma:
                    cnt += 1
                    o.ms = cnt
        for e in ENGS:
            R = dma_ring_sizes.get(e, 0)
            if R == 0:
                assert not any(o.dma for o in self.ops[e])
                continue
            ring = [es.enter_context(nc.semaphore(f"d_{e}_{i}")) for i in range(R)]
            i = 0
            for o in self.ops[e]:
                if o.dma:
                    o.sem = ring[i % R]
                    o.val = 16 * (i // R + 1)
                    o.pre = (ring[i % R], 16 * (i // R)) if i >= R else None
                    i += 1

    def emit(self, block):
        csems = self.csems

        def run_engine(eng_name, eobj):
            seen_c = {e: 0 for e in ENGS}
            seen_d = {}
            for o in self.ops[eng_name]:
                for d in o.deps:
                    if d.dma:
                        k = id(d.sem)
                        if seen_d.get(k, 0) >= d.val:
                            continue
                        eobj.wait_ge(d.sem, d.val)
                        seen_d[k] = d.val
                    else:
                        if seen_c[d.eng] >= d.ms:
                            continue
                        ep = (d.ms - 1) // SEM_CAP
                        eobj.wait_ge(csems[d.eng][ep], (d.ms - 1) % SEM_CAP + 1)
                        seen_c[d.eng] = d.ms
                if o.dma and o.pre is not None:
                    k = id(o.pre[0])
                    if seen_d.get(k, 0) < o.pre[1]:
                        eobj.wait_ge(o.pre[0], o.pre[1])
                        seen_d[k] = o.pre[1]
                ins = o.fn(eobj)
                if o.dma:
                    ins.then_inc(o.sem, 16)
                elif o.needed:
                    ep = (o.ms - 1) // SEM_CAP
                    ins.then_inc(csems[eng_name][ep], 1)
            last = {}
            for o in self.ops[eng_name]:
                if o.dma:
                    last[id(o.sem)] = (o.sem, o.val)
            for sem, val in last.values():
                if seen_d.get(id(sem), 0) < val:
                    eobj.wait_ge(sem, val)

        @block.tensor
        def _(t):
            run_engine("pe", t)

        @block.scalar
        def _(s):
            run_engine("act", s)

        @block.vector
        def _(v):
            run_engine("dve", v)

        @block.gpsimd
        def _(g):
            run_engine("pool", g)

        @block.sync
        def _(sy):
            run_engine("sp", sy)


class Unit:
    def __init__(self, n_own, n_oth, halo):
        self.n_own, self.n_oth, self.halo = n_own, n_oth, halo
        self.n_rows = (n_own + n_oth + (2 if halo else 0)) * 128


def build(units, stop_at=None):
    nc = bass.Bass("TRN2", target_bir_lowering=False)
    NROW = sum(u.n_rows for u in units)
    NOUT = sum(u.n_own for u in units) * 128
    MAXKV = max(u.n_own + u.n_oth for u in units)
    MAXKB = max(u.n_own + 2 for u in units)

    xin = nc.dram_tensor("xin", [NROW, D], F32, kind="ExternalInput").ap()
    rope = nc.dram_tensor("rope", [NROW, 128], F32, kind="ExternalInput").ap()
    wsrc = nc.dram_tensor("wsrc", [NBLK, 128, 4096], F32, kind="ExternalInput").ap()
    c128 = nc.dram_tensor("c128", [128, 288], F32, kind="ExternalInput").ap()
    crow = nc.dram_tensor("crow", [1, 2184], F32, kind="ExternalInput").ap()
    flags = nc.dram_tensor("flags", [1, 2], F32, kind="ExternalInput").ap()
    ohx = nc.dram_tensor("ohx", [33, 512], F32, kind="ExternalInput").ap()
    rbaug = nc.dram_tensor("rbaug", [33, 8], F32, kind="ExternalInput").ap()
    yout = nc.dram_tensor("yout", [NOUT, D], F32, kind="ExternalOutput").ap()
    wsc = nc.dram_tensor("wsc", [NBLK, 128, 4096], BF16, kind="Internal").ap()
    wvd = nc.dram_tensor("wvd", [8, 512], F32, kind="Internal").ap()

    P = Prog()
    P.stop_at = stop_at
    es = ExitStack()
    with es:
        def sb(name, shape, dt):
            return es.enter_context(nc.sbuf_tensor(name, shape, dt))

        ring = [sb(f"ring{i}", [128, 4096], BF16) for i in range(NS)]
        kaT = sb("kaT", [128, MAXKV * 128], BF16)
        va = sb("va", [128, MAXKV, 192], BF16)
        kbT = sb("kbT", [128, MAXKB * 128], BF16)
        vb = sb("vb", [128, MAXKB, 192], BF16)
        xc = sb("xc", [128, 2 * CH, D], F32)
        xn = sb("xn", [128, 2, D], BF16)
        hT = sb("hT", [128, KC, 512], BF16)
        fs = [sb(f"fs{i}", [128, D], F32) for i in range(4)]
        qn16 = sb("qn16", [128, 2, 512], BF16)
        ktmp = sb("ktmp", [128, 2, 2, 256], BF16)
        qaT = sb("qaT", [128, 4, 512], BF16)
        qbT = sb("qbT", [128, 4, 512], BF16)
        pt = sb("pt", [128, 6, 1024], BF16)
        pb0 = sb("pb0", [128, 2, 1024], BF16)
        yaT = sb("yaT", [128, 4, 512], BF16)
        ybT = sb("ybT", [128, 4, 512], BF16)
        aT = sb("aT", [128, NJ, 512], BF16)
        mixT = aT[:, NJ - KC:NJ, :]
        Et = sb("Et", [128, 8, 384], BF16)
        hT1 = sb("hT1", [128, KC, 256], BF16)
        cpk = sb("cpk", [128, 288], F32)
        crb = sb("crb", [128, 2184], F32)
        flg = sb("flg", [128, 2], F32)
        ident = sb("ident", [128, 128], BF16)
        ropet = sb("ropet", [128, 4, 128], F32)
        stat = sb("stat", [128, 12, 64], F32)
        epsb = sb("epsb", [128, 1], F32)
        stA = sb("stA", [128, 2, 64], F32)
        stP = sb("stP", [128, 2, 64], F32)
        expsink = sb("expsink", [128, 8], F32)
        rba_s = sb("rba_s", [33, 8], F32)

        rc = fs[1][:].rearrange("p (j q) -> p j q", j=2)
        gab = pb0[:].rearrange("p k (a q) -> p k a q", a=2)
        sg = pt[:, 2:4, 0:512]
        junk = pt[:, 4:6, :]
        PSALL = es.enter_context(nc.psum_tensor("psall", [128, 4096], F32))
        PSALLb = PSALL[:].bitcast(BF16)
        PS = [PSALL[:, i * 1024:(i + 1) * 1024] for i in range(4)]
        PSb = [PSALLb[:, i * 2048:(i + 1) * 2048] for i in range(4)]

        B_ring = [Buf(f"ring{i}") for i in range(NS)]
        B_fs = [[Buf(f"fs{i}a"), Buf(f"fs{i}b")] for i in range(4)]
        B_pt = [Buf(f"pt{i}") for i in range(6)]
        B_pb0 = [Buf("pb0a"), Buf("pb0b")]
        B_wsc = [Buf(f"wsc{i}") for i in range(NBLK)]
        B_kaT = [Buf(f"kaT{i}") for i in range(MAXKV)]
        B_va = [Buf(f"va{i}") for i in range(MAXKV)]
        B_kbT = [Buf(f"kbT{i}") for i in range(MAXKB)]
        B_vb = [Buf(f"vb{i}") for i in range(MAXKB)]
        B_xc = [Buf(f"xc{i}") for i in range(2 * CH)]
        B_xn = [Buf(f"xn{i}") for i in range(2)]
        B_hT = [Buf(f"hT{i}") for i in range(CH)]
        B_qn = Buf("qn16a"), Buf("qn16b")
        B_kt = [[Buf("kt00"), Buf("kt01")], [Buf("kt10"), Buf("kt11")]]
        B_qT = [Buf(f"qT{i}") for i in range(CH)]
        B_pb = Buf("pb")
        B_yaT = [Buf(f"yaT{i}") for i in range(4)]
        B_ybT = [Buf(f"ybT{i}") for i in range(CH)]
        B_rc = B_fs[1]
        B_gab = B_pb0
        B_aT = [Buf(f"aT{i}") for i in range(NJ)]
        B_mixT = B_aT[NJ - KC:NJ]
        B_sg = B_pt[2:4]
        B_const = Buf("const")
        B_E = Buf("E")
        B_rope = [Buf(f"rope{i}") for i in range(4)]
        B_stat = [Buf(f"stat{i}") for i in range(12)]
        B_ps = [[Buf(f"ps{i}a", True), Buf(f"ps{i}b", True)] for i in range(4)]
        B_misc = [Buf(f"misc{i}") for i in range(8)]
        B_wvd = Buf("wvd")
        B_stA = [Buf("stA0"), Buf("stA1")]
        B_hT1 = [Buf("hT1a"), Buf("hT1b")]
        B_stP = [Buf("stP0"), Buf("stP1")]
        B_hk = [Buf("hk0"), Buf("hk1")]

        def psb(i, h):
            return PS[i][:, h * 512:(h + 1) * 512]

        def psb16(i, h):
            return PSb[i][:, h * 1024:(h + 1) * 1024]

        identf = cpk[:, 0:128]
        Jf = cpk[:, 128:256]
        g1T = cpk[:, 256:264]
        g3T = cpk[:, 264:272]
        bgT = cpk[:, 272:288]
        g2b = crb[:, 0:1024]
        g4b = crb[:, 1024:2048]
        gq64 = crb[:, 2048:2112]
        gk64 = crb[:, 2112:2176]
        sinkb = crb[:, 2176:2184]

        stat_i = [0]
        B_junk = B_pt[4:6]
        junk_i = [0]

        def new_junk():
            i = junk_i[0] % 2
            junk_i[0] += 1
            return junk[:, i, :], B_junk[i]

        def new_stat():
            i = stat_i[0] % 12
            stat_i[0] += 1
            return stat[:, i, :], B_stat[i]

        P.add("sp", lambda e: e.dma_start(out=cpk[:], in_=c128), writes=[B_const], dma=True)
        P.add("sp", lambda e: e.dma_start(out=crb[:], in_=crow.partition_broadcast(128)), writes=[B_misc[0]], dma=True)
        P.add("sp", lambda e: e.dma_start(out=flg[:], in_=flags.partition_broadcast(128)), writes=[B_misc[1]], dma=True)
        P.add("sp", lambda e: e.dma_start(out=rba_s[:], in_=rbaug), writes=[B_misc[3]], dma=True)
        P.add("dve", lambda e: e.memset(epsb[:], EPS), writes=[B_misc[4]])
        P.add("dve", lambda e: e.tensor_copy(out=ident[:], in_=identf), reads=[B_const], writes=[B_misc[5]])
        B_ident = B_misc[5]
        B_crb = B_misc[0]
        B_ones = Buf("ones")
        P.add("pool", lambda e: e.memset(va[:, :, 64:128], 1.0), writes=[B_ones])
        P.add("pool", lambda e: e.memset(vb[:, :, 64:128], 1.0), writes=[B_ones])

        P.mark("consts")
        seq = []
        ws_dry = [True]
        ws = {"issued": 0, "cons": 0}
        ws_done = set()

        def ws_prefetch():
            if ws_dry[0]:
                return
            while ws["issued"] < len(seq) and ws["issued"] < ws["cons"] + NS:
                i = ws["issued"]
                blk = seq[i]
                k = i % NS
                if blk not in ws_done:
                    ws_done.add(blk)
                    P.add("pool", lambda e, k=k, blk=blk: e.dma_start(out=ring[k][:], in_=wsrc[blk]), writes=[B_ring[k]], dma=True)
                    P.add("sp", lambda e, k=k, blk=blk: e.dma_start(out=wsc[blk], in_=ring[k][:]),
                          reads=[B_ring[k]], writes=[B_wsc[blk]], dma=True)
                else:
                    P.add("sp", lambda e, k=k, blk=blk: e.dma_start(out=ring[k][:], in_=wsc[blk]),
                          reads=[B_wsc[blk]], writes=[B_ring[k]], dma=True)
                ws["issued"] += 1

        def ws_acquire(blk, ahead=0):
            if ws_dry[0]:
                seq.append(blk)
                return 0
            i = ws["cons"] + ahead
            assert seq[i] == blk, (i, seq[i], blk)
            if ahead == 0:
                ws_prefetch()
            return i % NS

        def ws_release():
            if ws_dry[0]:
                return
            ws["cons"] += 1
            ws_prefetch()

        P.mark("conv")
        B_Eh = B_misc[7]
        B_sink = B_misc[4]
        P.add("act", lambda e: e.activation(out=expsink[:], in_=sinkb, func=AF.Exp), reads=[B_crb], writes=[B_misc[4]])
        def emit_etable():
            P.add("sp", lambda e: e.dma_start(out=fs[3][0:33, 0:512], in_=ohx), writes=[B_fs[3][0]], dma=True)
            P.add("pe", lambda e: e.matmul(PS[0][0:8, 0:512], rba_s[0:33, 0:8], fs[3][0:33, 0:512], start=True, stop=True),
                  reads=[B_fs[3][0], B_misc[3]], writes=[B_ps[0][0]])
            P.add("dve", lambda e: e.tensor_copy(out=fs[2][0:8, 0:512], in_=PS[0][0:8, 0:512]), reads=[B_ps[0][0]], writes=[B_fs[2][0]])
            P.add("sp", lambda e: e.dma_start(out=wvd, in_=fs[2][0:8, 0:512]), reads=[B_fs[2][0]], writes=[B_wvd], dma=True)
            for h in range(8):
                k = h % 2
                hap = bass.AP(tensor=wvd.tensor, offset=h * 512, ap=[[1, 128], [1, 384]])
                P.add("sp", lambda e, k=k, hap=hap: e.dma_start(out=fs[k][:, 0:384], in_=hap), reads=[B_wvd], writes=[B_fs[k][0]], dma=True)
                P.add("pe", lambda e, k=k: e.matmul(PS[1 + k][:, 0:384], Jf, fs[k][:, 0:384], start=True, stop=True),
                      reads=[B_const, B_fs[k][0]], writes=[B_ps[1 + k][0]])
                P.add("act", lambda e, k=k, h=h: e.activation(out=Et[:, h, :], in_=PS[1 + k][:, 0:384], func=AF.Exp),
                      reads=[B_ps[1 + k][0]], writes=[B_E])

        P.mark("etable")
        def rstd_batch(ssq_ap, ssq_buf, n, inv_n):
            st, stb = new_stat()
            P.add("act", lambda e: e.activation(out=st[:, 0:n], in_=ssq_ap, func=AF.Sqrt, scale=inv_n, bias=epsb[:, 0:1]),
                  reads=[ssq_buf, B_sink], writes=[stb])
            P.add("dve", lambda e: e.reciprocal(out=st[:, 32:32 + n], in_=st[:, 0:n]),
                  reads=[stb], writes=[stb])
            return st[:, 32:32 + n], stb

        xn_i = [0]
        tp_i = [0]

        def norm_stats(srcs, st=None, stb=None):
            n = len(srcs)
            if st is None:
                st, stb = new_stat()
            for i, (sap, sbuf_) in enumerate(srcs):
                jk, jkb = new_junk()
                P.add("act", lambda e, sap=sap, i=i, jk=jk: e.activation(out=jk, in_=sap, func=AF.Square, accum_out=st[:, i:i + 1]),
                      reads=[sbuf_], writes=[stb, jkb])
            P.add("act", lambda e: e.activation(out=st[:, 16:16 + n], in_=st[:, 0:n], func=AF.Sqrt, scale=1.0 / D, bias=epsb[:, 0:1]),
                  reads=[stb, B_sink], writes=[stb])
            P.add("dve", lambda e: e.reciprocal(out=st[:, 32:32 + n], in_=st[:, 16:16 + n]),
                  reads=[stb], writes=[stb])
            return st[:, 32:32 + n], stb

        def norm_transpose(srcs, gT, hslots, tbanks=None, pre=None, hdst=None):
            n = len(srcs)
            rs, rsb = pre if pre is not None else norm_stats(srcs)
            info = []
            hten, hbufs = hdst if hdst is not None else (hT, B_hT)

            def do_xn(i):
                sap, sbuf_ = srcs[i]
                k = xn_i[0] % 2
                xn_i[0] += 1
                P.add("dve", lambda e: e.tensor_scalar(out=xn[:, k, :], in0=sap, scalar1=rs[:, i:i + 1], scalar2=None, op0=ALU.mult),
                      reads=[sbuf_, rsb], writes=[B_xn[k]])
                if tbanks is None:
                    pi, ph = tp_i[0] % 4, 0
                    tp_i[0] += 1
                else:
                    pi, ph = tbanks[i]
                info.append((k, pi, ph))

            def do_tr(i):
                k, pi, ph = info[i]
                pv = psb16(pi, ph)

                def tr(e):
                    ins = None
                    for kc in range(KC):
                        ins = e.transpose(pv[:, kc * 128:(kc + 1) * 128], xn[:, k, kc * 128:(kc + 1) * 128], ident[:])
                    return ins
                P.add("pe", tr, reads=[B_xn[k], B_ident], writes=[B_ps[pi][ph]])

            def do_ev(i):
                k, pi, ph = info[i]
                pv = psb16(pi, ph)
                hs = hslots[i]
                P.add("dve", lambda e: e.tensor_tensor(
                    out=hten[:, :, hs * 128:(hs + 1) * 128], in0=pv.rearrange("p (k t) -> p k t", k=KC),
                    in1=gT.unsqueeze(2).to_broadcast([128, KC, 128]), op=ALU.mult),
                    reads=[B_ps[pi][ph], B_const], writes=[hbufs[hs]])
            do_xn(0)
            if n > 1:
                do_xn(1)
            for i in range(n):
                do_tr(i)
                if i + 1 < n and i >= 1:
                    pass
                do_ev(i)
                if i + 2 < n:
                    do_xn(i + 2)

        def qk_post(ps_ap, ps_buf, H, g64, rope_ap, rope_buf, fsq, fsq_buf, t1_ap, t1_buf, ssq_ap, ssq_buf):
            W = H * 64
            sq = fsq[:, 0:W]
            xg = fsq[:, 512:512 + W]
            P.add("act", lambda e: e.activation(out=sq, in_=ps_ap, func=AF.Square), reads=[ps_buf], writes=[fsq_buf[0]])
            P.add("dve", lambda e: e.tensor_reduce(out=ssq_ap, in_=sq.rearrange("p (h d) -> p h d", d=64), axis=AX.X, op=ALU.add),
                  reads=[fsq_buf[0]], writes=[ssq_buf])
            P.add("dve", lambda e: e.tensor_tensor(out=xg.rearrange("p (h d) -> p h d", d=64), in0=ps_ap.rearrange("p (h d) -> p h d", d=64),
                                                   in1=g64.unsqueeze(1).to_broadcast([128, H, 64]), op=ALU.mult),
                  reads=[ps_buf, B_crb], writes=[fsq_buf[1]])
            cosb = rope_ap[:, 0:64].unsqueeze(1).to_broadcast([128, H, 64])
            sin4 = rope_ap[:, 64:128].rearrange("p (r h d) -> p r h d", r=2, h=2)
            xg5 = xg.rearrange("p (H r h d) -> p H r h d", r=2, h=2, d=16)
            t15 = t1_ap.rearrange("p (H r h d) -> p H r h d", r=2, h=2, d=16)
            P.add("pool", lambda e: e.tensor_tensor(out=t1_ap.rearrange("p (h d) -> p h d", d=64), in0=xg.rearrange("p (h d) -> p h d", d=64), in1=cosb, op=ALU.mult),
                  reads=[fsq_buf[1], rope_buf], writes=[t1_buf])
            P.add("pool", lambda e: e.tensor_tensor(out=sq.rearrange("p (H r h d) -> p H r h d", r=2, h=2, d=16)[:, :, :, 0, :], in0=xg5[:, :, :, 1, :],
                                                    in1=sin4[:, :, 0, :].unsqueeze(1).to_broadcast([128, H, 2, 16]), op=ALU.mult),
                  reads=[fsq_buf[1], rope_buf, fsq_buf[0], ssq_buf], writes=[fsq_buf[0]])
            P.add("pool", lambda e: e.tensor_tensor(out=sq.rearrange("p (H r h d) -> p H r h d", r=2, h=2, d=16)[:, :, :, 1, :], in0=xg5[:, :, :, 0, :],
                                                    in1=sin4[:, :, 1, :].unsqueeze(1).to_broadcast([128, H, 2, 16]), op=ALU.mult),
                  reads=[fsq_buf[1], rope_buf], writes=[fsq_buf[0]])
            P.add("pool", lambda e: e.tensor_tensor(out=t1_ap, in0=t1_ap, in1=sq, op=ALU.add),
                  reads=[t1_buf, fsq_buf[0]], writes=[t1_buf])

        def emit_main():
            row0 = 0
            out0 = 0
            gch = [0]
            def do_group(u, row0, tiles, g0):
                grp = tiles[g0:g0 + 2]
                p = (g0 // 2) % 2
                n = len(grp)
                W = n * 128
                kind = grp[0][0]
                idx0 = grp[0][1]
                assert all(k_ == kind for k_, _ in grp)
                hasA = kind != "halo"
                hasB = kind != "oth"
                S = {}

                def s0():
                    xo1 = CH * (gch[0] % 2)
                    S['xo1'] = xo1
                    srcs = []
                    for gi in range(n):
                        r = row0 + (g0 + gi) * 128
                        P.add("sp", lambda e, gi=gi, r=r: e.dma_start(out=xc[:, xo1 + 2 * p + gi, :], in_=xin[r:r + 128, :]), writes=[B_xc[xo1 + 2 * p + gi]], dma=True)
                        srcs.append((xc[:, xo1 + 2 * p + gi, :], B_xc[xo1 + 2 * p + gi]))
                    S['srcs'] = srcs

                def sa():
                    srcs = S['srcs']
                    rs, rsb = norm_stats(srcs, st=stP[:, p, :], stb=B_stP[p])
                    for gi in range(n):
                        sap, sbuf_ = srcs[gi]
                        P.add("dve", lambda e, gi=gi, sap=sap: e.tensor_scalar(out=pt[:, gi, :], in0=sap, scalar1=rs[:, gi:gi + 1], scalar2=None, op0=ALU.mult),
                              reads=[sbuf_, rsb], writes=[B_pt[gi]])

                def sh2():
                    if hasA:
                        for gi in range(n):
                            r = row0 + (g0 + gi) * 128
                            P.add("sp", lambda e, gi=gi, r=r: e.dma_start(out=ropet[:, 2 * p + gi, :], in_=rope[r:r + 128, :]), writes=[B_rope[2 * p + gi]], dma=True)
                    for gi in range(n):
                        pv = psb16(p, gi)

                        def tr(e, gi=gi, pv=pv):
                            ins = None
                            for kc in range(KC):
                                ins = e.transpose(pv[:, kc * 128:(kc + 1) * 128], pt[:, gi, kc * 128:(kc + 1) * 128], ident[:])
                            return ins
                        P.add("pe", tr, reads=[B_pt[gi], B_ident], writes=[B_ps[p][gi]])
                        P.add("dve", lambda e, gi=gi, pv=pv: e.tensor_tensor(
                            out=hT1[:, :, gi * 128:(gi + 1) * 128], in0=pv.rearrange("p (k t) -> p k t", k=KC),
                            in1=g1T.unsqueeze(2).to_broadcast([128, KC, 128]), op=ALU.mult),
                            reads=[B_ps[p][gi], B_const], writes=[B_hT1[gi]])

                def sb_():
                    kslot = ws_acquire(BLK_KV)
                    wkv = ring[kslot]
                    banks = [B_ps[2 + p][gi] for gi in range(n)]
                    for gi in range(n):
                        def kvproj(e, gi=gi):
                            ins = None
                            for kc in range(KC):
                                ins = e.matmul(psb(2 + p, gi), hT1[:, kc, gi * 128:(gi + 1) * 128], wkv[:, kc * 512:(kc + 1) * 512],
                                               start=(kc == 0), stop=(kc == KC - 1))
                            return ins
                        P.add("pe", kvproj, reads=[B_hT1[gi], B_ring[kslot]], writes=[banks[gi]])
                    ws_release()

                    pkv = PSALL[:, (2 + p) * 1024:(2 + p) * 1024 + n * 512].rearrange("p (t c) -> p t c", c=512)
                    W = n * 128
                    ropes = [B_rope[2 * p + gi] for gi in range(n)]
                    if hasA:
                        kst, kstb = new_stat()
                        sq = fs[1 + 2 * p][:, 0:W]
                        xg = fs[1 + 2 * p][:, 512:512 + W]
                        t1 = fs[2][:, p * 512:p * 512 + W]
                        P.add("act", lambda e: e.activation(out=sq.rearrange("p (t c) -> p t c", c=128), in_=pkv[:, :, 0:128], func=AF.Square),
                              reads=banks, writes=[B_fs[1 + 2 * p][0]])
                        P.add("dve", lambda e: e.tensor_reduce(out=kst[:, 0:2 * n], in_=sq.rearrange("p (h d) -> p h d", d=64), axis=AX.X, op=ALU.add),
                              reads=[B_fs[1 + 2 * p][0]], writes=[kstb])
                        P.add("dve", lambda e: e.tensor_tensor(out=xg.rearrange("p (t h d) -> p t h d", h=2, d=64),
                                                               in0=pkv[:, :, 0:128].rearrange("p t (h d) -> p t h d", d=64),
                                                               in1=gk64.unsqueeze(1).unsqueeze(1).to_broadcast([128, n, 2, 64]), op=ALU.mult),
                              reads=banks + [B_crb], writes=[B_fs[1 + 2 * p][1]])
                        P.add("pool", lambda e: e.tensor_tensor(out=t1.rearrange("p (t h d) -> p t h d", h=2, d=64),
                                                                in0=xg.rearrange("p (t h d) -> p t h d", h=2, d=64),
                                                                in1=ropet[:, 2 * p:2 * p + n, 0:64].unsqueeze(2).to_broadcast([128, n, 2, 64]), op=ALU.mult),
                              reads=[B_fs[1 + 2 * p][1]] + ropes, writes=[B_fs[2][p]])
                        sq6 = sq.rearrange("p (t h r f d) -> p t h r f d", h=2, r=2, f=2, d=16)
                        xg6 = xg.rearrange("p (t h r f d) -> p t h r f d", h=2, r=2, f=2, d=16)
                        sn5 = ropet[:, 2 * p:2 * p + n, 64:128].rearrange("p t (r f d) -> p t r f d", r=2, f=2)
                        for hd in range(2):
                            for hf in range(2):
                                P.add("pool", lambda e, hd=hd, hf=hf: e.tensor_tensor(out=sq6[:, :, hd, :, hf, :], in0=xg6[:, :, hd, :, 1 - hf, :],
                                                                                      in1=sn5[:, :, :, hf, :], op=ALU.mult),
                                      reads=[B_fs[1 + 2 * p][1], kstb] + ropes, writes=[B_fs[1 + 2 * p][0]])
                        P.add("pool", lambda e: e.tensor_tensor(out=t1, in0=t1, in1=sq, op=ALU.add),
                              reads=[B_fs[2][p], B_fs[1 + 2 * p][0]], writes=[B_fs[2][p]])
                        P.add("act", lambda e: e.activation(
                            out=va[:, idx0:idx0 + n, :].rearrange("p t (a d) -> p t a d", d=64)[:, :, 0:3:2, :],
                            in_=pkv[:, :, 128:256].rearrange("p t (a d) -> p t a d", d=64), func=AF.Copy),
                            reads=banks + [B_ones], writes=[B_va[idx0 + gi] for gi in range(n)])
                    if hasB:
                        P.add("act", lambda e: e.activation(out=ktmp[:, p, 1, 0:W].rearrange("p (t c) -> p t c", c=128), in_=pkv[:, :, 256:384], func=AF.Copy),
                              reads=banks, writes=[B_kt[p][1]])
                        P.add("dve", lambda e: e.tensor_copy(
                            out=vb[:, idx0:idx0 + n, :].rearrange("p t (a d) -> p t a d", d=64)[:, :, 0:3:2, :],
                            in_=pkv[:, :, 384:512].rearrange("p t (a d) -> p t a d", d=64)),
                            reads=banks + [B_ones], writes=[B_vb[idx0 + gi] for gi in range(n)])
                        if kind == "halo":
                            for gi in range(n):
                                P.add("dve", lambda e, gi=gi: e.memset(vb[:, idx0 + gi, 64:128], 1.0), reads=[B_ones], writes=[B_vb[idx0 + gi]])
                                P.add("dve", lambda e, gi=gi: e.tensor_scalar(out=vb[:, idx0 + gi, :], in0=vb[:, idx0 + gi, :], scalar1=flg[:, gi:gi + 1],
                                                                            scalar2=None, op0=ALU.mult),
                                      reads=[B_vb[idx0 + gi], B_misc[1]], writes=[B_vb[idx0 + gi]])
                    if hasA:
                        krs, krsb = rstd_batch(kst[:, 0:2 * n], kstb, 2 * n, 1.0 / 64)
                        P.add("pool", lambda e: e.tensor_tensor(out=ktmp[:, p, 0, 0:W].rearrange("p (h d) -> p h d", d=64), in0=t1.rearrange("p (h d) -> p h d", d=64),
                                                               in1=krs.unsqueeze(2).to_broadcast([128, 2 * n, 64]), op=ALU.mult),
                              reads=[B_fs[2][p], krsb], writes=[B_kt[p][0]])

                def sc():
                    pbk = 2 + p
                    pv = psb16(pbk, 0)

                    def trk(e):
                        ins = None
                        for gi in range(n):
                            if hasA:
                                ins = e.transpose(pv[:, gi * 128:(gi + 1) * 128], ktmp[:, p, 0, gi * 128:(gi + 1) * 128], ident[:])
                            if hasB:
                                ins = e.transpose(pv[:, 512 + gi * 128:512 + (gi + 1) * 128], ktmp[:, p, 1, gi * 128:(gi + 1) * 128], ident[:])
                        return ins
                    P.add("pe", trk, reads=[B_kt[p][0], B_kt[p][1], B_ident], writes=[B_ps[pbk][0]])
                    if hasA:
                        P.add("dve", lambda e: e.tensor_copy(out=kaT[:, idx0 * 128:(idx0 + n) * 128], in_=pv[:, 0:W]),
                              reads=[B_ps[pbk][0]], writes=[B_kaT[idx0 + gi] for gi in range(n)])
                    if hasB:
                        P.add("act", lambda e: e.activation(out=kbT[:, idx0 * 128:(idx0 + n) * 128], in_=pv[:, 512:512 + W], func=AF.Copy),
                              reads=[B_ps[pbk][0]], writes=[B_kbT[idx0 + gi] for gi in range(n)])
                return s0, sa, sh2, sb_, sc

            def make_pass1(u, row0):
                tiles = [("own", i) for i in range(u.n_own)] + [("oth", u.n_own + i) for i in range(u.n_oth)]
                if u.halo:
                    tiles += [("halo", u.n_own), ("halo", u.n_own + 1)]
                groups = [do_group(u, row0, tiles, g0) for g0 in range(0, len(tiles), 2)]
                ng = len(groups)

                def mk(k):
                    def step():
                        for j_ in (4, 3, 2, 1, 0):
                            if 0 <= k - j_ < ng:
                                groups[k - j_][j_]()
                    return step
                steps = [mk(k) for k in range(ng + 4)]
                return steps

            rows = []
            r_ = 0
            for u in units:
                rows.append(r_)
                r_ += u.n_rows
            for s_ in make_pass1(units[0], rows[0]):
                s_()
            emit_etable()
            for ui, u in enumerate(units):
                nkv = u.n_own + u.n_oth
                next_p1 = make_pass1(units[ui + 1], rows[ui + 1]) if ui + 1 < len(units) else []

                def p1hook(next_p1=next_p1):
                    if next_p1:
                        next_p1.pop(0)()
                def stageA(c, par, u=u, nkv=nkv, row0=row0, out0=out0):
                    T0 = c * CH
                    xo = CH * par
                    S = {}

                    def a0():
                        srcs = []
                        for t in range(CH):
                            r = row0 + (T0 + t) * 128
                            P.add("sp", lambda e, t=t, r=r: e.dma_start(out=xc[:, xo + t, :], in_=xin[r:r + 128, :]), writes=[B_xc[xo + t]], dma=True)
                            P.add("sp", lambda e, t=t, r=r: e.dma_start(out=ropet[:, t, :], in_=rope[r:r + 128, :]), writes=[B_rope[t]], dma=True)
                            srcs.append((xc[:, xo + t, :], B_xc[xo + t]))
                        S['srcs'] = srcs
                        S['pre'] = norm_stats(srcs, st=stA[:, par, :], stb=B_stA[par])

                    def a1():
                        norm_transpose(S['srcs'], g1T, list(range(CH)), pre=S['pre'])
                        sa = ws_acquire(BLK_QA)
                        sbq = ws_acquire(BLK_QB, ahead=1)
                        qst, qstb = new_stat()
                        S['qst'], S['qstb'] = qst, qstb
                        for t in range(CH):
                            def qproj(e, t=t):
                                ins = None
                                for kc in range(KC):
                                    e.matmul(psb(2, t % 2), hT[:, kc, t * 128:(t + 1) * 128], ring[sa][:, kc * 512:(kc + 1) * 512],
                                             start=(kc == 0), stop=(kc == KC - 1))
                                for kc in range(KC):
                                    ins = e.matmul(psb(3, t % 2), hT[:, kc, t * 128:(t + 1) * 128], ring[sbq][:, kc * 512:(kc + 1) * 512],
                                                   start=(kc == 0), stop=(kc == KC - 1))
                                return ins
                            P.add("pe", qproj, reads=[B_hT[t], B_ring[sa], B_ring[sbq]], writes=[B_ps[2][t % 2], B_ps[3][t % 2]])
                            qk_post(psb(2, t % 2), B_ps[2][t % 2], 8, gq64, ropet[:, t, :], B_rope[t], fs[t % 2], B_fs[t % 2],
                                    fs[2 + t // 2][:, (t % 2) * 512:(t % 2) * 512 + 512], B_fs[2 + t // 2][t % 2],
                                    qst[:, t * 8:t * 8 + 8], qstb)
                            P.add("act", lambda e, t=t: e.activation(out=qn16[:, 1, :], in_=psb(3, t % 2), func=AF.Copy),
                                  reads=[B_ps[3][t % 2]], writes=[B_qn[1]])
                            pvb = psb16(t % 2, 1)

                            def trb(e, pvb=pvb):
                                ins = None
                                for j in range(4):
                                    ins = e.transpose(pvb[:, j * 128:(j + 1) * 128], qn16[:, 1, j * 128:(j + 1) * 128], ident[:])
                                return ins
                            P.add("pe", trb, reads=[B_qn[1], B_ident], writes=[B_ps[t % 2][1]])
                            P.add("dve", lambda e, t=t, pvb=pvb: e.tensor_copy(out=qbT[:, :, t * 128:(t + 1) * 128],
                                                                              in_=pvb[:, 0:512].rearrange("p (j q) -> p j q", j=4)),
                                  reads=[B_ps[t % 2][1]], writes=[B_qT[t]])
                        ws_release()
                        ws_release()

                    def part2a():
                        qst, qstb = S['qst'], S['qstb']
                        qrs, qrsb = rstd_batch(qst[:, 0:32], qstb, 32, 1.0 / 64)
                        for t in range(CH):
                            t1 = fs[2 + t // 2][:, (t % 2) * 512:(t % 2) * 512 + 512]
                            P.add("dve", lambda e, t=t, t1=t1: e.tensor_tensor(
                                out=yaT[:, t, :].rearrange("p (h d) -> p h d", d=64), in0=t1.rearrange("p (h d) -> p h d", d=64),
                                in1=qrs[:, t * 8:t * 8 + 8].unsqueeze(2).to_broadcast([128, 8, 64]), op=ALU.mult),
                                reads=[B_fs[2 + t // 2][t % 2], qrsb], writes=[B_yaT[t]])

                    def part2b():
                        for t in range(CH):
                            pva = psb16(0, t % 2)

                            def tra(e, pva=pva, t=t):
                                ins = None
                                for j in range(4):
                                    ins = e.transpose(pva[:, j * 128:(j + 1) * 128], yaT[:, t, j * 128:(j + 1) * 128], ident[:])
                                return ins
                            P.add("pe", tra, reads=[B_yaT[t], B_ident], writes=[B_ps[0][t % 2]])
                            P.add("act", lambda e, t=t, pva=pva: e.activation(out=qaT[:, :, t * 128:(t + 1) * 128],
                                                                              in_=pva[:, 0:512].rearrange("p (j q) -> p j q", j=4), func=AF.Copy),
                                  reads=[B_ps[0][t % 2], B_qT[t]], writes=[B_qT[t]])
                    return a0, a1, part2a, part2b

                def chunk_rest(c, par, nextA, p1h, u=u, nkv=nkv, row0=row0, out0=out0):
                    T0 = c * CH
                    xo = CH * par
                    P.mark("stageA")
                    steps = [(jp, i) for jp in range(4) for i in range(nkv)]

                    def emit_qk(g):
                        jp, i = steps[g]
                        s = g % 2

                        def f(e, jp=jp, i=i, s=s):
                            e.matmul(psb(s, 0), kaT[0:64, i * 128:(i + 1) * 128], qaT[0:64, jp, :], start=True, stop=True)
                            return e.matmul(psb(s, 1), kaT[64:128, i * 128:(i + 1) * 128], qaT[64:128, jp, :], start=True, stop=True)
                        P.add("pe", f, reads=[B_kaT[i]] + B_qT, writes=[B_ps[s][0], B_ps[s][1]])

                    def emit_exp_pv(g):
                        jp, i = steps[g]
                        s = g % 2
                        k = g % 3
                        a = 2 + (jp % 2)
                        P.add("act", lambda e, s=s, k=k: e.activation(out=pt[:, k, :], in_=PS[s], func=AF.Exp, scale=0.125),
                              reads=[B_ps[s][0], B_ps[s][1]], writes=[B_pt[k]])

                        def f(e, i=i, k=k, a=a):
                            e.matmul(psb(a, 0), va[:, i, 0:128], pt[:, k, 0:512], start=(i == 0), stop=(i == nkv - 1))
                            return e.matmul(psb(a, 1), va[:, i, 64:192], pt[:, k, 512:1024], start=(i == 0), stop=(i == nkv - 1))
                        P.add("pe", f, reads=[B_va[i], B_pt[k], B_ones], writes=[B_ps[a][0], B_ps[a][1]])
                        if i == nkv - 1:
                            P.add("dve", lambda e, a=a: e.reciprocal(out=rc[64:128, 0, :], in_=PS[a][64:128, 0:512]),
                                  reads=[B_ps[a][0]], writes=[B_rc[0]])
                            P.add("dve", lambda e, a=a, jp=jp: e.tensor_tensor(out=yaT[0:64, jp, :], in0=PS[a][0:64, 0:512], in1=rc[64:128, 0, :], op=ALU.mult),
                                  reads=[B_ps[a][0], B_rc[0]], writes=[B_yaT[jp]])
                            P.add("dve", lambda e, a=a: e.reciprocal(out=rc[0:64, 1, :], in_=PS[a][0:64, 512:1024]),
                                  reads=[B_ps[a][1]], writes=[B_rc[1]])
                            P.add("dve", lambda e, a=a, jp=jp: e.tensor_tensor(out=yaT[64:128, jp, :], in0=PS[a][64:128, 512:1024], in1=rc[0:64, 1, :], op=ALU.mult),
                                  reads=[B_ps[a][1], B_rc[1], B_yaT[jp]], writes=[B_yaT[jp]])
                    emit_qk(0)
                    for g in range(len(steps)):
                        if g + 1 < len(steps):
                            emit_qk(g + 1)
                        emit_exp_pv(g)

                    P.mark("stageB")
                    qblks = []
                    for t in range(CH):
                        T = T0 + t
                        blks = []
                        for o in range(3):
                            kb = T + o - 1
                            if 0 <= kb < u.n_own:
                                blks.append((kb, ("E", 2 - o)))
                            elif u.halo:
                                blks.append((u.n_own + (0 if o == 0 else 1), ("H", 0 if o == 0 else 1)))
                        qblks.append(blks)
                    bcnt = [0]

                    def emit_bq(t):
                        for bi, (kb, esel) in enumerate(qblks[t]):
                            s = bcnt[0] % 2
                            bcnt[0] += 1
                            k = (t % 2) * 3 + bi

                            def f(e, t=t, kb=kb, s=s):
                                e.matmul(psb(s, 0), kbT[0:64, kb * 128:(kb + 1) * 128], qbT[0:64, :, t * 128:(t + 1) * 128], start=True, stop=True)
                                return e.matmul(psb(s, 1), kbT[64:128, kb * 128:(kb + 1) * 128], qbT[64:128, :, t * 128:(t + 1) * 128], start=True, stop=True)
                            P.add("pe", f, reads=[B_kbT[kb], B_qT[t]], writes=[B_ps[s][0], B_ps[s][1]])
                            P.add("act", lambda e, s=s: e.activation(out=pb0[:, s, :], in_=PS[s], func=AF.Exp, scale=0.125),
                                  reads=[B_ps[s][0], B_ps[s][1]], writes=[B_pb0[s]])
                            if esel[0] == "E":
                                eap = Et[:, :, esel[1] * 128:(esel[1] + 1) * 128]
                                ebuf = B_E
                            else:
                                hb_ = 2 if esel[1] == 0 else 0
                                eap = Et[:, :, hb_ * 128:(hb_ + 1) * 128]
                                ebuf = B_E
                            P.add("pool", lambda e, k=k, s=s, eap=eap: e.tensor_tensor(out=pt[:, k, :].rearrange("p (h q) -> p h q", h=8),
                                                                                     in0=pb0[:, s, :].rearrange("p (h q) -> p h q", h=8), in1=eap, op=ALU.mult),
                                  reads=[B_pb0[s], ebuf], writes=[B_pt[k]])

                    def emit_bpv(t):
                        blks = qblks[t]
                        nb = len(blks)
                        k0 = (t % 2) * 3

                        def f(e, blks=blks, nb=nb, k0=k0):
                            ins = None
                            for h in range(8):
                                for bi, (kb, esel) in enumerate(blks):
                                    if h < 4:
                                        o_ = PS[2][:, h * 65:(h + 1) * 65]
                                        r_ = vb[:, kb, 0:65]
                                    else:
                                        o_ = PS[2][:, 512 + (h - 4) * 65:512 + (h - 3) * 65]
                                        r_ = vb[:, kb, 127:192]
                                    ins = e.matmul(o_, pt[:, k0 + bi, h * 128:(h + 1) * 128], r_, start=(bi == 0), stop=(bi == nb - 1))
                            return ins
                        P.add("pe", f, reads=[B_vb[kb] for kb, _ in blks] + [B_pt[k0 + bi] for bi in range(nb)] + [B_ones],
                              writes=[B_ps[2][0], B_ps[2][1]])
                        st, stb = new_stat()
                        a0 = PS[2][:, 0:260].rearrange("p (h c) -> p h c", c=65)
                        a1 = PS[2][:, 512:772].rearrange("p (h c) -> p h c", c=65)
                        P.add("dve", lambda e: e.tensor_tensor(out=st[:, 0:4].unsqueeze(2), in0=a0[:, :, 64:65],
                                                               in1=expsink[:, 0:4].unsqueeze(2), op=ALU.add),
                              reads=[B_ps[2][0], B_sink], writes=[stb])
                        P.add("dve", lambda e: e.tensor_tensor(out=st[:, 4:8].unsqueeze(2), in0=a1[:, :, 0:1],
                                                               in1=expsink[:, 4:8].unsqueeze(2), op=ALU.add),
                              reads=[B_ps[2][1], B_sink, stb], writes=[stb])
                        P.add("dve", lambda e: e.reciprocal(out=st[:, 8:16], in_=st[:, 0:8]), reads=[stb], writes=[stb])
                        yv = qn16[:, 0, :].rearrange("p (j k d) -> p j k d", j=4, k=2)
                        P.add("dve", lambda e: e.tensor_tensor(out=yv[:, :, 0, :], in0=a0[:, :, 0:64],
                                                               in1=st[:, 8:12].unsqueeze(2).to_broadcast([128, 4, 64]), op=ALU.mult),
                              reads=[B_ps[2][0], stb], writes=[B_qn[0]])
                        P.add("dve", lambda e: e.tensor_tensor(out=yv[:, :, 1, :], in0=a1[:, :, 1:65],
                                                               in1=st[:, 12:16].unsqueeze(2).to_broadcast([128, 4, 64]), op=ALU.mult),
                              reads=[B_ps[2][1], stb, B_qn[0]], writes=[B_qn[0]])
                        pvy = psb16(3, t % 2)

                        def try_(e):
                            ins = None
                            for j in range(4):
                                ins = e.transpose(pvy[:, j * 128:(j + 1) * 128], qn16[:, 0, j * 128:(j + 1) * 128], ident[:])
                            return ins
                        P.add("pe", try_, reads=[B_qn[0], B_ident], writes=[B_ps[3][t % 2]])
                        P.add("act", lambda e: e.activation(out=ybT[:, :, t * 128:(t + 1) * 128],
                                                            in_=pvy[:, 0:512].rearrange("p (j q) -> p j q", j=4), func=AF.Copy),
                              reads=[B_ps[3][t % 2]], writes=[B_ybT[t]])
                    emit_bq(0)
                    for t in range(CH):
                        if t + 1 < CH:
                            emit_bq(t + 1)
                        emit_bpv(t)

                    P.mark("stageB2")
                    nxt2 = nextA() if nextA is not None else None
                    if nxt2 is not None:
                        nxt2[0]()
                    for f_ in range(8):
                        sl = ws_acquire(BLK_MG + f_)
                        w = ring[sl]
                        pg, pz = (0, 1) if f_ % 2 == 0 else (2, 3)

                        def mg(e, w=w, pg=pg, pz=pz):
                            ins = None
                            for kc in range(KC):
                                e.matmul(psb(pg, 0), w[:, kc * 128:(kc + 1) * 128], hT[:, kc, :], start=(kc == 0), stop=(kc == KC - 1))
                            for kc in range(KC):
                                e.matmul(psb(pg, 1), w[:, 1024 + kc * 128:1024 + (kc + 1) * 128], hT[:, kc, :], start=(kc == 0), stop=(kc == KC - 1))
                            for pc in range(4):
                                e.matmul(psb(pz, 0), w[:, 2048 + pc * 128:2048 + (pc + 1) * 128], yaT[:, pc, :], start=(pc == 0), stop=(pc == 3))
                            for pc in range(4):
                                ins = e.matmul(psb(pz, 1), w[:, 2560 + pc * 128:2560 + (pc + 1) * 128], ybT[:, pc, :], start=(pc == 0), stop=(pc == 3))
                            return ins
                        P.add("pe", mg, reads=[B_ring[sl]] + B_hT + B_yaT + B_ybT,
                              writes=[B_ps[pg][0], B_ps[pg][1], B_ps[pz][0], B_ps[pz][1]])
                        ws_release()
                        k = f_ % 2
                        P.add("act", lambda e, pg=pg, k=k, f_=f_: e.activation(out=gab[:, k, 0, :], in_=psb(pg, 0), func=AF.Sigmoid, bias=bgT[:, f_:f_ + 1]),
                              reads=[B_ps[pg][0], B_const], writes=[B_gab[k]])
                        P.add("act", lambda e, pg=pg, k=k, f_=f_: e.activation(out=gab[:, k, 1, :], in_=psb(pg, 1), func=AF.Sigmoid, bias=bgT[:, 8 + f_:9 + f_]),
                              reads=[B_ps[pg][1], B_const], writes=[B_gab[k]])
                        P.add("dve", lambda e, pz=pz, k=k: e.tensor_tensor(out=fs[0][:, 0:512], in0=psb(pz, 0), in1=gab[:, k, 0, :], op=ALU.mult),
                              reads=[B_ps[pz][0], B_gab[k]], writes=[B_fs[0][0]])
                        P.add("dve", lambda e, pz=pz, k=k: e.tensor_tensor(out=fs[0][:, 512:1024], in0=psb(pz, 1), in1=gab[:, k, 1, :], op=ALU.mult),
                              reads=[B_ps[pz][1], B_gab[k]], writes=[B_fs[0][1]])
                        P.add("pool", lambda e, f_=f_: e.tensor_tensor(out=mixT[:, f_, :], in0=fs[0][:, 0:512], in1=fs[0][:, 512:1024], op=ALU.add),
                              reads=[B_fs[0][0], B_fs[0][1]], writes=[B_mixT[f_]])

                    P.mark("stageC1")
                    for h in range(2):
                        sl = ws_acquire(BLK_WO + h)
                        w = ring[sl]
                        for t in range(CH):
                            def wo(e, w=w, t=t, h=h):
                                ins = None
                                for kc in range(KC):
                                    ins = e.matmul(psb(t, h), mixT[:, kc, t * 128:(t + 1) * 128], w[:, kc * 512:(kc + 1) * 512],
                                                   start=(kc == 0), stop=(kc == KC - 1))
                                return ins
                            P.add("pe", wo, reads=[B_ring[sl]] + B_mixT, writes=[B_ps[t][h]])
                        ws_release()

                    def post_norm_residual(gb, dst_out):
                        st, stb = new_stat()
                        for t in range(CH):
                            jk, jkb = new_junk()
                            P.add("act", lambda e, t=t, jk=jk: e.activation(out=jk, in_=PS[t], func=AF.Square, accum_out=st[:, t:t + 1]),
                                  reads=[B_ps[t][0], B_ps[t][1]], writes=[stb, jkb])
                            P.add("dve", lambda e, t=t: e.tensor_tensor(out=fs[t][:], in0=PS[t], in1=gb, op=ALU.mult),
                                  reads=[B_ps[t][0], B_ps[t][1], B_crb], writes=[B_fs[t][0], B_fs[t][1]])
                        rs, rsb = rstd_batch(st[:, 0:CH], stb, CH, 1.0 / D)
                        for t in range(CH):
                            P.add("dve", lambda e, t=t: e.scalar_tensor_tensor(out=xc[:, xo + t, :], in0=fs[t][:], scalar=rs[:, t:t + 1],
                                                                                in1=xc[:, xo + t, :], op0=ALU.mult, op1=ALU.add),
                                  reads=[B_xc[xo + t], B_fs[t][0], B_fs[t][1], rsb], writes=[B_xc[xo + t]])
                            if dst_out is not None:
                                r = dst_out + t * 128
                                P.add("pool", lambda e, t=t, r=r: e.dma_start(out=yout[r:r + 128, :], in_=xc[:, xo + t, :]), reads=[B_xc[xo + t]], dma=True)
                    post_norm_residual(g2b, None)

                    P.mark("stageC2")
                    norm_transpose([(xc[:, xo + t, :], B_xc[xo + t]) for t in range(CH)], g3T, list(range(CH)))
                    for b in range(11):
                        sl = ws_acquire(BLK_GU + b)
                        w = ring[sl]
                        for jj in range(2):
                            j = 2 * b + jj
                            pi = j % 4

                            def gu(e, w=w, jj=jj, pi=pi):
                                ins = None
                                for kc in range(KC):
                                    e.matmul(psb(pi, 0), w[:, jj * 2048 + kc * 128:jj * 2048 + (kc + 1) * 128], hT[:, kc, :], start=(kc == 0), stop=(kc == KC - 1))
                                for kc in range(KC):
                                    ins = e.matmul(psb(pi, 1), w[:, jj * 2048 + 1024 + kc * 128:jj * 2048 + 1024 + (kc + 1) * 128], hT[:, kc, :],
                                                   start=(kc == 0), stop=(kc == KC - 1))
                                return ins
                            P.add("pe", gu, reads=[B_ring[sl]] + B_hT, writes=[B_ps[pi][0], B_ps[pi][1]])
                            k = j % 2
                            P.add("act", lambda e, pi=pi, k=k: e.activation(out=sg[:, k, :], in_=psb(pi, 0), func=AF.Silu),
                                  reads=[B_ps[pi][0]], writes=[B_sg[k]])
                            P.add("dve", lambda e, pi=pi, k=k, j=j: e.tensor_tensor(out=aT[:, j, :], in0=psb(pi, 1), in1=sg[:, k, :], op=ALU.mult),
                                  reads=[B_ps[pi][1], B_sg[k]], writes=[B_aT[j]])
                        ws_release()
                        if p1h is not None:
                            p1h()
                    if nxt2 is not None:
                        nxt2[1]()
                    for wbk in range(6):
                        sl = ws_acquire(BLK_WD + wbk)
                        w = ring[sl]
                        for jj in range(4):
                            j = 4 * wbk + jj
                            if j >= NJ:
                                break

                            def dn(e, w=w, jj=jj, j=j):
                                ins = None
                                for t in range(CH):
                                    for h in range(2):
                                        ins = e.matmul(psb(t, h), aT[:, j, t * 128:(t + 1) * 128], w[:, jj * 1024 + h * 512:jj * 1024 + (h + 1) * 512],
                                                       start=(j == 0), stop=(j == NJ - 1))
                                return ins
                            P.add("pe", dn, reads=[B_ring[sl], B_aT[j]], writes=[B_ps[t][h] for t in range(CH) for h in range(2)])
                        ws_release()
                    if nxt2 is not None:
                        nxt2[2]()
                    post_norm_residual(g4b, out0 + T0 * 128)
                    if nxt2 is not None:
                        nxt2[3]()
                nch = u.n_own // CH
                par0 = gch[0] % 2
                for f_ in stageA(0, par0):
                    f_()
                for c in range(nch):
                    par = gch[0] % 2
                    gch[0] += 1
                    nextA = (lambda c=c, par=par: stageA(c + 1, 1 - par)) if c + 1 < nch else None
                    chunk_rest(c, par, nextA, p1hook if c + 1 == nch else None)
                while next_p1:
                    next_p1.pop(0)()
                row0 += u.n_rows
                out0 += u.n_own * 128

        P.cut = True
        emit_main()
        P.cut = False
        ws_dry[0] = False
        emit_main()
        P.prepare(nc, es, {"sp": 16, "pool": 40})
        block = es.enter_context(nc.Block())
        P.emit(block)
    return nc


HEAD_DIM = 64
GRID_W = 64
ROPE_THETA = 10000.0
ROPE_HALF = 32
N_BUCKETS = 32
MAX_DISTANCE = 128
PAIR_ORDER = [0, 4, 1, 5, 2, 6, 3, 7]


def _rope_table(S):
    import jax
    import jax.numpy as jnp
    with jax.default_device(jax.devices("cpu")[0]):
        ROWS = S // GRID_W
        pos = jnp.arange(S)
        rows = jnp.repeat(jnp.arange(ROWS), GRID_W).astype(jnp.float32)
        cols = (pos % GRID_W).astype(jnp.float32)
        inv_freq = 1.0 / (ROPE_THETA ** (jnp.arange(0, ROPE_HALF, 2, dtype=jnp.float32) / ROPE_HALF))
        fr = rows[:, None] * inv_freq[None, :]
        fc = cols[:, None] * inv_freq[None, :]
        er = jnp.concatenate([fr, fr], axis=-1)
        ec = jnp.concatenate([fc, fc], axis=-1)
        cr, sr, cc, sc = (np.asarray(a, dtype=np.float32) for a in (jnp.cos(er), jnp.sin(er), jnp.cos(ec), jnp.sin(ec)))
    sgn = np.concatenate([-np.ones(16, np.float32), np.ones(16, np.float32)])
    return np.concatenate([cr, cc, sr * sgn, sc * sgn], axis=1).astype(np.float32)


def _bucket_onehot():
    import jax
    import jax.numpy as jnp
    with jax.default_device(jax.devices("cpu")[0]):
        rel = 255 - jnp.arange(512)
        half = N_BUCKETS // 2
        max_exact = half // 2
        ret = jnp.where(rel > 0, half, 0)
        n = jnp.abs(rel)
        nf = jnp.maximum(n, 1).astype(jnp.float32)
        large = max_exact + (jnp.log(nf / max_exact) / math.log(MAX_DISTANCE / max_exact) * (half - max_exact)).astype(jnp.int32)
        large = jnp.minimum(large, half - 1)
        bucket = np.asarray(ret + jnp.where(n < max_exact, n, large))
        rel = np.asarray(rel)
    oh = np.zeros((33, 512), np.float32)
    valid = np.abs(rel) <= 128
    for j in range(512):
        if valid[j]:
            oh[bucket[j], j] = 1.0
        else:
            oh[32, j] = -30000.0
    return oh


def _weight_blocks(inp):
    w_in = np.asarray(inp["w_in"][0], np.float32)
    w_gate = np.asarray(inp["w_gate"][0], np.float32)
    wa = np.asarray(inp["w_branch_a"][0], np.float32)
    wb = np.asarray(inp["w_branch_b"][0], np.float32)
    w_out = np.asarray(inp["w_out"][0], np.float32)
    wg = np.asarray(inp["w_ffn_gate"][0], np.float32)
    wu = np.asarray(inp["w_ffn_up"][0], np.float32)
    wd = np.asarray(inp["w_ffn_down"][0], np.float32)
    blk = np.zeros((NBLK, 128, 4096), np.float32)

    def kmaj(w):
        K, N = w.shape
        return w.reshape(K // 128, 128, N).transpose(1, 0, 2)
    qa_cols = np.concatenate([np.arange(h * 64, (h + 1) * 64) for h in PAIR_ORDER])
    kv_cols = np.concatenate([np.arange(512, 768), np.arange(1280, 1536)])
    blk[BLK_KV] = kmaj(w_in[:, kv_cols]).reshape(128, 4096)
    blk[BLK_QA] = kmaj(w_in[:, qa_cols]).reshape(128, 4096)
    blk[BLK_QB] = kmaj(w_in[:, 768 + qa_cols]).reshape(128, 4096)
    rows = qa_cols
    wa_p = kmaj(wa[rows, :])
    wb_p = kmaj(wb[rows, :])
    wgk = kmaj(w_gate)
    for f in range(8):
        b = blk[BLK_MG + f]
        b[:, 0:1024] = wgk[:, :, f * 128:(f + 1) * 128].reshape(128, 1024)
        b[:, 1024:2048] = wgk[:, :, 1024 + f * 128:1024 + (f + 1) * 128].reshape(128, 1024)
        b[:, 2048:2560] = wa_p[:, :, f * 128:(f + 1) * 128].reshape(128, 512)
        b[:, 2560:3072] = wb_p[:, :, f * 128:(f + 1) * 128].reshape(128, 512)
    wok = kmaj(w_out)
    for h in range(2):
        blk[BLK_WO + h] = wok[:, :, h * 512:(h + 1) * 512].reshape(128, 4096)
    wgk2 = kmaj(wg)
    wuk2 = kmaj(wu)
    for b_ in range(11):
        for jj in range(2):
            j = 2 * b_ + jj
            blk[BLK_GU + b_][:, jj * 2048:jj * 2048 + 1024] = wgk2[:, :, j * 128:(j + 1) * 128].reshape(128, 1024)
            blk[BLK_GU + b_][:, jj * 2048 + 1024:jj * 2048 + 2048] = wuk2[:, :, j * 128:(j + 1) * 128].reshape(128, 1024)
    wdk = kmaj(wd)
    for b_ in range(6):
        js = list(range(4 * b_, min(4 * b_ + 4, NJ)))
        blk[BLK_WD + b_][:, 0:len(js) * 1024] = wdk[:, js, :].reshape(128, len(js) * 1024)
    return blk


def _consts(inp):
    c128 = np.zeros((128, 288), np.float32)
    c128[:, 0:128] = np.eye(128, dtype=np.float32)
    c128[:, 128:256] = np.eye(128, dtype=np.float32)[::-1]
    c128[:, 256:264] = np.asarray(inp["norm_mix_pre"][0], np.float32).reshape(8, 128).T
    c128[:, 264:272] = np.asarray(inp["norm_ffn_pre"][0], np.float32).reshape(8, 128).T
    c128[:, 272:288] = np.asarray(inp["b_gate"][0], np.float32).reshape(16, 128).T
    crow = np.concatenate([np.asarray(inp["norm_mix_post"][0], np.float32), np.asarray(inp["norm_ffn_post"][0], np.float32),
                           np.asarray(inp["q_norm_a"][0], np.float32), np.asarray(inp["k_norm_a"][0], np.float32),
                           np.asarray(inp["sink_b"][0], np.float32)])[None, :]
    rbaug = np.concatenate([np.asarray(inp["rel_bias"], np.float32), np.ones((1, 8), np.float32)], axis=0)
    return c128, np.ascontiguousarray(crow), np.ascontiguousarray(rbaug)


def _core_inputs(inp, prompts, sample, units):
    xs, ropes = [], []
    for xp in prompts:
        xs.append(xp)
        ropes.append(_rope_table(xp.shape[0]))
    flags = np.zeros((1, 2), np.float32)
    if sample is not None:
        xf, half = sample
        S2 = xf.shape[0]
        H = S2 // 2
        rt = _rope_table(S2)
        own = slice(half * H, (half + 1) * H)
        oth = slice((1 - half) * H, (2 - half) * H)
        z = np.zeros((128, D), np.float32)
        prev = xf[half * H - 128:half * H] if half == 1 else z
        nxt = xf[(half + 1) * H:(half + 1) * H + 128] if half == 0 else z
        flags[0, 0] = 1.0 if half == 1 else 0.0
        flags[0, 1] = 1.0 if half == 0 else 0.0
        xs += [xf[own], xf[oth], prev, nxt]
        ropes += [rt[own], rt[oth], np.zeros((256, 128), np.float32)]
    xin = np.ascontiguousarray(np.concatenate(xs, axis=0), dtype=np.float32)
    rope = np.ascontiguousarray(np.concatenate(ropes, axis=0), dtype=np.float32)
    return xin, rope, flags


_CACHE = {}


def kernel(**inp):
    xp = np.asarray(inp["x_prompt"], np.float32)
    xsm = np.asarray(inp["x_sample"], np.float32)
    B, S, _ = xp.shape
    B2, S2, _ = xsm.shape
    ncores = 8
    ppc = B // ncores
    assert B2 * 2 == ncores
    units = [Unit(S // 128, 0, False) for _ in range(ppc)] + [Unit(S2 // 256, S2 // 256, True)]
    key = (S, S2, ppc)
    if key not in _CACHE:
        _CACHE[key] = build(units)
    nc = _CACHE[key]
    blk = _weight_blocks(inp)
    c128, crow, rbaug = _consts(inp)
    ohx = _bucket_onehot()
    in_maps = []
    for c in range(ncores):
        xin, rope, flags = _core_inputs(inp, [xp[c * ppc + i] for i in range(ppc)], (xsm[c // 2], c % 2), units)
        in_maps.append({"xin": xin, "rope": rope, "wsrc": blk, "c128": c128, "crow": crow, "flags": flags,
                        "ohx": ohx, "rbaug": rbaug})
    res = run_bass_kernel_spmd(nc, in_maps, core_ids=list(range(ncores)))
    yp = np.zeros_like(xp)
    ys = np.zeros_like(xsm)
    H = S2 // 2
    for c in range(ncores):
        y = np.asarray(res.results[c]["yout"], np.float32)
        for i in range(ppc):
            yp[c * ppc + i] = y[i * S:(i + 1) * S]
        ys[c // 2, (c % 2) * H:(c % 2 + 1) * H] = y[ppc * S:ppc * S + H]
    return (yp, ys)
```
